# Optimizing a Trainium2 kernel written in Bass

```python
import jax, jax.numpy as jnp
from jax import lax
import numpy as np

D_MODEL = 1024
BATCH = 8
SEQ = 2048
DEPTH = 2

GRID_W = 64
CTX_LEN = 256
N_MIXERS = 2
SSD_EXPAND = 2
SSD_D_INNER = SSD_EXPAND * D_MODEL
SSD_HEAD_DIM = 64
SSD_HEADS = SSD_D_INNER // SSD_HEAD_DIM
SSD_GROUPS = 8
SSD_HPG = SSD_HEADS // SSD_GROUPS
SSD_STATE = 128
SSD_CONV = 5
SSD_CHUNK = 128
SSD_CONV_DIM = SSD_D_INNER + 2 * SSD_GROUPS * SSD_STATE
SSD_IN_DIM = SSD_D_INNER + SSD_CONV_DIM + 2 * SSD_HEADS
CONF_KERNEL = 31
FFN_HIDDEN = ((8 * D_MODEL // 3 + 255) // 256) * 256
FFN_CONV = 3
N_SSD_LAYERS = (DEPTH + 1) // 2
N_CONF_LAYERS = DEPTH // 2
EPS = 1e-6

kernel_name = 'hybrid_ssd_conformer_dit_ctx_prefix'


def rmsnorm(h, w):
    hf = h.astype(jnp.float32)
    y = hf * lax.rsqrt(jnp.mean(hf * hf, axis=-1, keepdims=True) + EPS)
    return (y * w.astype(jnp.float32)).astype(h.dtype)


def layernorm(h, w, b):
    hf = h.astype(jnp.float32)
    mu = jnp.mean(hf, axis=-1, keepdims=True)
    d = hf - mu
    y = d * lax.rsqrt(jnp.mean(d * d, axis=-1, keepdims=True) + EPS)
    return (y * w.astype(jnp.float32) + b.astype(jnp.float32)).astype(h.dtype)


def modulate(h, g, shift, scale):
    return rmsnorm(h, g) * (1 + scale) + shift


def ada_params(cond, w, b):
    m = jax.nn.silu(cond) @ w + b
    return jnp.split(m, 6, axis=-1)


def dwconv1d(u, w, b):
    k, ch = w.shape
    pad = k // 2
    y = lax.conv_general_dilated(u, w[:, None, :].astype(u.dtype), window_strides=(1,),
                                 padding=[(pad, pad)], dimension_numbers=('NWC', 'WIO', 'NWC'),
                                 feature_group_count=ch)
    return y + b


def dwconv2d_grid(u, w, b):
    bsz, l, ch = u.shape
    rows = l // GRID_W
    u4 = u.reshape(bsz, rows, GRID_W, ch)
    kh, kw, _ = w.shape
    y = lax.conv_general_dilated(u4, w[:, :, None, :].astype(u.dtype), window_strides=(1, 1),
                                 padding=[(kh // 2, kh // 2), (kw // 2, kw // 2)],
                                 dimension_numbers=('NHWC', 'HWIO', 'NHWC'),
                                 feature_group_count=ch)
    return y.reshape(bsz, l, ch) + b


def ssd_scan(x, dt, A, B, C, s0):
    bsz, l, g, r, p = x.shape
    n = B.shape[-1]
    q = SSD_CHUNK
    nc = l // q
    x = x.astype(jnp.float32).reshape(bsz, nc, q, g, r, p)
    dt = dt.reshape(bsz, nc, q, g, r)
    B = B.astype(jnp.float32).reshape(bsz, nc, q, g, n)
    C = C.astype(jnp.float32).reshape(bsz, nc, q, g, n)
    acum = jnp.cumsum(dt * A, axis=2)
    xdt = x * dt[..., None]
    seg = acum[:, :, :, None] - acum[:, :, None, :]
    mask = jnp.tril(jnp.ones((q, q), dtype=bool))[:, :, None, None]
    decay = jnp.exp(jnp.where(mask, seg, -jnp.inf))
    cb = jnp.einsum('bcign,bcjgn->bcijg', C, B)
    y_diag = jnp.einsum('bcijgr,bcjgrp->bcigrp', cb[..., None] * decay, xdt)
    decay_to_end = jnp.exp(acum[:, :, -1:] - acum)
    chunk_states = jnp.einsum('bcjgn,bcjgrp->bcgrpn', B, xdt * decay_to_end[..., None])
    chunk_decay = jnp.exp(acum[:, :, -1])

    def step(s, inp):
        dec, st = inp
        return dec[..., None, None] * s + st, s

    final, entering = lax.scan(step, s0.astype(jnp.float32),
                               (jnp.moveaxis(chunk_decay, 1, 0), jnp.moveaxis(chunk_states, 1, 0)))
    entering = jnp.moveaxis(entering, 0, 1)
    y_off = jnp.einsum('bcign,bcgrpn->bcigrp', C, entering) * jnp.exp(acum)[..., None]
    y = (y_diag + y_off).reshape(bsz, l, g, r, p)
    return y, final


def ssd_mixer(u, w_in, conv_w, conv_b, dt_bias, a_log, d_skip, norm_w, w_out, s0_fwd, s0_bwd):
    bsz, l, _ = u.shape
    di, gn = SSD_D_INNER, SSD_GROUPS * SSD_STATE
    proj = u @ w_in
    z = proj[..., :di]
    xbc = jax.nn.silu(dwconv1d(proj[..., di:di + SSD_CONV_DIM], conv_w, conv_b))
    dt_raw = proj[..., di + SSD_CONV_DIM:]
    xs = xbc[..., :di].reshape(bsz, l, SSD_GROUPS, SSD_HPG, SSD_HEAD_DIM)
    Bm = xbc[..., di:di + gn].reshape(bsz, l, SSD_GROUPS, SSD_STATE)
    Cm = xbc[..., di + gn:].reshape(bsz, l, SSD_GROUPS, SSD_STATE)
    dt = jax.nn.softplus(dt_raw.astype(jnp.float32).reshape(bsz, l, 2, SSD_GROUPS, SSD_HPG)
                         + dt_bias.astype(jnp.float32).reshape(2, SSD_GROUPS, SSD_HPG))
    A = -jnp.exp(a_log.astype(jnp.float32)).reshape(2, SSD_GROUPS, SSD_HPG)
    y_f, s_f = ssd_scan(xs, dt[:, :, 0], A[0], Bm, Cm, s0_fwd)
    y_b, s_b = ssd_scan(jnp.flip(xs, 1), jnp.flip(dt[:, :, 1], 1), A[1],
                        jnp.flip(Bm, 1), jnp.flip(Cm, 1), s0_bwd)
    y = y_f + jnp.flip(y_b, 1) + d_skip.astype(jnp.float32).reshape(SSD_GROUPS, SSD_HPG)[:, :, None] * xs.astype(jnp.float32)
    y = y.reshape(bsz, l, di) * jax.nn.silu(z.astype(jnp.float32))
    out = rmsnorm(y, norm_w).astype(u.dtype) @ w_out
    return out, (s_f, s_b)


def conformer_conv(u, w1, b1, w_dw, b_dw, ln_w, ln_b, w2, b2):
    h = u @ w1 + b1
    a, g = jnp.split(h, 2, axis=-1)
    h = a * jax.nn.sigmoid(g)
    h = dwconv1d(h, w_dw, b_dw)
    h = jax.nn.silu(layernorm(h, ln_w, ln_b))
    return h @ w2 + b2


def conv_ffn(u, w_up, conv_w, conv_b, w_down, on_grid):
    h = u @ w_up
    val, gate = jnp.split(h, 2, axis=-1)
    if on_grid:
        gate = dwconv2d_grid(gate, conv_w, conv_b)
    else:
        gate = dwconv1d(gate, conv_w[FFN_CONV // 2], conv_b)
    return (jax.nn.silu(gate) * val) @ w_down


def setup_inputs(seed: int = 0) -> dict:
    key = jax.random.key(seed)
    ks = jax.random.split(key, 32)
    f32 = jnp.float32

    def nrm(k, shape, scale):
        return jax.random.normal(k, shape, f32) * scale

    D = D_MODEL
    u = jax.random.uniform(ks[10], (N_SSD_LAYERS, 2, SSD_HEADS), f32)
    dt0 = jnp.exp(u * (np.log(0.1) - np.log(0.001)) + np.log(0.001)).astype(f32)
    dt_bias = dt0 + jnp.log(-jnp.expm1(-dt0))
    a_log = jnp.log(jax.random.uniform(ks[11], (N_SSD_LAYERS, 2, SSD_HEADS), f32, 1.0, 16.0))
    return {
        'x': nrm(ks[0], (BATCH, SEQ, D), 1.0),
        'c': nrm(ks[1], (BATCH, D), 1.0),
        'ctx': nrm(ks[2], (BATCH, CTX_LEN, D), 1.0),
        'c_ctx': nrm(ks[3], (D,), 1.0),
        'mod_w': nrm(ks[4], (DEPTH, D, 6 * D), 0.5 * D ** -0.5),
        'mod_b': nrm(ks[5], (DEPTH, 6 * D), 0.02),
        'norm1_w': 1.0 + nrm(ks[6], (DEPTH, D), 0.02),
        'norm2_w': 1.0 + nrm(ks[7], (DEPTH, D), 0.02),
        'ssd_w_in': nrm(ks[8], (N_SSD_LAYERS, D, SSD_IN_DIM), D ** -0.5),
        'ssd_conv_w': nrm(ks[9], (N_SSD_LAYERS, SSD_CONV, SSD_CONV_DIM), SSD_CONV ** -0.5),
        'ssd_conv_b': nrm(ks[12], (N_SSD_LAYERS, SSD_CONV_DIM), 0.02),
        'ssd_dt_bias': dt_bias,
        'ssd_a_log': a_log,
        'ssd_d': 1.0 + nrm(ks[13], (N_SSD_LAYERS, SSD_HEADS), 0.1),
        'ssd_norm_w': 1.0 + nrm(ks[14], (N_SSD_LAYERS, SSD_D_INNER), 0.02),
        'ssd_w_out': nrm(ks[15], (N_SSD_LAYERS, SSD_D_INNER, D), SSD_D_INNER ** -0.5),
        'conf_w_pw1': nrm(ks[16], (N_CONF_LAYERS, D, 2 * D), D ** -0.5),
        'conf_b_pw1': nrm(ks[17], (N_CONF_LAYERS, 2 * D), 0.02),
        'conf_w_dw': nrm(ks[18], (N_CONF_LAYERS, CONF_KERNEL, D), CONF_KERNEL ** -0.5),
        'conf_b_dw': nrm(ks[19], (N_CONF_LAYERS, D), 0.02),
        'conf_ln_w': 1.0 + nrm(ks[20], (N_CONF_LAYERS, D), 0.02),
        'conf_ln_b': nrm(ks[21], (N_CONF_LAYERS, D), 0.02),
        'conf_w_pw2': nrm(ks[22], (N_CONF_LAYERS, D, D), D ** -0.5),
        'conf_b_pw2': nrm(ks[23], (N_CONF_LAYERS, D), 0.02),
        'ffn_w_up': nrm(ks[24], (DEPTH, D, 2 * FFN_HIDDEN), D ** -0.5),
        'ffn_conv_w': nrm(ks[25], (DEPTH, FFN_CONV, FFN_CONV, FFN_HIDDEN), (FFN_CONV * FFN_CONV) ** -0.5),
        'ffn_conv_b': nrm(ks[26], (DEPTH, FFN_HIDDEN), 0.02),
        'ffn_w_down': nrm(ks[27], (DEPTH, FFN_HIDDEN, D), FFN_HIDDEN ** -0.5),
        'final_norm_w': 1.0 + nrm(ks[28], (D,), 0.02),
    }


def reference(x, c, ctx, c_ctx, mod_w, mod_b, norm1_w, norm2_w,
              ssd_w_in, ssd_conv_w, ssd_conv_b, ssd_dt_bias, ssd_a_log, ssd_d, ssd_norm_w, ssd_w_out,
              conf_w_pw1, conf_b_pw1, conf_w_dw, conf_b_dw, conf_ln_w, conf_ln_b, conf_w_pw2, conf_b_pw2,
              ffn_w_up, ffn_conv_w, ffn_conv_b, ffn_w_down, final_norm_w):
    h, hc = x, ctx
    bsz = x.shape[0]
    for i in range(DEPTH):
        kind = i % N_MIXERS
        j = i // N_MIXERS
        last = i == DEPTH - 1
        need_ctx = (not last) or kind == 0
        sh1, sc1, g1, sh2, sc2, g2 = ada_params(c[:, None, :], mod_w[i], mod_b[i])
        a = modulate(h, norm1_w[i], sh1, sc1)
        if need_ctx:
            csh1, csc1, cg1, csh2, csc2, cg2 = ada_params(c_ctx, mod_w[i], mod_b[i])
            ac = modulate(hc, norm1_w[i], csh1, csc1)
        if kind == 0:
            ssd_p = (ssd_w_in[j], ssd_conv_w[j], ssd_conv_b[j], ssd_dt_bias[j], ssd_a_log[j],
                     ssd_d[j], ssd_norm_w[j], ssd_w_out[j])
            zeros = jnp.zeros((bsz, SSD_GROUPS, SSD_HPG, SSD_HEAD_DIM, SSD_STATE), jnp.float32)
            yc, (s_f, s_b) = ssd_mixer(ac, *ssd_p, zeros, zeros)
            y, _ = ssd_mixer(a, *ssd_p, s_f, s_b)
        else:
            conf_p = (conf_w_pw1[j], conf_b_pw1[j], conf_w_dw[j], conf_b_dw[j],
                      conf_ln_w[j], conf_ln_b[j], conf_w_pw2[j], conf_b_pw2[j])
            y = conformer_conv(a, *conf_p)
            if not last:
                yc = conformer_conv(ac, *conf_p)
        h = h + g1 * y
        h = h + g2 * conv_ffn(modulate(h, norm2_w[i], sh2, sc2), ffn_w_up[i], ffn_conv_w[i],
                              ffn_conv_b[i], ffn_w_down[i], True)
        if not last:
            hc = hc + cg1 * yc
            hc = hc + cg2 * conv_ffn(modulate(hc, norm2_w[i], csh2, csc2), ffn_w_up[i], ffn_conv_w[i],
                                     ffn_conv_b[i], ffn_w_down[i], False)
    return rmsnorm(h, final_norm_w)
```

```python
import numpy as np
import concourse.bass as bass
import concourse.mybir as mybir
from concourse.bass_utils import run_bass_kernel_spmd

F32 = mybir.dt.float32
BF16 = mybir.dt.bfloat16
AF = mybir.ActivationFunctionType
ALU = mybir.AluOpType
AX = mybir.AxisListType

D = 1024
T = 2048
TC = 256
NB = D // 128
DI = 2048
NH = 32
NG = 8
NS = 128
CONVD = 4096
INDIM = 6208
FH = 2816
NFB = FH // 128
EPS = 1e-6
EPOCH = 30000
NSLOT = 8


class Res:
    __slots__ = ("name", "lw", "rd")

    def __init__(self, name):
        self.name = name
        self.lw = None
        self.rd = {}


class KB:
    ENG = ("sp", "pe", "dve", "act", "pool")

    def __init__(self):
        self.nc = bass.Bass("TRN2", target_bir_lowering=False)
        self.streams = {e: [] for e in self.ENG}
        self.cnt = {e: 0 for e in self.ENG}
        self.cursem = {}
        self.known = {e: {} for e in self.ENG}
        self.semkey = {}
        self.nsem = 0
        self.dma_i = {e: 0 for e in self.ENG}
        self.dma_slots = {}
        self.uid = 0
        self.out_tokens = []

    def newsem(self, name):
        s = self.nc.alloc_semaphore(f"{name}_{self.nsem}")
        self.nsem += 1
        self.semkey[id(s)] = s
        return s

    def sb(self, name, shape, dt):
        self.uid += 1
        return self.nc.alloc_sbuf_tensor(f"{name}_{self.uid}", list(shape), dt)

    def dram(self, name, shape, dt, kind="Internal"):
        return self.nc.dram_tensor(name, list(shape), dt, kind=kind)

    def _engsem(self, e):
        if e not in self.cursem:
            self.cursem[e] = self.newsem("e" + e)
        return self.cursem[e]

    def _wait(self, e, tok):
        sem, val = tok
        k = self.known[e]
        if k.get(id(sem), 0) >= val:
            return
        k[id(sem)] = val
        self.streams[e].append(("w", sem, val))

    def _deps(self, e, reads, writes):
        own = id(self._engsem(e))
        for r in reads:
            if r.lw is not None:
                if e == "pe" and id(r.lw[0]) == own:
                    continue
                self._wait(e, r.lw)
        for w in writes:
            if w.lw is not None and id(w.lw[0]) != own:
                self._wait(e, w.lw)
            for t in w.rd.values():
                if id(t[0]) != own:
                    self._wait(e, t)

    def _mark(self, tok, reads, writes):
        for r in reads:
            k = id(tok[0])
            if k not in r.rd or r.rd[k][1] < tok[1]:
                r.rd[k] = tok
        for w in writes:
            w.lw = tok
            w.rd = {}

    def op(self, e, fn, reads=(), writes=(), inc=True):
        self._deps(e, reads, writes)
        sem = self._engsem(e)
        tok = (sem, self.cnt[e] + 1)
        self.streams[e].append(("o", fn, sem if inc else None, 1))
        if inc:
            self.cnt[e] += 1
            if self.cnt[e] >= EPOCH:
                del self.cursem[e]
                self.cnt[e] = 0
        self._mark(tok, reads, writes)
        return tok

    def dma(self, q, out, in_, reads=(), writes=(), **kw):
        self._deps(q, reads, writes)
        if q not in self.dma_slots:
            self.dma_slots[q] = [self.newsem("d" + q) for _ in range(NSLOT)]
        i = self.dma_i[q]
        self.dma_i[q] += 1
        sem = self.dma_slots[q][i % NSLOT]
        prev = 16 * (i // NSLOT)
        if prev > 0:
            self._wait(q, (sem, prev))
        tok = (sem, prev + 16)
        self.streams[q].append(("o", lambda eng: eng.dma_start(out=out, in_=in_, **kw), sem, 16))
        self._mark(tok, reads, writes)
        return tok

    def finish(self):
        for tok in self.out_tokens:
            self._wait("sp", tok)
        nc = self.nc
        streams = self.streams
        with nc.Block() as block:
            def mk(stream):
                def body(eng):
                    for it in stream:
                        if it[0] == "w":
                            eng.wait_ge(it[1], it[2])
                        else:
                            ins = it[1](eng)
                            if it[2] is not None:
                                ins.then_inc(it[2], it[3])
                return body
            block.sync(mk(streams["sp"]))
            block.tensor(mk(streams["pe"]))
            block.vector(mk(streams["dve"]))
            block.scalar(mk(streams["act"]))
            block.gpsimd(mk(streams["pool"]))
        return nc

    def mm(self, out, lhsT, rhs, start, stop, reads, writes, inc=None):
        if inc is None:
            inc = True
        return self.op("pe", lambda e: e.matmul(out, lhsT, rhs, start=start, stop=stop),
                       reads, writes, inc=inc)

    def mmg(self, out, lhsT, rhs, start, stop, reads, writes):
        return self.mm(out, lhsT, rhs, start, stop, reads, writes, inc=bool(stop))

    def tr(self, out, in_, ident, reads, writes, inc=True):
        return self.op("pe", lambda e: e.transpose(out, in_, ident), reads, writes, inc=inc)

    def act(self, out, in_, func, reads, writes, bias=0.0, scale=1.0, eng="act", accum_out=None):
        if accum_out is None:
            return self.op("act", lambda e: e.activation(out, in_, func, bias=bias, scale=scale),
                           reads, writes)
        return self.op("act", lambda e: e.activation(out, in_, func, bias=bias, scale=scale,
                                                     accum_out=accum_out), reads, writes)

    def tt(self, eng, out, in0, in1, op, reads, writes):
        return self.op(eng, lambda e: e.tensor_tensor(out, in0, in1, op), reads, writes)

    def ts(self, eng, out, in0, s1, s2, op0, op1, reads, writes):
        if s2 is None:
            return self.op(eng, lambda e: e.tensor_scalar(out, in0, s1, None, op0), reads, writes)
        return self.op(eng, lambda e: e.tensor_scalar(out, in0, s1, s2, op0, op1), reads, writes)

    def stt(self, eng, out, in0, scalar, in1, op0, op1, reads, writes):
        return self.op(eng, lambda e: e.scalar_tensor_tensor(out, in0, scalar, in1, op0, op1),
                       reads, writes)

    def cp(self, eng, out, in_, reads, writes):
        if eng == "act":
            return self.op(eng, lambda e: e.copy(out, in_), reads, writes)
        return self.op(eng, lambda e: e.tensor_copy(out, in_), reads, writes)

    def memset(self, eng, ap, val, writes):
        return self.op(eng, lambda e: e.memset(ap, val), (), writes)


class Arena:
    def __init__(self, kb, words):
        self.t = kb.nc.alloc_sbuf_tensor("arena", [128, words], F32)
        self.words = words
        self.top = 0

    def mark(self):
        return self.top

    def release(self, m):
        self.top = m

    def _alloc(self, words):
        words = (words + 7) // 8 * 8
        off = self.top
        self.top += words
        assert self.top <= self.words, f"arena overflow {self.top} > {self.words}"
        return off

    def f32(self, n):
        off = self._alloc(n)
        return self.t[:, off:off + n]

    def bf(self, n):
        w = (n + 1) // 2
        off = self._alloc(w)
        return self.t[:, off:off + w].bitcast(BF16)[:, 0:n]


class Ctx:
    pass


def r3(ap, a):
    return ap.rearrange("p (a b) -> p a b", a=a)


def build(stage=99):
    k = KB()
    nc = k.nc
    g = Ctx()
    g.k = k
    def din(name, shape):
        return nc.dram_tensor(name, list(shape), F32, kind="ExternalInput").ap()
    I = {}
    I["x"] = din("x", [T, D])
    I["ctx"] = din("ctx", [TC, D])
    I["cvec"] = din("cvec", [2, D])
    I["mod_w"] = din("mod_w", [2, D, 6 * D])
    I["mod_b"] = din("mod_b", [2, 6 * D])
    I["norm1_w"] = din("norm1_w", [2, D])
    I["norm2_w"] = din("norm2_w", [2, D])
    I["ssd_w_in"] = din("ssd_w_in", [D, INDIM])
    I["ssd_conv_w"] = din("ssd_conv_w", [5, CONVD])
    I["ssd_conv_b"] = din("ssd_conv_b", [CONVD])
    I["ssd_dt_bias"] = din("ssd_dt_bias", [64])
    I["ssd_a_log"] = din("ssd_a_log", [64])
    I["ssd_d"] = din("ssd_d", [NH])
    I["ssd_norm_w"] = din("ssd_norm_w", [DI])
    I["ssd_w_out"] = din("ssd_w_out", [DI, D])
    I["conf_w_pw1"] = din("conf_w_pw1", [D, 2 * D])
    I["conf_b_pw1"] = din("conf_b_pw1", [2 * D])
    I["conf_w_dw"] = din("conf_w_dw", [31, D])
    I["conf_b_dw"] = din("conf_b_dw", [D])
    I["conf_ln_w"] = din("conf_ln_w", [D])
    I["conf_ln_b"] = din("conf_ln_b", [D])
    I["conf_w_pw2"] = din("conf_w_pw2", [D, D])
    I["conf_b_pw2"] = din("conf_b_pw2", [D])
    I["ffn_w_up"] = din("ffn_w_up", [2, D, 2 * FH])
    I["ffn_conv_w"] = din("ffn_conv_w", [2, 9, FH])
    I["ffn_conv_b"] = din("ffn_conv_b", [2, FH])
    I["ffn_w_down"] = din("ffn_w_down", [2, FH, D])
    I["final_norm_w"] = din("final_norm_w", [D])
    out = nc.dram_tensor("out", [T, D], F32, kind="ExternalOutput").ap()
    g.I = I
    g.out = out

    ar = Arena(k, 53100)
    g.ar = ar
    g.psum = nc.alloc_psum_tensor("psall", [128, 4096], F32)
    g.banks = []
    for i in range(8):
        g.banks.append((g.psum[:, i * 512:(i + 1) * 512], Res(f"psb{i}")))
    g.bank_i = 0

    def bank():
        b = g.banks[g.bank_i % 6]
        g.bank_i += 1
        return b[0], b[1]
    g.bank = bank

    c = Ctx()
    g.c = c
    c.res = Res("consts")
    c.ident_f = ar.f32(128)
    c.ones_f = ar.f32(128)
    c.ident_b = ar.bf(128)
    c.ones_b = ar.bf(128)
    k.memset("pool", c.ident_f, 0.0, [c.res])
    k.op("pool", lambda e: e.affine_select(out=c.ident_f, in_=c.ident_f, pattern=[[-1, 128]],
                                           compare_op=ALU.not_equal, fill=1.0, base=0,
                                           channel_multiplier=1), [c.res], [c.res])
    k.memset("pool", c.ones_f, 1.0, [c.res])
    k.cp("pool", c.ident_b, c.ident_f, [c.res], [c.res])
    k.cp("pool", c.ones_b, c.ones_f, [c.res], [c.res])
    g.vstage = [ar.f32(128), ar.f32(128)]
    g.vsres = [Res("vs0"), Res("vs1")]
    g.vs_i = 0
    g.wst = ar.f32(4096)
    g.wstres = Res("wst")
    g.wsth_res = [Res("wsth0"), Res("wsth1")]
    g.wbf = [ar.bf(4096), ar.bf(4096)]
    g.wbfres = [Res("wbf0"), Res("wbf1")]
    g.w_i = 0

    V = Ctx()
    g.V = V
    V.fnw, V.fnw_r = load_vec(g, I["final_norm_w"], NB)
    V.n1w, V.n1w_r = load_vec(g, I["norm1_w"].rearrange("a b -> (a b)"), 2 * NB)
    V.n2w, V.n2w_r = load_vec(g, I["norm2_w"].rearrange("a b -> (a b)"), 2 * NB)
    V.modb, V.modb_r = load_vec(g, I["mod_b"].rearrange("a b -> (a b)"), 96)
    V.cv, V.cv_r = load_vec(g, I["cvec"].rearrange("a b -> (a b)"), 16)
    V.fcb, V.fcb_r = load_vec(g, I["ffn_conv_b"].rearrange("a b -> (a b)"), 2 * NFB)
    V.fcw, V.fcw_r = load_vec(g, I["ffn_conv_w"].rearrange("a b c -> (a b c)"), 2 * 9 * NFB)
    V.scb, V.scb_r = load_vec(g, I["ssd_conv_b"], 32)
    V.scw, V.scw_r = load_vec(g, I["ssd_conv_w"].rearrange("a b -> (a b)"), 5 * 32)
    V.cb1, V.cb1_r = load_vec(g, I["conf_b_pw1"], 16)
    V.cbdw, V.cbdw_r = load_vec(g, I["conf_b_dw"], 8)
    V.clnw, V.clnw_r = load_vec(g, I["conf_ln_w"], 8)
    V.clnb, V.clnb_r = load_vec(g, I["conf_ln_b"], 8)
    V.cb2, V.cb2_r = load_vec(g, I["conf_b_pw2"], 8)
    V.cdw, V.cdw_r = load_vec(g, I["conf_w_dw"].rearrange("a b -> (a b)"), 31 * 8)
    V.cs = r3(ar.bf(16), 8)
    V.cs_r = Res("cs")
    tmpc = ar.f32(16)
    tmpc_r = Res("tmpc")
    k.act(tmpc, V.cv, AF.Silu, [V.cv_r], [tmpc_r])
    V.cs32 = r3(ar.f32(16), 8)
    for s_ in range(2):
        k.cp("dve", V.cs[:, :, s_], tmpc[:, s_ * 8:(s_ + 1) * 8], [tmpc_r], [V.cs_r])
        k.cp("dve", V.cs32[:, :, s_], tmpc[:, s_ * 8:(s_ + 1) * 8], [tmpc_r], [V.cs_r])
    V.maskL = ar.f32(512)
    V.maskR = ar.f32(512)
    V.mask_r = Res("masks")
    k.memset("pool", V.maskL, 1.0, [V.mask_r])
    k.memset("pool", V.maskR, 1.0, [V.mask_r])
    k.memset("pool", r3(V.maskL, 8)[:, :, 63:64], 0.0, [V.mask_r])
    k.memset("pool", r3(V.maskR, 8)[:, :, 0:1], 0.0, [V.mask_r])
    g.modT = [r3(ar.f32(96), 48), r3(ar.f32(96), 48)]
    g.A1 = [r3(ar.f32(16), 8), r3(ar.f32(16), 8)]
    g.A2 = [r3(ar.f32(16), 8), r3(ar.f32(16), 8)]
    g.mod_r = [Res("mod0"), Res("mod1")]
    g.modrow = ar.f32(256)
    g.modrow_r = Res("modrow")

    g.h_off = ar.top
    g.hT = r3(ar.f32(NB * T), NB)
    g.hcT = r3(ar.f32(NB * TC), NB)
    g.h_res = [[Res(f"h{b}_{t}") for t in range(T // 512)] for b in range(NB)]
    g.hc_res = [[Res(f"hc{b}")] for b in range(NB)]
    g.aT = r3(ar.bf(NB * T), NB)
    g.acT = r3(ar.bf(NB * TC), NB)
    g.a_res = [[Res(f"a{b}_{t}") for t in range(T // 512)] for b in range(NB)]
    g.ac_res = [[Res(f"ac{b}")] for b in range(NB)]
    g.pmark = ar.mark()

    for it in mod_params_items(g, 0):
        it()
    g.bg = mod_params_items(g, 1)
    if stage < 2 or not BG_INTERLEAVE:
        while g.bg:
            g.bg.pop(0)()
    barrier(g)
    load_stream(g, I["x"], g.hT, g.h_res, T)
    load_stream(g, I["ctx"], g.hcT, g.hc_res, TC)
    if stage >= 1:
        modulate(g, g.hT, g.h_res, T, g.A1[0], g.modT[0], 0, 0, g.aT, g.a_res)
        modulate(g, g.hcT, g.hc_res, TC, g.A1[0], g.modT[0], 0, 1, g.acT, g.ac_res)
        barrier(g)
        if stage >= 2:
            ssd_layer(g)
        while g.bg:
            g.bg.pop(0)()
        barrier(g)
    if stage >= 3:
        modulate(g, g.hT, g.h_res, T, g.A2[0], g.modT[0], 24, 0, g.aT, g.a_res)
        modulate(g, g.hcT, g.hc_res, TC, g.A2[0], g.modT[0], 24, 1, g.acT, g.ac_res)
        ffn(g, 0, g.aT, g.a_res, T, True, g.hT, g.h_res, 0)
        ffn(g, 0, g.acT, g.ac_res, TC, False, g.hcT, g.hc_res, 1)
        barrier(g)
    if stage >= 4:
        modulate(g, g.hT, g.h_res, T, g.A1[1], g.modT[1], 0, 0, g.aT, g.a_res)
        conformer(g)
        barrier(g)
    if stage >= 5:
        modulate(g, g.hT, g.h_res, T, g.A2[1], g.modT[1], 24, 0, g.aT, g.a_res)
        ffn(g, 1, g.aT, g.a_res, T, True, g.hT, g.h_res, 0)
        barrier(g)
    if stage == 99:
        final_norm(g)
    else:
        dbg = nc.dram_tensor("dbg", [128, NB * T], F32, kind="ExternalOutput").ap()
        dbgc = nc.dram_tensor("dbgc", [128, NB * TC], F32, kind="ExternalOutput").ap()
        k.out_tokens.append(k.dma("sp", dbg, g.hT.rearrange("p a b -> p (a b)"), all_res(g.h_res), []))
        k.out_tokens.append(k.dma("sp", dbgc, g.hcT.rearrange("p a b -> p (a b)"), all_res(g.hc_res), []))
    return k.finish()


def wload(g, src2d, kblks, col0, ncols, row0=0):
    k = g.k
    n = kblks * ncols
    assert n <= 4096
    st = r3(g.wst[:, 0:n], kblks)
    b = g.w_i % 2
    g.w_i += 1
    wb = r3(g.wbf[b][:, 0:n], kblks)
    src = src2d[row0:row0 + kblks * 128, col0:col0 + ncols].rearrange("(kb p) n -> p kb n", p=128)
    rs = [g.wstres, g.wsth_res[0], g.wsth_res[1]]
    k.dma(WQ, st, src, [], rs)
    k.cp("pool" if b == 0 else "act", wb, st, rs, [g.wbfres[b]])
    return wb, g.wbfres[b]


class WStream:
    def __init__(self, g, specs, bufs, bres, ahead):
        self.g, self.specs, self.bufs, self.bres, self.ahead = g, specs, bufs, bres, ahead
        self.loaded = {}
        self.nxt = 0

    def _load(self, i):
        g = self.g
        k = g.k
        src2d, kblks, col0, ncols, row0 = self.specs[i]
        n = kblks * ncols
        b = i % len(self.bufs)
        if n <= 2048:
            h = g.w_i % 2
            stf, stres = g.wst[:, h * 2048:h * 2048 + n], g.wsth_res[h]
        else:
            stf, stres = g.wst[:, 0:n], g.wstres
        g.w_i += 1
        st = r3(stf, kblks)
        wb = r3(self.bufs[b][:, 0:n], kblks)
        src = src2d[row0:row0 + kblks * 128, col0:col0 + ncols].rearrange("(kb p) n -> p kb n", p=128)
        rs = [stres] if n <= 2048 else [g.wstres, g.wsth_res[0], g.wsth_res[1]]
        k.dma("sp", st, src, [], rs)
        k.cp("pool", wb, st, rs, [self.bres[b]])
        self.loaded[i] = (wb, self.bres[b])

    def get(self, i):
        while self.nxt <= min(i + self.ahead, len(self.specs) - 1):
            self._load(self.nxt)
            self.nxt += 1
        return self.loaded.pop(i)


def mod_params_items(g, i):
    k, ar, V = g.k, g.ar, g.V
    mr = g.mod_r[i]
    items = []

    loaded = {}

    def loader(cg):
        def run():
            h = cg % 2
            st = r3(g.wst[:, h * 2048:(h + 1) * 2048], 8)
            src = g.I["mod_w"][i][:, cg * 256:(cg + 1) * 256].rearrange("(kb p) n -> p kb n", p=128)
            k.dma("sp", st, src, [], [g.wsth_res[h]])
            loaded[cg] = (st, g.wsth_res[h])
        return run

    def compute(cg):
        def run():
            psb, pres = g.banks[7]
            w, wres = loaded.pop(cg)
            for kb in range(8):
                k.mmg(psb[0:2, 0:256], V.cs32[:, kb, :], w[:, kb, :], kb == 0, kb == 7, [wres, V.cs_r], [pres])
            k.cp("dve", g.modrow[0:2, :], psb[0:2, 0:256], [pres], [g.modrow_r])
            for j in range(2):
                k.tr(psb[:, 256 + j * 2:256 + (j + 1) * 2], g.modrow[0:2, j * 128:(j + 1) * 128],
                     g.c.ident_f[0:2, 0:2], [g.modrow_r, g.c.res], [pres])
            m0 = cg * 2
            k.tt("dve", g.modT[i][:, m0:m0 + 2, :], r3(psb[:, 256:260], 2),
                 V.modb[:, i * 48 + m0:i * 48 + m0 + 2].unsqueeze(2).broadcast_to([128, 2, 2]), ALU.add,
                 [pres, V.modb_r], [mr])
        return run

    def both(cg):
        def run():
            if cg not in loaded:
                loader(cg)()
            if cg + 1 < 24:
                loader(cg + 1)()
            compute(cg)()
        return run
    for cg in range(24):
        items.append(both(cg))

    def fin():
        k.stt("dve", g.A1[i], g.modT[i][:, 8:16, :], 1.0,
              V.n1w[:, i * 8:(i + 1) * 8].unsqueeze(2).broadcast_to([128, 8, 2]), ALU.add, ALU.mult,
              [mr, V.n1w_r], [mr])
        k.stt("dve", g.A2[i], g.modT[i][:, 32:40, :], 1.0,
              V.n2w[:, i * 8:(i + 1) * 8].unsqueeze(2).broadcast_to([128, 8, 2]), ALU.add, ALU.mult,
              [mr, V.n2w_r], [mr])
    items.append(fin)
    return items


def modulate(g, srcT, sres, L, A, modT, sh0, s_, dstT, dres, mres=None):
    k, ar, c = g.k, g.ar, g.c
    mres = g.mod_r[0] if modT is g.modT[0] else g.mod_r[1]
    m = ar.mark()
    tw = min(512, L)
    sq = [ar.bf(tw), ar.bf(tw)]
    sqres = [Res("sq0"), Res("sq1")]
    rstd = ar.f32(tw)
    rres = Res("rstd")
    tmp = [ar.f32(tw), ar.f32(tw)]
    tres = [Res("t0"), Res("t1")]
    for tl in range(L // tw):
        sl = slice(tl * tw, (tl + 1) * tw)
        ps, pres = g.bank()
        ps = ps[:, 0:tw]
        for blk in range(NB):
            b = blk % 2
            k.act(sq[b], srcT[:, blk, sl], AF.Square, [sres[blk][tl]], [sqres[b]])
            k.mm(ps, c.ones_b, sq[b], blk == 0, blk == NB - 1, [sqres[b], c.res], [pres])
        k.ts("dve", rstd, ps, 1.0 / D, EPS, ALU.mult, ALU.add, [pres], [rres])
        k.act(rstd, rstd, AF.Sqrt, [rres], [rres])
        k.op("dve", lambda e: e.reciprocal(rstd, rstd), [rres], [rres])
        for blk in range(NB):
            b = blk % 2
            k.tt("dve", tmp[b], srcT[:, blk, sl], rstd, ALU.mult, [sres[blk][tl], rres], [tres[b]])
            k.act(dstT[:, blk, sl], tmp[b], AF.Identity, [tres[b], mres], [dres[blk][tl]],
                  bias=modT[:, sh0 + blk, s_:s_ + 1], scale=A[:, blk, s_:s_ + 1])
    barrier(g)
    ar.release(m)


def ffn(g, layer, aT, a_res, L, grid, hT, h_res, s_):
    k, ar, c, V = g.k, g.ar, g.c, g.V
    m = ar.mark()
    tw = min(512, L)
    nt = L // tw
    HL = 66
    W = HL + L + HL
    gpre = ar.bf(W)
    gL = ar.bf(W) if grid else None
    gR = ar.bf(W) if grid else None
    gres = Res("gpre")
    k.memset("pool", gpre, 0.0, [gres])
    if grid:
        k.memset("pool", gL, 0.0, [gres])
        k.memset("pool", gR, 0.0, [gres])
    hid = [ar.bf(L), ar.bf(L)]
    hres = [Res("hid0"), Res("hid1")]
    sg = [ar.f32(tw), ar.f32(tw)]
    sgres = [Res("sg0"), Res("sg1")]
    diag = [ar.bf(128) for _ in range(9)]
    dres = Res("diag")
    wup = g.I["ffn_w_up"][layer]
    wdn = g.I["ffn_w_down"][layer]
    specs = []
    for fp_ in range(NFB // 2):
        specs.append((wup, 8, fp_ * 256, 256, 0))
        specs.append((wup, 8, FH + fp_ * 256, 256, 0))
        specs.append((wdn, 2, 0, D, fp_ * 256))
    wsm = WStream(g, specs, [ar.bf(2048) for _ in range(6)], [Res(f"fw{i_}") for i_ in range(6)], 3)
    g2 = g.modT[layer]
    mres = g.mod_r[layer]
    taps = [(ky, kx) for ky in range(3) for kx in range(3)] if grid else [(1, kx) for kx in range(3)]
    for fp in range(NFB // 2):
        wv, wvres = wsm.get(fp * 3)
        wg, wgres = wsm.get(fp * 3 + 1)
        for fi in range(2):
            f = fp * 2 + fi
            for tl in range(nt):
                ps, pres = g.bank()
                ps = ps[:, 0:tw]
                for kb in range(NB):
                    k.mmg(ps, wg[:, kb, fi * 128:(fi + 1) * 128], aT[:, kb, tl * tw:(tl + 1) * tw],
                         kb == 0, kb == NB - 1, [wgres, a_res[kb][tl]], [pres])
                dsl = slice(HL + tl * tw, HL + (tl + 1) * tw)
                k.cp("act", gpre[:, dsl], ps, [pres], [gres])
                if grid:
                    k.tt("dve", gL[:, dsl], ps, V.maskL, ALU.mult, [pres, V.mask_r], [gres])
                    k.tt("dve", gR[:, dsl], ps, V.maskR, ALU.mult, [pres, V.mask_r], [gres])
            for ti, (ky, kx) in enumerate(taps):
                col = layer * 9 * NFB + (ky * 3 + kx) * NFB + f
                k.ts("dve", diag[ti], c.ident_b, V.fcw[:, col:col + 1], None, ALU.mult, None,
                     [c.res, V.fcw_r], [dres])
            for tl in range(nt):
                ps, pres = g.bank()
                ps = ps[:, 0:tw]
                for ti, (ky, kx) in enumerate(taps):
                    srcb = gpre if (not grid or kx == 1) else (gL if kx == 0 else gR)
                    off = HL + tl * tw + ((ky - 1) * 64 if grid else 0) + (kx - 1)
                    k.mmg(ps, diag[ti], srcb[:, off:off + tw], ti == 0, ti == len(taps) - 1,
                         [dres, gres], [pres])
                b = tl % 2
                cbc = layer * NFB + f
                k.act(sg[b], ps, AF.Silu, [pres, V.fcb_r], [sgres[b]], bias=V.fcb[:, cbc:cbc + 1])
                ps2, pres2 = g.bank()
                ps2 = ps2[:, 0:tw]
                for kb in range(NB):
                    k.mmg(ps2, wv[:, kb, fi * 128:(fi + 1) * 128], aT[:, kb, tl * tw:(tl + 1) * tw],
                         kb == 0, kb == NB - 1, [wvres, a_res[kb][tl]], [pres2])
                k.tt("dve", hid[fi][:, tl * tw:(tl + 1) * tw], ps2, sg[b], ALU.mult,
                     [pres2, sgres[b]], [hres[fi]])
        wd, wdres = wsm.get(fp * 3 + 2)
        for db in range(NB):
            for tl in range(nt):
                ps, pres = g.bank()
                ps = ps[:, 0:tw]
                for fi in range(2):
                    k.mmg(ps, wd[:, fi, db * 128:(db + 1) * 128], hid[fi][:, tl * tw:(tl + 1) * tw],
                         fi == 0, fi == 1, [wdres, hres[fi]], [pres])
                hsl = hT[:, db, tl * tw:(tl + 1) * tw]
                k.stt("dve", hsl, ps, g2[:, 40 + db, s_:s_ + 1], hsl, ALU.mult, ALU.add,
                      [pres, mres, h_res[db][tl]], [h_res[db][tl]])
    barrier(g)
    ar.release(m)


def conformer(g):
    k, ar, c, V = g.k, g.ar, g.c, g.V
    m = ar.mark()
    HL = 16
    W = HL + T + HL
    glu = [ar.bf(W) for _ in range(NB)]
    glu_r = [Res(f"glu{i}") for i in range(NB)]
    sgm = [ar.f32(512), ar.f32(512)]
    sgm_r = [Res("sgm0"), Res("sgm1")]
    w1 = g.I["conf_w_pw1"]
    nt = T // 512
    cbufs = [g.wbf[i_ // 4][:, (i_ % 4) * 1024:(i_ % 4 + 1) * 1024] for i_ in range(8)]
    cbres = [Res(f"cw{i_}") for i_ in range(8)]
    specs = []
    for cb_ in range(NB):
        specs.append((w1, 8, cb_ * 128, 128, 0))
        specs.append((w1, 8, D + cb_ * 128, 128, 0))
    ws1 = WStream(g, specs, cbufs, cbres, 4)
    for cb in range(NB):
        k.memset("pool", glu[cb], 0.0, [glu_r[cb]])
        wa, wares = ws1.get(cb * 2)
        wgt, wgres = ws1.get(cb * 2 + 1)
        for tl in range(nt):
            sl = slice(tl * 512, (tl + 1) * 512)
            psg, presg = g.bank()
            for kb in range(NB):
                k.mmg(psg, wgt[:, kb, :], g.aT[:, kb, sl], kb == 0, kb == NB - 1,
                     [wgres, g.a_res[kb][tl]], [presg])
            b = tl % 2
            k.act(sgm[b], psg, AF.Sigmoid, [presg, V.cb1_r], [sgm_r[b]], bias=V.cb1[:, 8 + cb:9 + cb])
            psa, presa = g.bank()
            for kb in range(NB):
                k.mmg(psa, wa[:, kb, :], g.aT[:, kb, sl], kb == 0, kb == NB - 1,
                     [wares, g.a_res[kb][tl]], [presa])
            k.stt("dve", glu[cb][:, HL + tl * 512:HL + (tl + 1) * 512], psa, V.cb1[:, cb:cb + 1], sgm[b],
                  ALU.add, ALU.mult, [presa, V.cb1_r, sgm_r[b]], [glu_r[cb]])
    barrier(g)
    cv = g.aT
    cv_r = g.a_res
    diag = [ar.bf(128) for _ in range(31)]
    dres = Res("cdiag")
    for cb in range(NB):
        for tp in range(31):
            col = tp * 8 + cb
            k.ts("dve", diag[tp], c.ident_b, V.cdw[:, col:col + 1], None, ALU.mult, None,
                 [c.res, V.cdw_r], [dres])
        for tl in range(nt):
            ps, pres = g.bank()
            for tp in range(31):
                off = HL + tl * 512 + tp - 15
                k.mmg(ps, diag[tp], glu[cb][:, off:off + 512], tp == 0, tp == 30, [dres, glu_r[cb]], [pres])
            k.act(cv[:, cb, tl * 512:(tl + 1) * 512], ps, AF.Identity, [pres, V.cbdw_r], [cv_r[cb][tl]],
                  bias=V.cbdw[:, cb:cb + 1])
    barrier(g)
    sq = [ar.bf(512), ar.bf(512)]
    sq_r = [Res("csq0"), Res("csq1")]
    mean = ar.f32(512)
    rstd = ar.f32(512)
    nmr = ar.f32(512)
    st_r = Res("lnstat")
    t1 = sgm
    t1_r = [Res("lt0"), Res("lt1")]
    hln = glu
    for tl in range(nt):
        sl = slice(tl * 512, (tl + 1) * 512)
        ps1, pres1 = g.bank()
        ps2, pres2 = g.bank()
        for cb in range(NB):
            b = cb % 2
            k.mmg(ps1, c.ones_b, cv[:, cb, sl], cb == 0, cb == NB - 1, [c.res, cv_r[cb][tl]], [pres1])
            k.tt("dve", sq[b], cv[:, cb, sl], cv[:, cb, sl], ALU.mult, [cv_r[cb][tl]], [sq_r[b]])
            k.mm(ps2, c.ones_b, sq[b], cb == 0, cb == NB - 1, [c.res, sq_r[b]], [pres2])
        k.ts("dve", mean, ps1, 1.0 / D, None, ALU.mult, None, [pres1], [st_r])
        k.tt("dve", nmr, mean, mean, ALU.mult, [st_r], [st_r])
        k.stt("dve", rstd, ps2, 1.0 / D, nmr, ALU.mult, ALU.subtract, [pres2, st_r], [st_r])
        k.ts("dve", rstd, rstd, EPS, None, ALU.add, None, [st_r], [st_r])
        k.act(rstd, rstd, AF.Sqrt, [st_r], [st_r])
        k.op("dve", lambda e: e.reciprocal(rstd, rstd), [st_r], [st_r])
        k.stt("dve", nmr, mean, -1.0, rstd, ALU.mult, ALU.mult, [st_r], [st_r])
        for cb in range(NB):
            b = cb % 2
            k.tt("dve", t1[b], cv[:, cb, sl], rstd, ALU.mult, [cv_r[cb][tl], st_r], [t1_r[b]])
            k.tt("dve", t1[b], t1[b], nmr, ALU.add, [t1_r[b], st_r], [t1_r[b]])
            k.act(hln[cb][:, sl], t1[b], AF.Silu, [t1_r[b], V.clnw_r, V.clnb_r], [glu_r[cb]],
                  bias=V.clnb[:, cb:cb + 1], scale=V.clnw[:, cb:cb + 1])
    barrier(g)
    w2 = g.I["conf_w_pw2"]
    ws2 = WStream(g, [(w2, 8, db_ * 128, 128, 0) for db_ in range(NB)], cbufs, cbres, 3)
    g1 = g.modT[1]
    mres = g.mod_r[1]
    yb = sgm
    yb_r = [Res("yb0"), Res("yb1")]
    for db in range(NB):
        wp, wpres = ws2.get(db)
        for tl in range(nt):
            sl = slice(tl * 512, (tl + 1) * 512)
            ps, pres = g.bank()
            for cb in range(NB):
                k.mmg(ps, wp[:, cb, :], hln[cb][:, sl], cb == 0, cb == NB - 1, [wpres, glu_r[cb]], [pres])
            b = tl % 2
            k.ts("dve", yb[b], ps, V.cb2[:, db:db + 1], g1[:, 16 + db, 0:1], ALU.add, ALU.mult,
                 [pres, V.cb2_r, mres], [yb_r[b]])
            k.tt("dve", g.hT[:, db, sl], g.hT[:, db, sl], yb[b], ALU.add,
                 [yb_r[b], g.h_res[db][tl]], [g.h_res[db][tl]])
    barrier(g)
    ar.release(m)


def psbf(ps):
    return ps.bitcast(BF16)


def ssd_layer(g):
    k, ar, c, V, nc = g.k, g.ar, g.c, g.V, g.k.nc
    top_save = ar.top
    ar.top = g.h_off
    S = Ctx()
    g.S = S
    S.cres = Res("ssdc")

    tmpf = None

    def tri(dst_b, fill, pattern, cm, base=0, view=None, init=1.0):
        src = tmpf[:, 0:dst_b.shape[1]]
        k.memset("pool", src, init, [S.cres])
        vv = src if view is None else view(src)
        k.op("pool", lambda e: e.affine_select(out=vv, in_=vv, pattern=pattern, compare_op=ALU.is_ge,
                                               fill=fill, base=base, channel_multiplier=cm),
             [S.cres], [S.cres])
        k.cp("pool", dst_b, src, [S.cres], [S.cres])
    S.U = [ar.bf(128), ar.bf(128)]
    S.Vm = [ar.bf(128), ar.bf(128)]
    S.NEG = [ar.bf(512), ar.bf(512)]
    S.dtb = ar.f32(64)
    S.Abc = ar.f32(64)
    S.Dbc = ar.f32(32)
    S.nwbc = ar.f32(DI)
    k.dma("sp", S.dtb, g.I["ssd_dt_bias"].partition_broadcast(128), [], [S.cres])
    k.dma("sp", S.Abc, g.I["ssd_a_log"].partition_broadcast(128), [], [S.cres])
    k.dma("sp", S.Dbc, g.I["ssd_d"].partition_broadcast(128), [], [S.cres])
    k.dma("sp", S.nwbc, g.I["ssd_norm_w"].partition_broadcast(128), [], [S.cres])
    k.act(S.Abc, S.Abc, AF.Exp, [S.cres], [S.cres])
    k.ts("dve", S.Abc, S.Abc, -1.0, None, ALU.mult, None, [S.cres], [S.cres])
    NCH = (TC + T) // 128
    S.dt = r3(ar.f32(NCH * 64), NCH)
    S.dthi = r3(ar.bf(NCH * 64), NCH)
    S.dtlo = r3(ar.bf(NCH * 64), NCH)
    S.ndthi = r3(ar.bf(NCH * 64), NCH)
    S.ndtlo = r3(ar.bf(NCH * 64), NCH)
    S.dt_r = Res("dtall")
    S.St = [ar.f32(DI), ar.f32(DI)]
    S.St_r = [[Res(f"S{d}_{q}") for q in range(4)] for d in range(2)]
    for d_ in range(2):
        k.memset("pool", S.St[d_], 0.0, S.St_r[d_])
    barrier(g)
    base_mark = ar.mark()
    tmpf = ar.f32(512)
    tri(S.U[0], 0.0, [[1, 128]], -1)
    tri(S.U[1], 0.0, [[-1, 128]], 1)
    tri(S.Vm[0], 0.0, [[-1, 128]], 1, base=-1)
    tri(S.Vm[1], 0.0, [[1, 128]], -1, base=-1)
    tri(S.NEG[0], -30000.0, [[0, 4], [1, 128]], -1, view=lambda a: r3(a, 4), init=0.0)
    tri(S.NEG[1], -30000.0, [[0, 4], [-1, 128]], 1, view=lambda a: r3(a, 4), init=0.0)
    barrier(g)
    ar.release(base_mark)
    seqs = []
    for nm, L, aT, a_res, ch0 in (("c", TC, g.acT, g.ac_res, 0), ("l", T, g.aT, g.a_res, TC // 128)):
        q_ = Ctx()
        q_.L, q_.aT, q_.a_res, q_.ch0, q_.nm = L, aT, a_res, ch0, nm
        q_.ZS = nc.dram_tensor("ZS" + nm, [L, DI], F32, kind="Internal").ap()
        q_.YP = nc.dram_tensor("YP" + nm, [L, DI], F32, kind="Internal").ap()
        q_.Y1 = nc.dram_tensor("Y1" + nm, [L, D], F32, kind="Internal").ap()
        q_.XT = nc.dram_tensor("XT" + nm, [L, DI], BF16, kind="Internal").ap()
        q_.BK = nc.dram_tensor("BK" + nm, [L, 1024], BF16, kind="Internal").ap()
        q_.BTd = nc.dram_tensor("BT" + nm, [L // 128, 128, 1024], BF16, kind="Internal").ap()
        q_.CTd = nc.dram_tensor("CT" + nm, [L // 128, 128, 1024], BF16, kind="Internal").ap()
        q_.dres = Res("dram" + nm)
        seqs.append(q_)
    for q_ in seqs:
        ar.top = top_save
        ssd_phaseA(g, q_)
        barrier(g)
    ar.release(base_mark)
    ssd_sweeps(g, seqs)
    barrier(g)
    ar.top = top_save
    S.dres_all = Res('dres_all')
    load_stream(g, g.I["x"], g.hT, g.h_res, T, ysrc=seqs[1].Y1, gate=(g.modT[0], 16, 0, g.mod_r[0]))
    load_stream(g, g.I["ctx"], g.hcT, g.hc_res, TC, ysrc=seqs[0].Y1, gate=(g.modT[0], 16, 1, g.mod_r[0]))


def ssd_phaseA(g, q_):
    k, ar, c, V, S = g.k, g.ar, g.c, g.V, g.S
    L, aT, a_res = q_.L, q_.aT, q_.a_res
    tw = min(512, L)
    nt = L // tw
    nch = L // 128
    win = g.I["ssd_w_in"]
    tmp = [ar.f32(64), ar.f32(64)]
    tmp_r = [Res("dtt0"), Res("dtt1")]
    specs = [(win, 8, DI + CONVD, 64, 0)] + [(win, 8, cg_ * 512, 512, 0) for cg_ in range(4)] + \
            [(win, 8, DI + f_ * 512, 512, 0) for f_ in range(8)]
    wsa = WStream(g, specs, g.wbf, g.wbfres, 1)
    wdt, wdtres = wsa.get(0)
    for ch in range(nch):
        gch = q_.ch0 + ch
        ps, pres = g.bank()
        ps = ps[:, 0:64]
        tl = (ch * 128) // tw
        for kb in range(NB):
            k.mmg(ps, aT[:, kb, ch * 128:(ch + 1) * 128], wdt[:, kb, :], kb == 0, kb == NB - 1,
                 [wdtres, a_res[kb][tl]], [pres])
        b = ch % 2
        k.tt("dve", tmp[b], ps, S.dtb, ALU.add, [pres, S.cres], [tmp_r[b]])
        k.act(S.dt[:, gch, :], tmp[b], AF.Softplus, [tmp_r[b]], [S.dt_r])
        k.tt("dve", tmp[b], S.dt[:, gch, :], S.Abc, ALU.mult, [S.dt_r, S.cres], [tmp_r[b]])
        k.cp("dve", S.dthi[:, gch, :], tmp[b], [tmp_r[b]], [S.dt_r])
        k.tt("dve", S.dtlo[:, gch, :], tmp[b], S.dthi[:, gch, :], ALU.subtract, [tmp_r[b], S.dt_r], [S.dt_r])
        k.ts("pool", S.ndthi[:, gch, :], S.dthi[:, gch, :], -1.0, None, ALU.mult, None, [S.dt_r], [S.dt_r])
        k.ts("pool", S.ndtlo[:, gch, :], S.dtlo[:, gch, :], -1.0, None, ALU.mult, None, [S.dt_r], [S.dt_r])
    zs = [ar.f32(512), ar.f32(512)]
    zs_r = [Res("zs0"), Res("zs1")]
    zi = 0
    for cg in range(4):
        wz, wzres = wsa.get(1 + cg)
        for ch in range(nch):
            tl = (ch * 128) // tw
            ps, pres = g.bank()
            for kb in range(NB):
                k.mmg(ps, aT[:, kb, ch * 128:(ch + 1) * 128], wz[:, kb, :], kb == 0, kb == NB - 1,
                     [wzres, a_res[kb][tl]], [pres])
            b = zi % 2
            zi += 1
            k.act(zs[b], ps, AF.Silu, [pres], [zs_r[b]])
            k.dma("sp", q_.ZS[ch * 128:(ch + 1) * 128, cg * 512:(cg + 1) * 512], zs[b], [zs_r[b]], [])
    pres_ = [ar.bf(L + 4), ar.bf(L + 4)]
    pre_rs = [Res("pre0"), Res("pre1")]
    xcs = [ar.bf(L), ar.bf(L)]
    xc_rs = [Res("xc0"), Res("xc1")]
    tokTs = [r3(ar.bf(nch * 128), nch), r3(ar.bf(nch * 128), nch)]
    tokT_rs = [Res("tokT0"), Res("tokT1")]
    diags = [[ar.bf(128) for _ in range(5)] for _ in range(2)]
    dg_rs = [Res("sdiag0"), Res("sdiag1")]
    for b in range(2):
        k.memset("pool", pres_[b], 0.0, [pre_rs[b]])
    for fbg in range(8):
        w, wres = wsa.get(5 + fbg)
        for fi in range(4):
            fb = fbg * 4 + fi
            pb_ = fb % 2
            pre, pre_r, xc, xc_r = pres_[pb_], pre_rs[pb_], xcs[pb_], xc_rs[pb_]
            tokT, tokT_r, diag, dg_r = tokTs[pb_], tokT_rs[pb_], diags[pb_], dg_rs[pb_]
            for tl in range(nt):
                ps, pres = g.bank()
                ps = ps[:, 0:tw]
                for kb in range(NB):
                    k.mmg(ps, w[:, kb, fi * 128:(fi + 1) * 128], aT[:, kb, tl * tw:(tl + 1) * tw],
                         kb == 0, kb == NB - 1, [wres, a_res[kb][tl]], [pres])
                k.cp("act", pre[:, 2 + tl * tw:2 + (tl + 1) * tw], ps, [pres], [pre_r])
            for tp in range(5):
                col = tp * 32 + fb
                k.ts("dve", diag[tp], c.ident_b, V.scw[:, col:col + 1], None, ALU.mult, None,
                     [c.res, V.scw_r], [dg_r])
            for tl in range(nt):
                ps, pres = g.bank()
                ps = ps[:, 0:tw]
                for tp in range(5):
                    k.mmg(ps, diag[tp], pre[:, tl * tw + tp:tl * tw + tp + tw], tp == 0, tp == 4,
                         [dg_r, pre_r], [pres])
                k.act(xc[:, tl * tw:(tl + 1) * tw], ps, AF.Silu, [pres, V.scb_r], [xc_r],
                      bias=V.scb[:, fb:fb + 1])
            if fb < 24:
                n4 = 2 if nch < 4 else 4
                for c4 in range(nch // n4):
                    ps, pres = g.bank()
                    pb = psbf(ps)
                    for j in range(n4):
                        ch = c4 * n4 + j
                        k.tr(pb[:, j * 128:(j + 1) * 128], xc[:, ch * 128:(ch + 1) * 128], c.ident_b,
                             [xc_r, c.res], [pres])
                    k.cp("dve", tokT[:, c4 * n4:(c4 + 1) * n4, :], r3(pb[:, 0:n4 * 128], n4), [pres], [tokT_r])
                if fb < 16:
                    dst = q_.XT.rearrange("(c p) f -> p c f", p=128)[:, :, fb * 128:(fb + 1) * 128]
                else:
                    dst = q_.BK.rearrange("(c p) f -> p c f", p=128)[:, :, (fb - 16) * 128:(fb - 15) * 128]
                k.dma("sp", dst, tokT, [tokT_r], [])
            if fb >= 16:
                gi = (fb - 16) % 8
                dd = q_.BTd if fb < 24 else q_.CTd
                dst = dd.rearrange("c n (g t) -> n c g t", g=8)[:, :, gi, :]
                k.dma("sp", dst, r3(xc, nch), [xc_r], [])


PIPE = True
WQ = "sp"
BG_INTERLEAVE = True


def ssd_sweeps(g, seqs):
    k, ar, c, V, S = g.k, g.ar, g.c, g.V, g.S
    wout = r3(ar.bf(16 * D), 16)
    wout_r = Res("wout")
    for qc in range(4):
        st = r3(g.wst[:, 0:4096], 16)
        rs = [g.wstres, g.wsth_res[0], g.wsth_res[1]]
        k.dma("sp", st, g.I["ssd_w_out"][:, qc * 256:(qc + 1) * 256].rearrange("(kb p) n -> p kb n", p=128),
              [], rs)
        k.cp("pool", wout[:, :, qc * 256:(qc + 1) * 256], st, rs, [wout_r])
    Sbf = ar.bf(DI)
    Sbf_r = [Res(f"Sbf{q}") for q in range(8)]
    St_r = [[Res(f"St{d}_{q}") for q in range(8)] for d in range(2)]
    LB = []
    for P in range(3):
        b = Ctx()
        b.xtok = ar.bf(DI); b.btok = ar.bf(1024)
        b.BT = r3(ar.bf(1024), 8); b.CT = r3(ar.bf(1024), 8)
        b.x_r = Res(f"inx{P}"); b.b_r = Res(f"inb{P}"); b.BT_r = Res(f"inBT{P}"); b.CT_r = Res(f"inCT{P}")
        LB.append(b)
    CH = []
    for P in range(2):
        b = Ctx()
        b.cbT = r3(ar.bf(1024), 8); b.cb_r = Res(f"cbT{P}")
        b.eall = ar.f32(96); b.dtdec = ar.f32(32); b.e_r = Res(f"eall{P}")
        b.xdt = ar.bf(DI); b.xdd = ar.bf(DI); b.xdt_r = Res(f"xdt{P}"); b.xdd_r = Res(f"xdd{P}")
        CH.append(b)
    UN = []
    for P in range(3):
        b = Ctx()
        b.E = ar.bf(512); b.E_r = Res(f"E{P}")
        b.MT = ar.bf(512); b.MT_r = Res(f"MT{P}")
        UN.append(b)
    UY = []
    for P in range(2):
        b = Ctx()
        b.ytmp = ar.f32(256); b.ytmp_r = Res(f"ytmp{P}")
        b.ytmp2 = ar.f32(256); b.ytmp2_r = Res(f"ytmpb{P}")
        b.ydir = ar.f32(256); b.ydir_r = Res(f"ydir{P}")
        UY.append(b)
    UL = []
    for P in range(3):
        b = Ctx()
        b.yp = ar.f32(256); b.yp_r = Res(f"ypt{P}")
        b.zs = ar.f32(256); b.zs_r = Res(f"zst{P}")
        UL.append(b)
    yg = ar.f32(DI); yg_r = Res("yg")
    yn = ar.bf(DI); yn_r = Res("yn")
    ssq = ar.f32(16); ssq_r = Res("ssq")
    ynT = r3(ar.bf(DI), 16); ynT_r = Res("ynT")
    B = g.banks
    seg_b = [B[0], B[1], B[2]]
    y_b = [B[3], B[4]]
    os_b = [B[5], B[6]]
    pro_b = [B[7], B[7]]

    def loads(q_, ci, ch, d):
        Lb = LB[ci % 3]
        rows = slice(ch * 128, (ch + 1) * 128)
        k.dma("sp", Lb.BT.rearrange("p a b -> p (a b)"), q_.BTd[ch], [], [Lb.BT_r])
        k.dma("sp", Lb.CT.rearrange("p a b -> p (a b)"), q_.CTd[ch], [], [Lb.CT_r])
        k.dma("sp", Lb.xtok, q_.XT[rows, :], [], [Lb.x_r])
        k.dma("sp", Lb.btok, q_.BK[rows, :], [], [Lb.b_r])

    def prologue(q_, ci, ch, d):
        P = CH[ci % 2]
        Lb = LB[ci % 3]
        gch = q_.ch0 + ch
        dsl = slice(d * 32, (d + 1) * 32)
        for half in range(2):
            ps, pres = pro_b[0]
            for gl in range(4):
                gg = half * 4 + gl
                k.mmg(ps[:, gl * 128:(gl + 1) * 128], Lb.BT[:, gg, :], Lb.CT[:, gg, :], gl == 0, gl == 3,
                      [Lb.BT_r, Lb.CT_r], [pres])
            k.cp("act", P.cbT[:, half * 4:(half + 1) * 4, :], r3(ps, 4), [pres], [P.cb_r])
        ps, pres = pro_b[1]
        first = True
        for ci_, lh in enumerate((S.Vm[d], S.U[d], c.ones_b)):
            k.mmg(ps[:, ci_ * 32:(ci_ + 1) * 32], lh, S.dthi[:, gch, dsl], first, False,
                  [S.dt_r, S.cres, c.res], [pres])
            first = False
            k.mmg(ps[:, ci_ * 32:(ci_ + 1) * 32], lh, S.dtlo[:, gch, dsl], False, ci_ == 2,
                  [S.dt_r, S.cres, c.res], [pres])
        k.act(P.eall, ps[:, 0:96], AF.Exp, [pres], [P.e_r])
        k.tt("dve", P.dtdec, S.dt[:, gch, dsl], P.eall[:, 0:32], ALU.mult, [S.dt_r, P.e_r], [P.e_r])
        k.tt("pool", r3(P.xdt, 32), r3(Lb.xtok, 32),
             S.dt[:, gch, dsl].unsqueeze(2).broadcast_to([128, 32, 64]), ALU.mult, [Lb.x_r, S.dt_r], [P.xdt_r])
        k.tt("pool", r3(P.xdd, 32), r3(Lb.xtok, 32), P.dtdec.unsqueeze(2).broadcast_to([128, 32, 64]),
             ALU.mult, [Lb.x_r, P.e_r], [P.xdd_r])

    def seg(q_, ci, ch, d, gg, ui):
        P = CH[ci % 2]
        U_ = UN[ui % 3]
        gch = q_.ch0 + ch
        if d == 1:
            L_ = UL[ui % 3]
            rows_ = slice(ch * 128, (ch + 1) * 128)
            gc_ = slice(gg * 256, (gg + 1) * 256)
            k.dma("sp", L_.yp, q_.YP[rows_, gc_], [], [L_.yp_r])
            k.dma("sp", L_.zs, q_.ZS[rows_, gc_], [], [L_.zs_r])
        ps, pres = seg_b[ui % 3]
        first = True
        for hl in range(4):
            h = d * 32 + gg * 4 + hl
            for arr in (S.dthi, S.dtlo):
                k.mmg(ps[:, hl * 128:(hl + 1) * 128], arr[:, gch, h:h + 1].broadcast_to([128, 128]), S.U[d],
                     first, False, [S.dt_r, S.cres], [pres])
                first = False
        hs = d * 32 + gg * 4
        for arr in (S.ndthi, S.ndtlo):
            k.mmg(ps, S.U[d], arr[:, gch, hs:hs + 4].unsqueeze(2).broadcast_to([128, 4, 128]), False, False,
                 [S.dt_r, S.cres], [pres])
        k.mmg(ps, c.ident_b, S.NEG[d], False, True, [c.res, S.cres], [pres])
        k.act(U_.E, ps, AF.Exp, [pres], [U_.E_r])
        k.tt("pool", r3(U_.MT, 4), r3(U_.E, 4), P.cbT[:, gg, :].unsqueeze(1).broadcast_to([128, 4, 128]),
             ALU.mult, [U_.E_r, P.cb_r], [U_.MT_r])

    def rest(q_, ci, ch, d, gg, ui):
        P = CH[ci % 2]
        Lb = LB[ci % 3]
        U_ = UN[ui % 3]
        Y_ = UY[ui % 2]
        rows = slice(ch * 128, (ch + 1) * 128)
        gc = slice(gg * 256, (gg + 1) * 256)
        psY, presY = y_b[ui % 2]
        for hl in range(4):
            h = gg * 4 + hl
            k.mmg(psY[:, hl * 64:(hl + 1) * 64], U_.MT[:, hl * 128:(hl + 1) * 128], P.xdt[:, h * 64:(h + 1) * 64],
                 hl == 0, hl == 3, [U_.MT_r, P.xdt_r], [presY])
        psO, presO = os_b[ui % 2]
        k.mmg(psO[:, 0:256], Lb.CT[:, gg, :], Sbf[:, gc], True, False, [Lb.CT_r, Sbf_r[gg]], [presO])
        k.mmg(psO[:, 256:512], Lb.btok[:, gg * 128:(gg + 1) * 128], P.xdd[:, gc], False, True,
             [Lb.b_r, P.xdd_r], [presO])
        k.tt("dve", r3(Y_.ytmp, 4), r3(psO[:, 0:256], 4),
             P.eall[:, 32 + gg * 4:32 + (gg + 1) * 4].unsqueeze(2).broadcast_to([128, 4, 64]), ALU.mult,
             [presO, P.e_r], [Y_.ytmp_r])
        k.tt("dve", Y_.ydir, psY[:, 0:256], Y_.ytmp, ALU.add, [presY, Y_.ytmp_r], [Y_.ydir_r])
        if d == 0:
            k.tt("dve", r3(Y_.ytmp2, 4), r3(Lb.xtok[:, gc], 4),
                 S.Dbc[:, gg * 4:(gg + 1) * 4].unsqueeze(2).broadcast_to([128, 4, 64]), ALU.mult,
                 [Lb.x_r, S.cres], [Y_.ytmp2_r])
            k.tt("dve", Y_.ydir, Y_.ydir, Y_.ytmp2, ALU.add, [Y_.ytmp2_r, Y_.ydir_r], [Y_.ydir_r])
            k.dma("sp", q_.YP[rows, gc], Y_.ydir, [Y_.ydir_r], [])
        else:
            L_ = UL[ui % 3]
            k.tt("dve", Y_.ydir, Y_.ydir, L_.yp, ALU.add, [Y_.ydir_r, L_.yp_r], [Y_.ydir_r])
            k.tt("dve", yg[:, gc], Y_.ydir, L_.zs, ALU.mult, [Y_.ydir_r, L_.zs_r], [yg_r])
        st = S.St[d][:, gc]
        k.tt("pool", r3(st, 4), r3(st, 4),
             P.eall[:, 64 + gg * 4:64 + (gg + 1) * 4].unsqueeze(2).broadcast_to([128, 4, 64]), ALU.mult,
             [St_r[d][gg], P.e_r], [St_r[d][gg]])
        k.tt("dve", st, st, psO[:, 256:512], ALU.add, [St_r[d][gg], presO], [St_r[d][gg]])
        k.cp("act", Sbf[:, gc], st, [St_r[d][gg]], [Sbf_r[gg]])

    def epilogue(q_, ci, ch, d, gg, ui):
        rows = slice(ch * 128, (ch + 1) * 128)
        sqv = yn.bitcast(F32)
        yout = yn.bitcast(F32)

        def p0():
            for hf in range(2):
                hs = slice(hf * 1024, (hf + 1) * 1024)
                k.tt("dve", sqv, yg[:, hs], yg[:, hs], ALU.mult, [yg_r], [yn_r])
                k.op("dve", lambda e, hf=hf: e.reduce_sum(ssq[:, hf:hf + 1], sqv, axis=AX.X), [yn_r], [ssq_r])
            k.tt("dve", ssq[:, 8:9], ssq[:, 0:1], ssq[:, 1:2], ALU.add, [ssq_r], [ssq_r])
            k.ts("dve", ssq[:, 9:10], ssq[:, 8:9], 1.0 / DI, EPS, ALU.mult, ALU.add, [ssq_r], [ssq_r])
            k.act(ssq[:, 9:10], ssq[:, 9:10], AF.Sqrt, [ssq_r], [ssq_r])
            k.op("dve", lambda e: e.reciprocal(ssq[:, 10:11], ssq[:, 9:10]), [ssq_r], [ssq_r])
            k.stt("dve", yn, yg, ssq[:, 10:11], S.nwbc, ALU.mult, ALU.mult, [yg_r, ssq_r, S.cres], [yn_r])

        def ptr(f4):
            def run():
                ps, pres = g.banks[7]
                pb = psbf(ps)
                for j in range(4):
                    fb = f4 * 4 + j
                    k.tr(pb[:, j * 128:(j + 1) * 128], yn[:, fb * 128:(fb + 1) * 128], c.ident_b,
                         [yn_r, c.res], [pres])
                k.cp("act", ynT[:, f4 * 4:(f4 + 1) * 4, :], r3(pb[:, 0:512], 4), [pres], [ynT_r])
            return run

        def pout(half):
            def run():
                ps, pres = g.banks[7]
                for fb in range(16):
                    k.mmg(ps, ynT[:, fb, :], wout[:, fb, half * 512:(half + 1) * 512], fb == 0, fb == 15,
                          [ynT_r, wout_r], [pres])
                k.cp("act", yout[:, half * 512:(half + 1) * 512], ps, [pres], [yn_r])
                if half == 1:
                    k.dma("sp", q_.Y1[rows, :], yout, [yn_r], [])
            return run
        return [p0] + [ptr(f4) for f4 in range(4)] + [pout(0), pout(1)]

    ui = 0
    for q_ in seqs:
        nch = q_.L // 128
        for d in range(2):
            for gq in range(8):
                k.cp("pool", Sbf[:, gq * 256:(gq + 1) * 256], S.St[d][:, gq * 256:(gq + 1) * 256],
                     [St_r[d][gq]], [Sbf_r[gq]])
            order = list(range(nch)) if d == 0 else list(range(nch - 1, -1, -1))
            units = [(q_, ci, ch, d, gg) for ci, ch in enumerate(order) for gg in range(8)]
            AHEAD = 2
            nu = len(units)
            PRO = 5
            LD = 10
            pending = []
            for step in range(-LD, nu + AHEAD):
                lstep = step + LD
                if 0 <= lstep < nu and units[lstep][4] == 0:
                    u = units[lstep]
                    loads(u[0], u[1], u[2], u[3])
                pstep = step + PRO
                if 0 <= pstep < nu and units[pstep][4] == 0:
                    u = units[pstep]
                    prologue(u[0], u[1], u[2], u[3])
                if 0 <= step < nu:
                    seg(*units[step], ui + step)
                if step >= AHEAD:
                    vi = step - AHEAD
                    v = units[vi]
                    rest(*v, ui + vi)
                    if pending:
                        pending.pop(0)()
                    if v[4] == 7:
                        if d == 1:
                            while pending:
                                pending.pop(0)()
                            pcs = epilogue(*v, ui + vi)
                            pcs.pop(0)()
                            pending.extend(pcs)
                        if g.bg:
                            g.bg.pop(0)()
            while pending:
                pending.pop(0)()
            ui += nu
            barrier(g)


def load_vec(g, src_flat, n):
    k, ar, c = g.k, g.ar, g.c
    dst = ar.f32(n)
    dres = Res("vec")
    done = 0
    while done < n:
        m = min(128, n - done)
        if not hasattr(g, "vstage"):
            g.vstage = [g.ar.f32(128), g.ar.f32(128)]
            g.vsres = [Res("vs0"), Res("vs1")]
            g.vs_i = 0
        b = g.vs_i % 2
        g.vs_i += 1
        st, sres = g.vstage[b], g.vsres[b]
        k.dma("sp", st[0:m, :], src_flat[done * 128:(done + m) * 128].rearrange("(r c) -> r c", c=128),
              [], [sres])
        ps, pres = g.bank()
        k.tr(ps[:, 0:m], st[0:m, :], c.ident_f[0:m, 0:m], [sres, c.res], [pres])
        k.cp("dve", dst[:, done:done + m], ps[:, 0:m], [pres], [dres])
        done += m
    return dst, dres


def all_res(rr):
    return [r for row in rr for r in row]


def barrier(g):
    k = g.k
    toks = []
    for e in k.ENG:
        if e in k.cursem and k.cnt[e] > 0:
            toks.append((k.cursem[e], k.cnt[e]))
    for q, slots in k.dma_slots.items():
        n = k.dma_i[q]
        for s_i, sem in enumerate(slots):
            uses = (n - s_i + NSLOT - 1) // NSLOT if n > s_i else 0
            if uses > 0:
                toks.append((sem, 16 * uses))
    for e in k.ENG:
        for t in toks:
            k._wait(e, t)


def load_stream(g, src, dstT, dres, L, ysrc=None, gate=None):
    k, ar, c = g.k, g.ar, g.c
    m = ar.mark()
    xin = [ar.f32(D), ar.f32(D)]
    xres = [Res("xin0"), Res("xin1")]
    yin = [ar.f32(D), ar.f32(D)] if ysrc is not None else None
    yres = [Res("yin0"), Res("yin1")]
    tw = min(L, 512)
    for ch in range(L // 128):
        b = ch % 2
        k.dma("sp", xin[b], src[ch * 128:(ch + 1) * 128, :], [], [xres[b]])
        tl = (ch * 128) // tw
        for q in range(2):
            ps, pres = g.bank()
            for j in range(4):
                blk = q * 4 + j
                k.tr(ps[:, j * 128:(j + 1) * 128], xin[b][:, blk * 128:(blk + 1) * 128], c.ident_f,
                     [xres[b], c.res], [pres], inc=(j == 3))
            k.cp("dve" if q == 0 else "act", dstT[:, q * 4:(q + 1) * 4, ch * 128:(ch + 1) * 128],
                 r3(ps, 4), [pres], [dres[q * 4 + j][tl] for j in range(4)])
        if ysrc is not None:
            modT, g0, s_, mres = gate
            k.dma("sp", yin[b], ysrc[ch * 128:(ch + 1) * 128, :], [], [yres[b]])
            for q in range(2):
                ps, pres = g.bank()
                for j in range(4):
                    blk = q * 4 + j
                    k.tr(ps[:, j * 128:(j + 1) * 128], yin[b][:, blk * 128:(blk + 1) * 128], c.ident_f,
                         [yres[b], c.res], [pres], inc=(j == 3))
                for j in range(4):
                    blk = q * 4 + j
                    dsl = dstT[:, blk, ch * 128:(ch + 1) * 128]
                    k.stt("dve", dsl, ps[:, j * 128:(j + 1) * 128], modT[:, g0 + blk, s_:s_ + 1], dsl,
                          ALU.mult, ALU.add, [pres, mres, dres[blk][tl]], [dres[blk][tl]])
    barrier(g)
    ar.release(m)


FN_STOP = 99


def final_norm(g):
    k, ar, c = g.k, g.ar, g.c
    m = ar.mark()
    fw, fres = load_vec(g, g.I["final_norm_w"], NB)
    if FN_STOP == 0:
        return
    sq = [ar.bf(512), ar.bf(512)]
    sqres = [Res("sq0"), Res("sq1")]
    rstd = ar.f32(512)
    rres = Res("rstd")
    yt = [ar.f32(512), ar.f32(512)]
    ytres = [Res("yt0"), Res("yt1")]
    ost = r3(ar.f32(4 * D), 4)
    ores = Res("ost")
    for tl in range(T // 512):
        sl = slice(tl * 512, (tl + 1) * 512)
        ps, pres = g.bank()
        for blk in range(NB):
            b = blk % 2
            k.act(sq[b], g.hT[:, blk, sl], AF.Square, [g.h_res[blk][tl]], [sqres[b]])
            k.mm(ps, c.ones_b, sq[b], blk == 0, blk == NB - 1, [sqres[b], c.res], [pres])
        if FN_STOP == 1:
            continue
        k.ts("dve", rstd, ps, 1.0 / D, EPS, ALU.mult, ALU.add, [pres], [rres])
        k.act(rstd, rstd, AF.Sqrt, [rres], [rres])
        if FN_STOP == 2:
            continue
        k.op("dve", lambda e: e.reciprocal(rstd, rstd), [rres], [rres])
        if FN_STOP == 3:
            continue
        for blk in range(NB):
            b = blk % 2
            k.stt("dve", yt[b], g.hT[:, blk, sl], fw[:, blk:blk + 1], rstd, ALU.mult, ALU.mult,
                  [g.h_res[blk][tl], fres, rres], [ytres[b]])
            if FN_STOP == 4:
                continue
            ps2, pres2 = g.bank()
            for j in range(4):
                k.tr(ps2[:, j * 128:(j + 1) * 128], yt[b][:, j * 128:(j + 1) * 128], c.ident_f,
                     [ytres[b], c.res], [pres2], inc=(j == 3))
            k.cp("act", ost[:, :, blk * 128:(blk + 1) * 128], r3(ps2, 4), [pres2], [ores])
        tok = k.dma("sp", g.out[sl, :].rearrange("(c p) d -> p c d", p=128), ost, [ores], [])
        k.out_tokens.append(tok)
    ar.release(m)


_CACHE = {}


def _prep_inputs(inp, b):
    f = lambda a: np.ascontiguousarray(np.asarray(a, dtype=np.float32))
    m = {}
    m["x"] = f(inp["x"][b])
    m["ctx"] = f(inp["ctx"][b])
    m["cvec"] = f(np.stack([np.asarray(inp["c"])[b], np.asarray(inp["c_ctx"])], 0))
    for nm in ("mod_w", "mod_b", "norm1_w", "norm2_w", "ffn_w_up", "ffn_conv_b", "ffn_w_down",
               "final_norm_w"):
        m[nm] = f(inp[nm])
    m["ffn_conv_w"] = f(np.asarray(inp["ffn_conv_w"]).reshape(2, 9, FH))
    for nm in ("ssd_w_in", "ssd_conv_w", "ssd_conv_b", "ssd_d", "ssd_norm_w", "ssd_w_out",
               "conf_w_pw1", "conf_b_pw1", "conf_w_dw", "conf_b_dw", "conf_ln_w", "conf_ln_b",
               "conf_w_pw2", "conf_b_pw2"):
        m[nm] = f(np.asarray(inp[nm])[0])
    m["ssd_dt_bias"] = f(np.asarray(inp["ssd_dt_bias"])[0].reshape(64))
    m["ssd_a_log"] = f(np.asarray(inp["ssd_a_log"])[0].reshape(64))
    return m


def kernel(**inputs):
    if "nc" not in _CACHE:
        _CACHE["nc"] = build()
    nc = _CACHE["nc"]
    in_maps = [_prep_inputs(inputs, b) for b in range(8)]
    res = run_bass_kernel_spmd(nc, in_maps, core_ids=list(range(8)))
    return np.stack([np.asarray(r["out"], dtype=np.float32) for r in res.results], 0)
```

```python
import numpy as np
import concourse.bass as bass
import concourse.mybir as mybir
from concourse.bass_utils import run_bass_kernel_spmd

F32 = mybir.dt.float32
BF16 = mybir.dt.bfloat16
AF = mybir.ActivationFunctionType
ALU = mybir.AluOpType
AX = mybir.AxisListType

D = 1024
T = 2048
TC = 256
NB = D // 128
DI = 2048
NH = 32
NG = 8
NS = 128
CONVD = 4096
INDIM = 6208
FH = 2816
NFB = FH // 128
EPS = 1e-6
EPOCH = 30000
NSLOT = 8


class Res:
    __slots__ = ("name", "lw", "rd")

    def __init__(self, name):
        self.name = name
        self.lw = None
        self.rd = {}


class KB:
    ENG = ("sp", "pe", "dve", "act", "pool")

    def __init__(self):
        self.nc = bass.Bass("TRN2", target_bir_lowering=False)
        self.streams = {e: [] for e in self.ENG}
        self.cnt = {e: 0 for e in self.ENG}
        self.cursem = {}
        self.known = {e: {} for e in self.ENG}
        self.semkey = {}
        self.nsem = 0
        self.dma_i = {e: 0 for e in self.ENG}
        self.dma_slots = {}
        self.uid = 0
        self.out_tokens = []

    def newsem(self, name):
        s = self.nc.alloc_semaphore(f"{name}_{self.nsem}")
        self.nsem += 1
        self.semkey[id(s)] = s
        return s

    def sb(self, name, shape, dt):
        self.uid += 1
        return self.nc.alloc_sbuf_tensor(f"{name}_{self.uid}", list(shape), dt)

    def dram(self, name, shape, dt, kind="Internal"):
        return self.nc.dram_tensor(name, list(shape), dt, kind=kind)

    def _engsem(self, e):
        if e not in self.cursem:
            self.cursem[e] = self.newsem("e" + e)
        return self.cursem[e]

    def _wait(self, e, tok):
        sem, val = tok
        k = self.known[e]
        if k.get(id(sem), 0) >= val:
            return
        k[id(sem)] = val
        self.streams[e].append(("w", sem, val))

    def _deps(self, e, reads, writes):
        own = id(self._engsem(e))
        for r in reads:
            if r.lw is not None:
                if e == "pe" and id(r.lw[0]) == own:
                    continue
                self._wait(e, r.lw)
        for w in writes:
            if w.lw is not None and id(w.lw[0]) != own:
                self._wait(e, w.lw)
            for t in w.rd.values():
                if id(t[0]) != own:
                    self._wait(e, t)

    def _mark(self, tok, reads, writes):
        for r in reads:
            k = id(tok[0])
            if k not in r.rd or r.rd[k][1] < tok[1]:
                r.rd[k] = tok
        for w in writes:
            w.lw = tok
            w.rd = {}

    def op(self, e, fn, reads=(), writes=(), inc=True):
        self._deps(e, reads, writes)
        sem = self._engsem(e)
        tok = (sem, self.cnt[e] + 1)
        self.streams[e].append(("o", fn, sem if inc else None, 1))
        if inc:
            self.cnt[e] += 1
            if self.cnt[e] >= EPOCH:
                del self.cursem[e]
                self.cnt[e] = 0
        self._mark(tok, reads, writes)
        return tok

    def dma(self, q, out, in_, reads=(), writes=(), **kw):
        self._deps(q, reads, writes)
        if q not in self.dma_slots:
            self.dma_slots[q] = [self.newsem("d" + q) for _ in range(NSLOT)]
        i = self.dma_i[q]
        self.dma_i[q] += 1
        sem = self.dma_slots[q][i % NSLOT]
        prev = 16 * (i // NSLOT)
        if prev > 0:
            self._wait(q, (sem, prev))
        tok = (sem, prev + 16)
        self.streams[q].append(("o", lambda eng: eng.dma_start(out=out, in_=in_, **kw), sem, 16))
        self._mark(tok, reads, writes)
        return tok

    def finish(self):
        for tok in self.out_tokens:
            self._wait("sp", tok)
        nc = self.nc
        streams = self.streams
        with nc.Block() as block:
            def mk(stream):
                def body(eng):
                    for it in stream:
                        if it[0] == "w":
                            eng.wait_ge(it[1], it[2])
                        else:
                            ins = it[1](eng)
                            if it[2] is not None:
                                ins.then_inc(it[2], it[3])
                return body
            block.sync(mk(streams["sp"]))
            block.tensor(mk(streams["pe"]))
            block.vector(mk(streams["dve"]))
            block.scalar(mk(streams["act"]))
            block.gpsimd(mk(streams["pool"]))
        return nc

    def mm(self, out, lhsT, rhs, start, stop, reads, writes, inc=None):
        if inc is None:
            inc = True
        return self.op("pe", lambda e: e.matmul(out, lhsT, rhs, start=start, stop=stop),
                       reads, writes, inc=inc)

    def mmg(self, out, lhsT, rhs, start, stop, reads, writes):
        return self.mm(out, lhsT, rhs, start, stop, reads, writes, inc=bool(stop))

    def tr(self, out, in_, ident, reads, writes, inc=True):
        return self.op("pe", lambda e: e.transpose(out, in_, ident), reads, writes, inc=inc)

    def act(self, out, in_, func, reads, writes, bias=0.0, scale=1.0, eng="act", accum_out=None):
        if accum_out is None:
            return self.op("act", lambda e: e.activation(out, in_, func, bias=bias, scale=scale),
                           reads, writes)
        return self.op("act", lambda e: e.activation(out, in_, func, bias=bias, scale=scale,
                                                     accum_out=accum_out), reads, writes)

    def tt(self, eng, out, in0, in1, op, reads, writes):
        return self.op(eng, lambda e: e.tensor_tensor(out, in0, in1, op), reads, writes)

    def ts(self, eng, out, in0, s1, s2, op0, op1, reads, writes):
        if s2 is None:
            return self.op(eng, lambda e: e.tensor_scalar(out, in0, s1, None, op0), reads, writes)
        return self.op(eng, lambda e: e.tensor_scalar(out, in0, s1, s2, op0, op1), reads, writes)

    def stt(self, eng, out, in0, scalar, in1, op0, op1, reads, writes):
        return self.op(eng, lambda e: e.scalar_tensor_tensor(out, in0, scalar, in1, op0, op1),
                       reads, writes)

    def cp(self, eng, out, in_, reads, writes):
        if eng == "act":
            return self.op(eng, lambda e: e.copy(out, in_), reads, writes)
        return self.op(eng, lambda e: e.tensor_copy(out, in_), reads, writes)

    def memset(self, eng, ap, val, writes):
        return self.op(eng, lambda e: e.memset(ap, val), (), writes)


class Arena:
    def __init__(self, kb, words):
        self.t = kb.nc.alloc_sbuf_tensor("arena", [128, words], F32)
        self.words = words
        self.top = 0

    def mark(self):
        return self.top

    def release(self, m):
        self.top = m

    def _alloc(self, words):
        words = (words + 7) // 8 * 8
        off = self.top
        self.top += words
        assert self.top <= self.words, f"arena overflow {self.top} > {self.words}"
        return off

    def f32(self, n):
        off = self._alloc(n)
        return self.t[:, off:off + n]

    def bf(self, n):
        w = (n + 1) // 2
        off = self._alloc(w)
        return self.t[:, off:off + w].bitcast(BF16)[:, 0:n]


class Ctx:
    pass


def r3(ap, a):
    return ap.rearrange("p (a b) -> p a b", a=a)


def build(stage=99):
    k = KB()
    nc = k.nc
    g = Ctx()
    g.k = k
    def din(name, shape):
        return nc.dram_tensor(name, list(shape), F32, kind="ExternalInput").ap()
    I = {}
    I["x"] = din("x", [T, D])
    I["ctx"] = din("ctx", [TC, D])
    I["cvec"] = din("cvec", [2, D])
    I["mod_w"] = din("mod_w", [2, D, 6 * D])
    I["mod_b"] = din("mod_b", [2, 6 * D])
    I["norm1_w"] = din("norm1_w", [2, D])
    I["norm2_w"] = din("norm2_w", [2, D])
    I["ssd_w_in"] = din("ssd_w_in", [D, INDIM])
    I["ssd_conv_w"] = din("ssd_conv_w", [5, CONVD])
    I["ssd_conv_b"] = din("ssd_conv_b", [CONVD])
    I["ssd_dt_bias"] = din("ssd_dt_bias", [64])
    I["ssd_a_log"] = din("ssd_a_log", [64])
    I["ssd_d"] = din("ssd_d", [NH])
    I["ssd_norm_w"] = din("ssd_norm_w", [DI])
    I["ssd_w_out"] = din("ssd_w_out", [DI, D])
    I["conf_w_pw1"] = din("conf_w_pw1", [D, 2 * D])
    I["conf_b_pw1"] = din("conf_b_pw1", [2 * D])
    I["conf_w_dw"] = din("conf_w_dw", [31, D])
    I["conf_b_dw"] = din("conf_b_dw", [D])
    I["conf_ln_w"] = din("conf_ln_w", [D])
    I["conf_ln_b"] = din("conf_ln_b", [D])
    I["conf_w_pw2"] = din("conf_w_pw2", [D, D])
    I["conf_b_pw2"] = din("conf_b_pw2", [D])
    I["ffn_w_up"] = din("ffn_w_up", [2, D, 2 * FH])
    I["ffn_conv_w"] = din("ffn_conv_w", [2, 9, FH])
    I["ffn_conv_b"] = din("ffn_conv_b", [2, FH])
    I["ffn_w_down"] = din("ffn_w_down", [2, FH, D])
    I["final_norm_w"] = din("final_norm_w", [D])
    out = nc.dram_tensor("out", [T, D], F32, kind="ExternalOutput").ap()
    g.I = I
    g.out = out

    ar = Arena(k, 53100)
    g.ar = ar
    g.psum = nc.alloc_psum_tensor("psall", [128, 4096], F32)
    g.banks = []
    for i in range(8):
        g.banks.append((g.psum[:, i * 512:(i + 1) * 512], Res(f"psb{i}")))
    g.bank_i = 0

    def bank():
        b = g.banks[g.bank_i % 6]
        g.bank_i += 1
        return b[0], b[1]
    g.bank = bank

    c = Ctx()
    g.c = c
    c.res = Res("consts")
    c.ident_f = ar.f32(128)
    c.ones_f = ar.f32(128)
    c.ident_b = ar.bf(128)
    c.ones_b = ar.bf(128)
    k.memset("pool", c.ident_f, 0.0, [c.res])
    k.op("pool", lambda e: e.affine_select(out=c.ident_f, in_=c.ident_f, pattern=[[-1, 128]],
                                           compare_op=ALU.not_equal, fill=1.0, base=0,
                                           channel_multiplier=1), [c.res], [c.res])
    k.memset("pool", c.ones_f, 1.0, [c.res])
    k.cp("pool", c.ident_b, c.ident_f, [c.res], [c.res])
    k.cp("pool", c.ones_b, c.ones_f, [c.res], [c.res])
    g.vstage = [ar.f32(128), ar.f32(128)]
    g.vsres = [Res("vs0"), Res("vs1")]
    g.vs_i = 0
    g.wst = ar.f32(4096)
    g.wstres = Res("wst")
    g.wsth_res = [Res("wsth0"), Res("wsth1")]
    g.wbf = [ar.bf(4096), ar.bf(4096)]
    g.wbfres = [Res("wbf0"), Res("wbf1")]
    g.w_i = 0

    V = Ctx()
    g.V = V
    V.fnw, V.fnw_r = load_vec(g, I["final_norm_w"], NB)
    V.n1w, V.n1w_r = load_vec(g, I["norm1_w"].rearrange("a b -> (a b)"), 2 * NB)
    V.n2w, V.n2w_r = load_vec(g, I["norm2_w"].rearrange("a b -> (a b)"), 2 * NB)
    V.modb, V.modb_r = load_vec(g, I["mod_b"].rearrange("a b -> (a b)"), 96)
    V.cv, V.cv_r = load_vec(g, I["cvec"].rearrange("a b -> (a b)"), 16)
    V.fcb, V.fcb_r = load_vec(g, I["ffn_conv_b"].rearrange("a b -> (a b)"), 2 * NFB)
    V.fcw, V.fcw_r = load_vec(g, I["ffn_conv_w"].rearrange("a b c -> (a b c)"), 2 * 9 * NFB)
    V.scb, V.scb_r = load_vec(g, I["ssd_conv_b"], 32)
    V.scw, V.scw_r = load_vec(g, I["ssd_conv_w"].rearrange("a b -> (a b)"), 5 * 32)
    V.cb1, V.cb1_r = load_vec(g, I["conf_b_pw1"], 16)
    V.cbdw, V.cbdw_r = load_vec(g, I["conf_b_dw"], 8)
    V.clnw, V.clnw_r = load_vec(g, I["conf_ln_w"], 8)
    V.clnb, V.clnb_r = load_vec(g, I["conf_ln_b"], 8)
    V.cb2, V.cb2_r = load_vec(g, I["conf_b_pw2"], 8)
    V.cdw, V.cdw_r = load_vec(g, I["conf_w_dw"].rearrange("a b -> (a b)"), 31 * 8)
    V.cs = r3(ar.bf(16), 8)
    V.cs_r = Res("cs")
    tmpc = ar.f32(16)
    tmpc_r = Res("tmpc")
    k.act(tmpc, V.cv, AF.Silu, [V.cv_r], [tmpc_r])
    V.cs32 = r3(ar.f32(16), 8)
    for s_ in range(2):
        k.cp("dve", V.cs[:, :, s_], tmpc[:, s_ * 8:(s_ + 1) * 8], [tmpc_r], [V.cs_r])
        k.cp("dve", V.cs32[:, :, s_], tmpc[:, s_ * 8:(s_ + 1) * 8], [tmpc_r], [V.cs_r])
    V.maskL = ar.f32(512)
    V.maskR = ar.f32(512)
    V.mask_r = Res("masks")
    k.memset("pool", V.maskL, 1.0, [V.mask_r])
    k.memset("pool", V.maskR, 1.0, [V.mask_r])
    k.memset("pool", r3(V.maskL, 8)[:, :, 63:64], 0.0, [V.mask_r])
    k.memset("pool", r3(V.maskR, 8)[:, :, 0:1], 0.0, [V.mask_r])
    g.modT = [r3(ar.f32(96), 48), r3(ar.f32(96), 48)]
    g.A1 = [r3(ar.f32(16), 8), r3(ar.f32(16), 8)]
    g.A2 = [r3(ar.f32(16), 8), r3(ar.f32(16), 8)]
    g.mod_r = [Res("mod0"), Res("mod1")]
    g.modrow = ar.f32(256)
    g.modrow_r = Res("modrow")

    g.h_off = ar.top
    g.hT = r3(ar.f32(NB * T), NB)
    g.hcT = r3(ar.f32(NB * TC), NB)
    g.h_res = [[Res(f"h{b}_{t}") for t in range(T // 512)] for b in range(NB)]
    g.hc_res = [[Res(f"hc{b}")] for b in range(NB)]
    g.aT = r3(ar.bf(NB * T), NB)
    g.acT = r3(ar.bf(NB * TC), NB)
    g.a_res = [[Res(f"a{b}_{t}") for t in range(T // 512)] for b in range(NB)]
    g.ac_res = [[Res(f"ac{b}")] for b in range(NB)]
    g.pmark = ar.mark()

    for it in mod_params_items(g, 0):
        it()
    g.bg = mod_params_items(g, 1)
    if stage < 2 or not BG_INTERLEAVE:
        while g.bg:
            g.bg.pop(0)()
    barrier(g)
    load_stream(g, I["x"], g.hT, g.h_res, T)
    load_stream(g, I["ctx"], g.hcT, g.hc_res, TC)
    if stage >= 1:
        modulate(g, g.hT, g.h_res, T, g.A1[0], g.modT[0], 0, 0, g.aT, g.a_res)
        modulate(g, g.hcT, g.hc_res, TC, g.A1[0], g.modT[0], 0, 1, g.acT, g.ac_res)
        barrier(g)
        if stage >= 2:
            ssd_layer(g)
        while g.bg:
            g.bg.pop(0)()
        barrier(g)
    if stage >= 3:
        modulate(g, g.hT, g.h_res, T, g.A2[0], g.modT[0], 24, 0, g.aT, g.a_res)
        modulate(g, g.hcT, g.hc_res, TC, g.A2[0], g.modT[0], 24, 1, g.acT, g.ac_res)
        ffn(g, 0, g.aT, g.a_res, T, True, g.hT, g.h_res, 0)
        ffn(g, 0, g.acT, g.ac_res, TC, False, g.hcT, g.hc_res, 1)
        barrier(g)
    if stage >= 4:
        modulate(g, g.hT, g.h_res, T, g.A1[1], g.modT[1], 0, 0, g.aT, g.a_res)
        conformer(g)
        barrier(g)
    if stage >= 5:
        modulate(g, g.hT, g.h_res, T, g.A2[1], g.modT[1], 24, 0, g.aT, g.a_res)
        ffn(g, 1, g.aT, g.a_res, T, True, g.hT, g.h_res, 0)
        barrier(g)
    if stage == 99:
        final_norm(g)
    else:
        dbg = nc.dram_tensor("dbg", [128, NB * T], F32, kind="ExternalOutput").ap()
        dbgc = nc.dram_tensor("dbgc", [128, NB * TC], F32, kind="ExternalOutput").ap()
        k.out_tokens.append(k.dma("sp", dbg, g.hT.rearrange("p a b -> p (a b)"), all_res(g.h_res), []))
        k.out_tokens.append(k.dma("sp", dbgc, g.hcT.rearrange("p a b -> p (a b)"), all_res(g.hc_res), []))
    return k.finish()


def wload(g, src2d, kblks, col0, ncols, row0=0):
    k = g.k
    n = kblks * ncols
    assert n <= 4096
    st = r3(g.wst[:, 0:n], kblks)
    b = g.w_i % 2
    g.w_i += 1
    wb = r3(g.wbf[b][:, 0:n], kblks)
    src = src2d[row0:row0 + kblks * 128, col0:col0 + ncols].rearrange("(kb p) n -> p kb n", p=128)
    rs = [g.wstres, g.wsth_res[0], g.wsth_res[1]]
    k.dma(WQ, st, src, [], rs)
    k.cp("pool" if b == 0 else "act", wb, st, rs, [g.wbfres[b]])
    return wb, g.wbfres[b]


class WStream:
    def __init__(self, g, specs, bufs, bres, ahead):
        self.g, self.specs, self.bufs, self.bres, self.ahead = g, specs, bufs, bres, ahead
        self.loaded = {}
        self.nxt = 0

    def _load(self, i):
        g = self.g
        k = g.k
        src2d, kblks, col0, ncols, row0 = self.specs[i]
        n = kblks * ncols
        b = i % len(self.bufs)
        if n <= 2048:
            h = g.w_i % 2
            stf, stres = g.wst[:, h * 2048:h * 2048 + n], g.wsth_res[h]
        else:
            stf, stres = g.wst[:, 0:n], g.wstres
        g.w_i += 1
        st = r3(stf, kblks)
        wb = r3(self.bufs[b][:, 0:n], kblks)
        src = src2d[row0:row0 + kblks * 128, col0:col0 + ncols].rearrange("(kb p) n -> p kb n", p=128)
        rs = [stres] if n <= 2048 else [g.wstres, g.wsth_res[0], g.wsth_res[1]]
        k.dma("sp", st, src, [], rs)
        k.cp("pool", wb, st, rs, [self.bres[b]])
        self.loaded[i] = (wb, self.bres[b])

    def get(self, i):
        while self.nxt <= min(i + self.ahead, len(self.specs) - 1):
            self._load(self.nxt)
            self.nxt += 1
        return self.loaded.pop(i)


def mod_params_items(g, i):
    k, ar, V = g.k, g.ar, g.V
    mr = g.mod_r[i]
    items = []

    loaded = {}

    def loader(cg):
        def run():
            h = cg % 2
            st = r3(g.wst[:, h * 2048:(h + 1) * 2048], 8)
            src = g.I["mod_w"][i][:, cg * 256:(cg + 1) * 256].rearrange("(kb p) n -> p kb n", p=128)
            k.dma("sp", st, src, [], [g.wsth_res[h]])
            loaded[cg] = (st, g.wsth_res[h])
        return run

    def compute(cg):
        def run():
            psb, pres = g.banks[7]
            w, wres = loaded.pop(cg)
            for kb in range(8):
                k.mmg(psb[0:2, 0:256], V.cs32[:, kb, :], w[:, kb, :], kb == 0, kb == 7, [wres, V.cs_r], [pres])
            k.cp("dve", g.modrow[0:2, :], psb[0:2, 0:256], [pres], [g.modrow_r])
            for j in range(2):
                k.tr(psb[:, 256 + j * 2:256 + (j + 1) * 2], g.modrow[0:2, j * 128:(j + 1) * 128],
                     g.c.ident_f[0:2, 0:2], [g.modrow_r, g.c.res], [pres])
            m0 = cg * 2
            k.tt("dve", g.modT[i][:, m0:m0 + 2, :], r3(psb[:, 256:260], 2),
                 V.modb[:, i * 48 + m0:i * 48 + m0 + 2].unsqueeze(2).broadcast_to([128, 2, 2]), ALU.add,
                 [pres, V.modb_r], [mr])
        return run

    def both(cg):
        def run():
            if cg not in loaded:
                loader(cg)()
            if cg + 1 < 24:
                loader(cg + 1)()
            compute(cg)()
        return run
    for cg in range(24):
        items.append(both(cg))

    def fin():
        k.stt("dve", g.A1[i], g.modT[i][:, 8:16, :], 1.0,
              V.n1w[:, i * 8:(i + 1) * 8].unsqueeze(2).broadcast_to([128, 8, 2]), ALU.add, ALU.mult,
              [mr, V.n1w_r], [mr])
        k.stt("dve", g.A2[i], g.modT[i][:, 32:40, :], 1.0,
              V.n2w[:, i * 8:(i + 1) * 8].unsqueeze(2).broadcast_to([128, 8, 2]), ALU.add, ALU.mult,
              [mr, V.n2w_r], [mr])
    items.append(fin)
    return items


def modulate(g, srcT, sres, L, A, modT, sh0, s_, dstT, dres, mres=None):
    k, ar, c = g.k, g.ar, g.c
    mres = g.mod_r[0] if modT is g.modT[0] else g.mod_r[1]
    m = ar.mark()
    tw = min(512, L)
    sq = [ar.bf(tw), ar.bf(tw)]
    sqres = [Res("sq0"), Res("sq1")]
    rstd = ar.f32(tw)
    rres = Res("rstd")
    tmp = [ar.f32(tw), ar.f32(tw)]
    tres = [Res("t0"), Res("t1")]
    for tl in range(L // tw):
        sl = slice(tl * tw, (tl + 1) * tw)
        ps, pres = g.bank()
        ps = ps[:, 0:tw]
        for blk in range(NB):
            b = blk % 2
            k.act(sq[b], srcT[:, blk, sl], AF.Square, [sres[blk][tl]], [sqres[b]])
            k.mm(ps, c.ones_b, sq[b], blk == 0, blk == NB - 1, [sqres[b], c.res], [pres])
        k.ts("dve", rstd, ps, 1.0 / D, EPS, ALU.mult, ALU.add, [pres], [rres])
        k.act(rstd, rstd, AF.Sqrt, [rres], [rres])
        k.op("dve", lambda e: e.reciprocal(rstd, rstd), [rres], [rres])
        for blk in range(NB):
            b = blk % 2
            k.tt("dve", tmp[b], srcT[:, blk, sl], rstd, ALU.mult, [sres[blk][tl], rres], [tres[b]])
            k.act(dstT[:, blk, sl], tmp[b], AF.Identity, [tres[b], mres], [dres[blk][tl]],
                  bias=modT[:, sh0 + blk, s_:s_ + 1], scale=A[:, blk, s_:s_ + 1])
    barrier(g)
    ar.release(m)


def ffn(g, layer, aT, a_res, L, grid, hT, h_res, s_):
    k, ar, c, V = g.k, g.ar, g.c, g.V
    m = ar.mark()
    tw = min(512, L)
    nt = L // tw
    HL = 66
    W = HL + L + HL
    gpre = ar.bf(W)
    gL = ar.bf(W) if grid else None
    gR = ar.bf(W) if grid else None
    gres = Res("gpre")
    k.memset("pool", gpre, 0.0, [gres])
    if grid:
        k.memset("pool", gL, 0.0, [gres])
        k.memset("pool", gR, 0.0, [gres])
    hid = [ar.bf(L), ar.bf(L)]
    hres = [Res("hid0"), Res("hid1")]
    sg = [ar.f32(tw), ar.f32(tw)]
    sgres = [Res("sg0"), Res("sg1")]
    diag = [ar.bf(128) for _ in range(9)]
    dres = Res("diag")
    wup = g.I["ffn_w_up"][layer]
    wdn = g.I["ffn_w_down"][layer]
    specs = []
    for fp_ in range(NFB // 2):
        specs.append((wup, 8, fp_ * 256, 256, 0))
        specs.append((wup, 8, FH + fp_ * 256, 256, 0))
        specs.append((wdn, 2, 0, D, fp_ * 256))
    wsm = WStream(g, specs, [ar.bf(2048) for _ in range(6)], [Res(f"fw{i_}") for i_ in range(6)], 3)
    g2 = g.modT[layer]
    mres = g.mod_r[layer]
    taps = [(ky, kx) for ky in range(3) for kx in range(3)] if grid else [(1, kx) for kx in range(3)]
    for fp in range(NFB // 2):
        wv, wvres = wsm.get(fp * 3)
        wg, wgres = wsm.get(fp * 3 + 1)
        for fi in range(2):
            f = fp * 2 + fi
            for tl in range(nt):
                ps, pres = g.bank()
                ps = ps[:, 0:tw]
                for kb in range(NB):
                    k.mmg(ps, wg[:, kb, fi * 128:(fi + 1) * 128], aT[:, kb, tl * tw:(tl + 1) * tw],
                         kb == 0, kb == NB - 1, [wgres, a_res[kb][tl]], [pres])
                dsl = slice(HL + tl * tw, HL + (tl + 1) * tw)
                k.cp("act", gpre[:, dsl], ps, [pres], [gres])
                if grid:
                    k.tt("dve", gL[:, dsl], ps, V.maskL, ALU.mult, [pres, V.mask_r], [gres])
                    k.tt("dve", gR[:, dsl], ps, V.maskR, ALU.mult, [pres, V.mask_r], [gres])
            for ti, (ky, kx) in enumerate(taps):
                col = layer * 9 * NFB + (ky * 3 + kx) * NFB + f
                k.ts("dve", diag[ti], c.ident_b, V.fcw[:, col:col + 1], None, ALU.mult, None,
                     [c.res, V.fcw_r], [dres])
            for tl in range(nt):
                ps, pres = g.bank()
                ps = ps[:, 0:tw]
                for ti, (ky, kx) in enumerate(taps):
                    srcb = gpre if (not grid or kx == 1) else (gL if kx == 0 else gR)
                    off = HL + tl * tw + ((ky - 1) * 64 if grid else 0) + (kx - 1)
                    k.mmg(ps, diag[ti], srcb[:, off:off + tw], ti == 0, ti == len(taps) - 1,
                         [dres, gres], [pres])
                b = tl % 2
                cbc = layer * NFB + f
                k.act(sg[b], ps, AF.Silu, [pres, V.fcb_r], [sgres[b]], bias=V.fcb[:, cbc:cbc + 1])
                ps2, pres2 = g.bank()
                ps2 = ps2[:, 0:tw]
                for kb in range(NB):
                    k.mmg(ps2, wv[:, kb, fi * 128:(fi + 1) * 128], aT[:, kb, tl * tw:(tl + 1) * tw],
                         kb == 0, kb == NB - 1, [wvres, a_res[kb][tl]], [pres2])
                k.tt("dve", hid[fi][:, tl * tw:(tl + 1) * tw], ps2, sg[b], ALU.mult,
                     [pres2, sgres[b]], [hres[fi]])
        wd, wdres = wsm.get(fp * 3 + 2)
        for db in range(NB):
            for tl in range(nt):
                ps, pres = g.bank()
                ps = ps[:, 0:tw]
                for fi in range(2):
                    k.mmg(ps, wd[:, fi, db * 128:(db + 1) * 128], hid[fi][:, tl * tw:(tl + 1) * tw],
                         fi == 0, fi == 1, [wdres, hres[fi]], [pres])
                hsl = hT[:, db, tl * tw:(tl + 1) * tw]
                k.stt("dve", hsl, ps, g2[:, 40 + db, s_:s_ + 1], hsl, ALU.mult, ALU.add,
                      [pres, mres, h_res[db][tl]], [h_res[db][tl]])
    barrier(g)
    ar.release(m)


def conformer(g):
    k, ar, c, V = g.k, g.ar, g.c, g.V
    m = ar.mark()
    HL = 16
    W = HL + T + HL
    glu = [ar.bf(W) for _ in range(NB)]
    glu_r = [Res(f"glu{i}") for i in range(NB)]
    sgm = [ar.f32(512), ar.f32(512)]
    sgm_r = [Res("sgm0"), Res("sgm1")]
    w1 = g.I["conf_w_pw1"]
    nt = T // 512
    specs = []
    for q4 in range(2):
        specs.append((w1, 8, q4 * 512, 512, 0))
        specs.append((w1, 8, D + q4 * 512, 512, 0))
    ws1 = WStream(g, specs, g.wbf, g.wbfres, 0)
    w1cur = {}
    for cb in range(NB):
        k.memset("pool", glu[cb], 0.0, [glu_r[cb]])
        if cb % 4 == 0:
            w1cur["a"] = ws1.get((cb // 4) * 2)
            w1cur["g"] = ws1.get((cb // 4) * 2 + 1)
        wa, wares = w1cur["a"][0][:, :, (cb % 4) * 128:(cb % 4 + 1) * 128], w1cur["a"][1]
        wgt, wgres = w1cur["g"][0][:, :, (cb % 4) * 128:(cb % 4 + 1) * 128], w1cur["g"][1]
        for tl in range(nt):
            sl = slice(tl * 512, (tl + 1) * 512)
            psg, presg = g.bank()
            for kb in range(NB):
                k.mmg(psg, wgt[:, kb, :], g.aT[:, kb, sl], kb == 0, kb == NB - 1,
                     [wgres, g.a_res[kb][tl]], [presg])
            b = tl % 2
            k.act(sgm[b], psg, AF.Sigmoid, [presg, V.cb1_r], [sgm_r[b]], bias=V.cb1[:, 8 + cb:9 + cb])
            psa, presa = g.bank()
            for kb in range(NB):
                k.mmg(psa, wa[:, kb, :], g.aT[:, kb, sl], kb == 0, kb == NB - 1,
                     [wares, g.a_res[kb][tl]], [presa])
            k.stt("dve", glu[cb][:, HL + tl * 512:HL + (tl + 1) * 512], psa, V.cb1[:, cb:cb + 1], sgm[b],
                  ALU.add, ALU.mult, [presa, V.cb1_r, sgm_r[b]], [glu_r[cb]])
    barrier(g)
    cv = g.aT
    cv_r = g.a_res
    diag = [ar.bf(128) for _ in range(31)]
    dres = Res("cdiag")
    for cb in range(NB):
        for tp in range(31):
            col = tp * 8 + cb
            k.ts("dve", diag[tp], c.ident_b, V.cdw[:, col:col + 1], None, ALU.mult, None,
                 [c.res, V.cdw_r], [dres])
        for tl in range(nt):
            ps, pres = g.bank()
            for tp in range(31):
                off = HL + tl * 512 + tp - 15
                k.mmg(ps, diag[tp], glu[cb][:, off:off + 512], tp == 0, tp == 30, [dres, glu_r[cb]], [pres])
            k.act(cv[:, cb, tl * 512:(tl + 1) * 512], ps, AF.Identity, [pres, V.cbdw_r], [cv_r[cb][tl]],
                  bias=V.cbdw[:, cb:cb + 1])
    barrier(g)
    sq = [ar.bf(512), ar.bf(512)]
    sq_r = [Res("csq0"), Res("csq1")]
    mean = ar.f32(512)
    rstd = ar.f32(512)
    nmr = ar.f32(512)
    st_r = Res("lnstat")
    t1 = sgm
    t1_r = [Res("lt0"), Res("lt1")]
    hln = glu
    for tl in range(nt):
        sl = slice(tl * 512, (tl + 1) * 512)
        ps1, pres1 = g.bank()
        ps2, pres2 = g.bank()
        for cb in range(NB):
            b = cb % 2
            k.mmg(ps1, c.ones_b, cv[:, cb, sl], cb == 0, cb == NB - 1, [c.res, cv_r[cb][tl]], [pres1])
            k.tt("dve", sq[b], cv[:, cb, sl], cv[:, cb, sl], ALU.mult, [cv_r[cb][tl]], [sq_r[b]])
            k.mm(ps2, c.ones_b, sq[b], cb == 0, cb == NB - 1, [c.res, sq_r[b]], [pres2])
        k.ts("dve", mean, ps1, 1.0 / D, None, ALU.mult, None, [pres1], [st_r])
        k.tt("dve", nmr, mean, mean, ALU.mult, [st_r], [st_r])
        k.stt("dve", rstd, ps2, 1.0 / D, nmr, ALU.mult, ALU.subtract, [pres2, st_r], [st_r])
        k.ts("dve", rstd, rstd, EPS, None, ALU.add, None, [st_r], [st_r])
        k.act(rstd, rstd, AF.Sqrt, [st_r], [st_r])
        k.op("dve", lambda e: e.reciprocal(rstd, rstd), [st_r], [st_r])
        k.stt("dve", nmr, mean, -1.0, rstd, ALU.mult, ALU.mult, [st_r], [st_r])
        for cb in range(NB):
            b = cb % 2
            k.tt("dve", t1[b], cv[:, cb, sl], rstd, ALU.mult, [cv_r[cb][tl], st_r], [t1_r[b]])
            k.tt("dve", t1[b], t1[b], nmr, ALU.add, [t1_r[b], st_r], [t1_r[b]])
            k.act(hln[cb][:, sl], t1[b], AF.Silu, [t1_r[b], V.clnw_r, V.clnb_r], [glu_r[cb]],
                  bias=V.clnb[:, cb:cb + 1], scale=V.clnw[:, cb:cb + 1])
    barrier(g)
    w2 = g.I["conf_w_pw2"]
    ws2 = WStream(g, [(w2, 8, q4 * 512, 512, 0) for q4 in range(2)], g.wbf, g.wbfres, 1)
    w2cur = {}
    g1 = g.modT[1]
    mres = g.mod_r[1]
    yb = sgm
    yb_r = [Res("yb0"), Res("yb1")]
    for db in range(NB):
        if db % 4 == 0:
            w2cur["w"] = ws2.get(db // 4)
        wp, wpres = w2cur["w"][0][:, :, (db % 4) * 128:(db % 4 + 1) * 128], w2cur["w"][1]
        for tl in range(nt):
            sl = slice(tl * 512, (tl + 1) * 512)
            ps, pres = g.bank()
            for cb in range(NB):
                k.mmg(ps, wp[:, cb, :], hln[cb][:, sl], cb == 0, cb == NB - 1, [wpres, glu_r[cb]], [pres])
            b = tl % 2
            k.ts("dve", yb[b], ps, V.cb2[:, db:db + 1], g1[:, 16 + db, 0:1], ALU.add, ALU.mult,
                 [pres, V.cb2_r, mres], [yb_r[b]])
            k.tt("dve", g.hT[:, db, sl], g.hT[:, db, sl], yb[b], ALU.add,
                 [yb_r[b], g.h_res[db][tl]], [g.h_res[db][tl]])
    barrier(g)
    ar.release(m)


def psbf(ps):
    return ps.bitcast(BF16)


def ssd_layer(g):
    k, ar, c, V, nc = g.k, g.ar, g.c, g.V, g.k.nc
    top_save = ar.top
    ar.top = g.h_off
    S = Ctx()
    g.S = S
    S.cres = Res("ssdc")

    tmpf = None

    def tri(dst_b, fill, pattern, cm, base=0, view=None, init=1.0):
        src = tmpf[:, 0:dst_b.shape[1]]
        k.memset("pool", src, init, [S.cres])
        vv = src if view is None else view(src)
        k.op("pool", lambda e: e.affine_select(out=vv, in_=vv, pattern=pattern, compare_op=ALU.is_ge,
                                               fill=fill, base=base, channel_multiplier=cm),
             [S.cres], [S.cres])
        k.cp("pool", dst_b, src, [S.cres], [S.cres])
    S.U = [ar.bf(128), ar.bf(128)]
    S.Vm = [ar.bf(128), ar.bf(128)]
    S.NEG = [ar.bf(512), ar.bf(512)]
    S.dtb = ar.f32(64)
    S.Abc = ar.f32(64)
    S.Dbc = ar.f32(32)
    S.nwbc = ar.f32(DI)
    k.dma("sp", S.dtb, g.I["ssd_dt_bias"].partition_broadcast(128), [], [S.cres])
    k.dma("sp", S.Abc, g.I["ssd_a_log"].partition_broadcast(128), [], [S.cres])
    k.dma("sp", S.Dbc, g.I["ssd_d"].partition_broadcast(128), [], [S.cres])
    k.dma("sp", S.nwbc, g.I["ssd_norm_w"].partition_broadcast(128), [], [S.cres])
    k.act(S.Abc, S.Abc, AF.Exp, [S.cres], [S.cres])
    k.ts("dve", S.Abc, S.Abc, -1.0, None, ALU.mult, None, [S.cres], [S.cres])
    NCH = (TC + T) // 128
    S.dt = r3(ar.f32(NCH * 64), NCH)
    S.dthi = r3(ar.bf(NCH * 64), NCH)
    S.dtlo = r3(ar.bf(NCH * 64), NCH)
    S.ndthi = r3(ar.bf(NCH * 64), NCH)
    S.ndtlo = r3(ar.bf(NCH * 64), NCH)
    S.dt_r = Res("dtall")
    S.St = [ar.f32(DI), ar.f32(DI)]
    S.St_r = [[Res(f"S{d}_{q}") for q in range(4)] for d in range(2)]
    for d_ in range(2):
        k.memset("pool", S.St[d_], 0.0, S.St_r[d_])
    barrier(g)
    base_mark = ar.mark()
    tmpf = ar.f32(512)
    tri(S.U[0], 0.0, [[1, 128]], -1)
    tri(S.U[1], 0.0, [[-1, 128]], 1)
    tri(S.Vm[0], 0.0, [[-1, 128]], 1, base=-1)
    tri(S.Vm[1], 0.0, [[1, 128]], -1, base=-1)
    tri(S.NEG[0], -30000.0, [[0, 4], [1, 128]], -1, view=lambda a: r3(a, 4), init=0.0)
    tri(S.NEG[1], -30000.0, [[0, 4], [-1, 128]], 1, view=lambda a: r3(a, 4), init=0.0)
    barrier(g)
    ar.release(base_mark)
    seqs = []
    for nm, L, aT, a_res, ch0 in (("c", TC, g.acT, g.ac_res, 0), ("l", T, g.aT, g.a_res, TC // 128)):
        q_ = Ctx()
        q_.L, q_.aT, q_.a_res, q_.ch0, q_.nm = L, aT, a_res, ch0, nm
        q_.ZS = nc.dram_tensor("ZS" + nm, [L, DI], F32, kind="Internal").ap()
        q_.YP = nc.dram_tensor("YP" + nm, [L, DI], F32, kind="Internal").ap()
        q_.Y1 = nc.dram_tensor("Y1" + nm, [L, D], F32, kind="Internal").ap()
        q_.XT = nc.dram_tensor("XT" + nm, [L, DI], BF16, kind="Internal").ap()
        q_.BK = nc.dram_tensor("BK" + nm, [L, 1024], BF16, kind="Internal").ap()
        q_.BTd = nc.dram_tensor("BT" + nm, [L // 128, 128, 1024], BF16, kind="Internal").ap()
        q_.CTd = nc.dram_tensor("CT" + nm, [L // 128, 128, 1024], BF16, kind="Internal").ap()
        q_.dres = Res("dram" + nm)
        seqs.append(q_)
    for q_ in seqs:
        ar.top = top_save
        ssd_phaseA(g, q_)
        barrier(g)
    ar.release(base_mark)
    ssd_sweeps(g, seqs)
    barrier(g)
    ar.top = top_save
    S.dres_all = Res('dres_all')
    load_stream(g, g.I["x"], g.hT, g.h_res, T, ysrc=seqs[1].Y1, gate=(g.modT[0], 16, 0, g.mod_r[0]))
    load_stream(g, g.I["ctx"], g.hcT, g.hc_res, TC, ysrc=seqs[0].Y1, gate=(g.modT[0], 16, 1, g.mod_r[0]))


def ssd_phaseA(g, q_):
    k, ar, c, V, S = g.k, g.ar, g.c, g.V, g.S
    L, aT, a_res = q_.L, q_.aT, q_.a_res
    tw = min(512, L)
    nt = L // tw
    nch = L // 128
    win = g.I["ssd_w_in"]
    tmp = [ar.f32(64), ar.f32(64)]
    tmp_r = [Res("dtt0"), Res("dtt1")]
    specs = [(win, 8, DI + CONVD, 64, 0)] + [(win, 8, cg_ * 512, 512, 0) for cg_ in range(4)] + \
            [(win, 8, DI + f_ * 512, 512, 0) for f_ in range(8)]
    wsa = WStream(g, specs, g.wbf, g.wbfres, 1)
    wdt, wdtres = wsa.get(0)
    for ch in range(nch):
        gch = q_.ch0 + ch
        ps, pres = g.bank()
        ps = ps[:, 0:64]
        tl = (ch * 128) // tw
        for kb in range(NB):
            k.mmg(ps, aT[:, kb, ch * 128:(ch + 1) * 128], wdt[:, kb, :], kb == 0, kb == NB - 1,
                 [wdtres, a_res[kb][tl]], [pres])
        b = ch % 2
        k.tt("dve", tmp[b], ps, S.dtb, ALU.add, [pres, S.cres], [tmp_r[b]])
        k.act(S.dt[:, gch, :], tmp[b], AF.Softplus, [tmp_r[b]], [S.dt_r])
        k.tt("dve", tmp[b], S.dt[:, gch, :], S.Abc, ALU.mult, [S.dt_r, S.cres], [tmp_r[b]])
        k.cp("dve", S.dthi[:, gch, :], tmp[b], [tmp_r[b]], [S.dt_r])
        k.tt("dve", S.dtlo[:, gch, :], tmp[b], S.dthi[:, gch, :], ALU.subtract, [tmp_r[b], S.dt_r], [S.dt_r])
        k.ts("pool", S.ndthi[:, gch, :], S.dthi[:, gch, :], -1.0, None, ALU.mult, None, [S.dt_r], [S.dt_r])
        k.ts("pool", S.ndtlo[:, gch, :], S.dtlo[:, gch, :], -1.0, None, ALU.mult, None, [S.dt_r], [S.dt_r])
    zs = [ar.f32(512), ar.f32(512)]
    zs_r = [Res("zs0"), Res("zs1")]
    zi = 0
    for cg in range(4):
        wz, wzres = wsa.get(1 + cg)
        for ch in range(nch):
            tl = (ch * 128) // tw
            ps, pres = g.bank()
            for kb in range(NB):
                k.mmg(ps, aT[:, kb, ch * 128:(ch + 1) * 128], wz[:, kb, :], kb == 0, kb == NB - 1,
                     [wzres, a_res[kb][tl]], [pres])
            b = zi % 2
            zi += 1
            k.act(zs[b], ps, AF.Silu, [pres], [zs_r[b]])
            k.dma("sp", q_.ZS[ch * 128:(ch + 1) * 128, cg * 512:(cg + 1) * 512], zs[b], [zs_r[b]], [])
    pres_ = [ar.bf(L + 4), ar.bf(L + 4)]
    pre_rs = [Res("pre0"), Res("pre1")]
    xcs = [ar.bf(L), ar.bf(L)]
    xc_rs = [Res("xc0"), Res("xc1")]
    tokTs = [r3(ar.bf(nch * 128), nch), r3(ar.bf(nch * 128), nch)]
    tokT_rs = [Res("tokT0"), Res("tokT1")]
    diags = [[ar.bf(128) for _ in range(5)] for _ in range(2)]
    dg_rs = [Res("sdiag0"), Res("sdiag1")]
    for b in range(2):
        k.memset("pool", pres_[b], 0.0, [pre_rs[b]])
    for fbg in range(8):
        w, wres = wsa.get(5 + fbg)
        for fi in range(4):
            fb = fbg * 4 + fi
            pb_ = fb % 2
            pre, pre_r, xc, xc_r = pres_[pb_], pre_rs[pb_], xcs[pb_], xc_rs[pb_]
            tokT, tokT_r, diag, dg_r = tokTs[pb_], tokT_rs[pb_], diags[pb_], dg_rs[pb_]
            for tl in range(nt):
                ps, pres = g.bank()
                ps = ps[:, 0:tw]
                for kb in range(NB):
                    k.mmg(ps, w[:, kb, fi * 128:(fi + 1) * 128], aT[:, kb, tl * tw:(tl + 1) * tw],
                         kb == 0, kb == NB - 1, [wres, a_res[kb][tl]], [pres])
                k.cp("act", pre[:, 2 + tl * tw:2 + (tl + 1) * tw], ps, [pres], [pre_r])
            for tp in range(5):
                col = tp * 32 + fb
                k.ts("dve", diag[tp], c.ident_b, V.scw[:, col:col + 1], None, ALU.mult, None,
                     [c.res, V.scw_r], [dg_r])
            for tl in range(nt):
                ps, pres = g.bank()
                ps = ps[:, 0:tw]
                for tp in range(5):
                    k.mmg(ps, diag[tp], pre[:, tl * tw + tp:tl * tw + tp + tw], tp == 0, tp == 4,
                         [dg_r, pre_r], [pres])
                k.act(xc[:, tl * tw:(tl + 1) * tw], ps, AF.Silu, [pres, V.scb_r], [xc_r],
                      bias=V.scb[:, fb:fb + 1])
            if fb < 24:
                n4 = 2 if nch < 4 else 4
                for c4 in range(nch // n4):
                    ps, pres = g.bank()
                    pb = psbf(ps)
                    for j in range(n4):
                        ch = c4 * n4 + j
                        k.tr(pb[:, j * 128:(j + 1) * 128], xc[:, ch * 128:(ch + 1) * 128], c.ident_b,
                             [xc_r, c.res], [pres])
                    k.cp("dve", tokT[:, c4 * n4:(c4 + 1) * n4, :], r3(pb[:, 0:n4 * 128], n4), [pres], [tokT_r])
                if fb < 16:
                    dst = q_.XT.rearrange("(c p) f -> p c f", p=128)[:, :, fb * 128:(fb + 1) * 128]
                else:
                    dst = q_.BK.rearrange("(c p) f -> p c f", p=128)[:, :, (fb - 16) * 128:(fb - 15) * 128]
                k.dma("sp", dst, tokT, [tokT_r], [])
            if fb >= 16:
                gi = (fb - 16) % 8
                dd = q_.BTd if fb < 24 else q_.CTd
                dst = dd.rearrange("c n (g t) -> n c g t", g=8)[:, :, gi, :]
                k.dma("sp", dst, r3(xc, nch), [xc_r], [])


PIPE = True
WQ = "sp"
BG_INTERLEAVE = True


def ssd_sweeps(g, seqs):
    k, ar, c, V, S = g.k, g.ar, g.c, g.V, g.S
    wout = r3(ar.bf(16 * D), 16)
    wout_r = Res("wout")
    for qc in range(4):
        st = r3(g.wst[:, 0:4096], 16)
        rs = [g.wstres, g.wsth_res[0], g.wsth_res[1]]
        k.dma("sp", st, g.I["ssd_w_out"][:, qc * 256:(qc + 1) * 256].rearrange("(kb p) n -> p kb n", p=128),
              [], rs)
        k.cp("pool", wout[:, :, qc * 256:(qc + 1) * 256], st, rs, [wout_r])
    Sbf = ar.bf(DI)
    Sbf_r = [Res(f"Sbf{q}") for q in range(8)]
    St_r = [[Res(f"St{d}_{q}") for q in range(8)] for d in range(2)]
    LB = []
    for P in range(3):
        b = Ctx()
        b.xtok = ar.bf(DI); b.btok = ar.bf(1024)
        b.BT = r3(ar.bf(1024), 8); b.CT = r3(ar.bf(1024), 8)
        b.x_r = Res(f"inx{P}"); b.b_r = Res(f"inb{P}"); b.BT_r = Res(f"inBT{P}"); b.CT_r = Res(f"inCT{P}")
        LB.append(b)
    CH = []
    for P in range(2):
        b = Ctx()
        b.cbT = r3(ar.bf(1024), 8); b.cb_r = Res(f"cbT{P}")
        b.eall = ar.f32(96); b.dtdec = ar.f32(32); b.e_r = Res(f"eall{P}")
        b.xdt = ar.bf(DI); b.xdd = ar.bf(DI); b.xdt_r = Res(f"xdt{P}"); b.xdd_r = Res(f"xdd{P}")
        CH.append(b)
    UN = []
    for P in range(3):
        b = Ctx()
        b.E = ar.bf(512); b.E_r = Res(f"E{P}")
        b.MT = ar.bf(512); b.MT_r = Res(f"MT{P}")
        UN.append(b)
    UY = []
    for P in range(2):
        b = Ctx()
        b.ytmp = ar.f32(256); b.ytmp_r = Res(f"ytmp{P}")
        b.ytmp2 = ar.f32(256); b.ytmp2_r = Res(f"ytmpb{P}")
        b.ydir = ar.f32(256); b.ydir_r = Res(f"ydir{P}")
        UY.append(b)
    UL = []
    for P in range(3):
        b = Ctx()
        b.yp = ar.f32(256); b.yp_r = Res(f"ypt{P}")
        b.zs = ar.f32(256); b.zs_r = Res(f"zst{P}")
        UL.append(b)
    yg = ar.f32(DI); yg_r = Res("yg")
    yn = ar.bf(DI); yn_r = Res("yn")
    ssq = ar.f32(16); ssq_r = Res("ssq")
    ynT = r3(ar.bf(DI), 16); ynT_r = Res("ynT")
    B = g.banks
    seg_b = [B[0], B[1], B[2]]
    y_b = [B[3], B[4]]
    os_b = [B[5], B[6]]
    pro_b = [B[7], B[7]]

    def loads(q_, ci, ch, d):
        Lb = LB[ci % 3]
        rows = slice(ch * 128, (ch + 1) * 128)
        k.dma("sp", Lb.BT.rearrange("p a b -> p (a b)"), q_.BTd[ch], [], [Lb.BT_r])
        k.dma("sp", Lb.CT.rearrange("p a b -> p (a b)"), q_.CTd[ch], [], [Lb.CT_r])
        k.dma("sp", Lb.xtok, q_.XT[rows, :], [], [Lb.x_r])
        k.dma("sp", Lb.btok, q_.BK[rows, :], [], [Lb.b_r])

    def prologue(q_, ci, ch, d):
        P = CH[ci % 2]
        Lb = LB[ci % 3]
        gch = q_.ch0 + ch
        dsl = slice(d * 32, (d + 1) * 32)
        for half in range(2):
            ps, pres = pro_b[0]
            for gl in range(4):
                gg = half * 4 + gl
                k.mmg(ps[:, gl * 128:(gl + 1) * 128], Lb.BT[:, gg, :], Lb.CT[:, gg, :], gl == 0, gl == 3,
                      [Lb.BT_r, Lb.CT_r], [pres])
            k.cp("act", P.cbT[:, half * 4:(half + 1) * 4, :], r3(ps, 4), [pres], [P.cb_r])
        ps, pres = pro_b[1]
        first = True
        for ci_, lh in enumerate((S.Vm[d], S.U[d], c.ones_b)):
            k.mmg(ps[:, ci_ * 32:(ci_ + 1) * 32], lh, S.dthi[:, gch, dsl], first, False,
                  [S.dt_r, S.cres, c.res], [pres])
            first = False
            k.mmg(ps[:, ci_ * 32:(ci_ + 1) * 32], lh, S.dtlo[:, gch, dsl], False, ci_ == 2,
                  [S.dt_r, S.cres, c.res], [pres])
        k.act(P.eall, ps[:, 0:96], AF.Exp, [pres], [P.e_r])
        k.tt("dve", P.dtdec, S.dt[:, gch, dsl], P.eall[:, 0:32], ALU.mult, [S.dt_r, P.e_r], [P.e_r])
        k.tt("pool", r3(P.xdt, 32), r3(Lb.xtok, 32),
             S.dt[:, gch, dsl].unsqueeze(2).broadcast_to([128, 32, 64]), ALU.mult, [Lb.x_r, S.dt_r], [P.xdt_r])
        k.tt("pool", r3(P.xdd, 32), r3(Lb.xtok, 32), P.dtdec.unsqueeze(2).broadcast_to([128, 32, 64]),
             ALU.mult, [Lb.x_r, P.e_r], [P.xdd_r])

    def seg(q_, ci, ch, d, gg, ui):
        P = CH[ci % 2]
        U_ = UN[ui % 3]
        gch = q_.ch0 + ch
        if d == 1:
            L_ = UL[ui % 3]
            rows_ = slice(ch * 128, (ch + 1) * 128)
            gc_ = slice(gg * 256, (gg + 1) * 256)
            k.dma("sp", L_.yp, q_.YP[rows_, gc_], [], [L_.yp_r])
            k.dma("sp", L_.zs, q_.ZS[rows_, gc_], [], [L_.zs_r])
        ps, pres = seg_b[ui % 3]
        first = True
        for hl in range(4):
            h = d * 32 + gg * 4 + hl
            for arr in (S.dthi, S.dtlo):
                k.mmg(ps[:, hl * 128:(hl + 1) * 128], arr[:, gch, h:h + 1].broadcast_to([128, 128]), S.U[d],
                     first, False, [S.dt_r, S.cres], [pres])
                first = False
        hs = d * 32 + gg * 4
        for arr in (S.ndthi, S.ndtlo):
            k.mmg(ps, S.U[d], arr[:, gch, hs:hs + 4].unsqueeze(2).broadcast_to([128, 4, 128]), False, False,
                 [S.dt_r, S.cres], [pres])
        k.mmg(ps, c.ident_b, S.NEG[d], False, True, [c.res, S.cres], [pres])
        k.act(U_.E, ps, AF.Exp, [pres], [U_.E_r])
        k.tt("pool", r3(U_.MT, 4), r3(U_.E, 4), P.cbT[:, gg, :].unsqueeze(1).broadcast_to([128, 4, 128]),
             ALU.mult, [U_.E_r, P.cb_r], [U_.MT_r])

    def rest(q_, ci, ch, d, gg, ui):
        P = CH[ci % 2]
        Lb = LB[ci % 3]
        U_ = UN[ui % 3]
        Y_ = UY[ui % 2]
        rows = slice(ch * 128, (ch + 1) * 128)
        gc = slice(gg * 256, (gg + 1) * 256)
        psY, presY = y_b[ui % 2]
        for hl in range(4):
            h = gg * 4 + hl
            k.mmg(psY[:, hl * 64:(hl + 1) * 64], U_.MT[:, hl * 128:(hl + 1) * 128], P.xdt[:, h * 64:(h + 1) * 64],
                 hl == 0, hl == 3, [U_.MT_r, P.xdt_r], [presY])
        psO, presO = os_b[ui % 2]
        k.mmg(psO[:, 0:256], Lb.CT[:, gg, :], Sbf[:, gc], True, False, [Lb.CT_r, Sbf_r[gg]], [presO])
        k.mmg(psO[:, 256:512], Lb.btok[:, gg * 128:(gg + 1) * 128], P.xdd[:, gc], False, True,
             [Lb.b_r, P.xdd_r], [presO])
        k.tt("dve", r3(Y_.ytmp, 4), r3(psO[:, 0:256], 4),
             P.eall[:, 32 + gg * 4:32 + (gg + 1) * 4].unsqueeze(2).broadcast_to([128, 4, 64]), ALU.mult,
             [presO, P.e_r], [Y_.ytmp_r])
        k.tt("dve", Y_.ydir, psY[:, 0:256], Y_.ytmp, ALU.add, [presY, Y_.ytmp_r], [Y_.ydir_r])
        if d == 0:
            k.tt("dve", r3(Y_.ytmp2, 4), r3(Lb.xtok[:, gc], 4),
                 S.Dbc[:, gg * 4:(gg + 1) * 4].unsqueeze(2).broadcast_to([128, 4, 64]), ALU.mult,
                 [Lb.x_r, S.cres], [Y_.ytmp2_r])
            k.tt("dve", Y_.ydir, Y_.ydir, Y_.ytmp2, ALU.add, [Y_.ytmp2_r, Y_.ydir_r], [Y_.ydir_r])
            k.dma("sp", q_.YP[rows, gc], Y_.ydir, [Y_.ydir_r], [])
        else:
            L_ = UL[ui % 3]
            k.tt("dve", Y_.ydir, Y_.ydir, L_.yp, ALU.add, [Y_.ydir_r, L_.yp_r], [Y_.ydir_r])
            k.tt("dve", yg[:, gc], Y_.ydir, L_.zs, ALU.mult, [Y_.ydir_r, L_.zs_r], [yg_r])
        st = S.St[d][:, gc]
        k.tt("pool", r3(st, 4), r3(st, 4),
             P.eall[:, 64 + gg * 4:64 + (gg + 1) * 4].unsqueeze(2).broadcast_to([128, 4, 64]), ALU.mult,
             [St_r[d][gg], P.e_r], [St_r[d][gg]])
        k.tt("dve", st, st, psO[:, 256:512], ALU.add, [St_r[d][gg], presO], [St_r[d][gg]])
        k.cp("dve", Sbf[:, gc], st, [St_r[d][gg]], [Sbf_r[gg]])

    def epilogue(q_, ci, ch, d, gg, ui):
        rows = slice(ch * 128, (ch + 1) * 128)
        sqv = yn.bitcast(F32)
        yout = yn.bitcast(F32)

        def p0():
            for hf in range(2):
                hs = slice(hf * 1024, (hf + 1) * 1024)
                k.tt("dve", sqv, yg[:, hs], yg[:, hs], ALU.mult, [yg_r], [yn_r])
                k.op("dve", lambda e, hf=hf: e.reduce_sum(ssq[:, hf:hf + 1], sqv, axis=AX.X), [yn_r], [ssq_r])
            k.tt("dve", ssq[:, 8:9], ssq[:, 0:1], ssq[:, 1:2], ALU.add, [ssq_r], [ssq_r])
            k.ts("dve", ssq[:, 9:10], ssq[:, 8:9], 1.0 / DI, EPS, ALU.mult, ALU.add, [ssq_r], [ssq_r])
            k.act(ssq[:, 9:10], ssq[:, 9:10], AF.Sqrt, [ssq_r], [ssq_r])
            k.op("dve", lambda e: e.reciprocal(ssq[:, 10:11], ssq[:, 9:10]), [ssq_r], [ssq_r])
            k.stt("dve", yn, yg, ssq[:, 10:11], S.nwbc, ALU.mult, ALU.mult, [yg_r, ssq_r, S.cres], [yn_r])

        def ptr(f4):
            def run():
                ps, pres = g.banks[7]
                pb = psbf(ps)
                for j in range(4):
                    fb = f4 * 4 + j
                    k.tr(pb[:, j * 128:(j + 1) * 128], yn[:, fb * 128:(fb + 1) * 128], c.ident_b,
                         [yn_r, c.res], [pres])
                k.cp("act", ynT[:, f4 * 4:(f4 + 1) * 4, :], r3(pb[:, 0:512], 4), [pres], [ynT_r])
            return run

        def pout(half):
            def run():
                ps, pres = g.banks[7]
                for fb in range(16):
                    k.mmg(ps, ynT[:, fb, :], wout[:, fb, half * 512:(half + 1) * 512], fb == 0, fb == 15,
                          [ynT_r, wout_r], [pres])
                k.cp("act", yout[:, half * 512:(half + 1) * 512], ps, [pres], [yn_r])
                if half == 1:
                    k.dma("sp", q_.Y1[rows, :], yout, [yn_r], [])
            return run
        return [p0] + [ptr(f4) for f4 in range(4)] + [pout(0), pout(1)]

    ui = 0
    for q_ in seqs:
        nch = q_.L // 128
        for d in range(2):
            for gq in range(8):
                k.cp("pool", Sbf[:, gq * 256:(gq + 1) * 256], S.St[d][:, gq * 256:(gq + 1) * 256],
                     [St_r[d][gq]], [Sbf_r[gq]])
            order = list(range(nch)) if d == 0 else list(range(nch - 1, -1, -1))
            units = [(q_, ci, ch, d, gg) for ci, ch in enumerate(order) for gg in range(8)]
            AHEAD = 2
            nu = len(units)
            PRO = 5
            LD = 10
            pending = []
            for step in range(-LD, nu + AHEAD):
                lstep = step + LD
                if 0 <= lstep < nu and units[lstep][4] == 0:
                    u = units[lstep]
                    loads(u[0], u[1], u[2], u[3])
                pstep = step + PRO
                if 0 <= pstep < nu and units[pstep][4] == 0:
                    u = units[pstep]
                    prologue(u[0], u[1], u[2], u[3])
                if 0 <= step < nu:
                    seg(*units[step], ui + step)
                if step >= AHEAD:
                    vi = step - AHEAD
                    v = units[vi]
                    rest(*v, ui + vi)
                    if pending:
                        pending.pop(0)()
                    if v[4] == 7:
                        if d == 1:
                            while pending:
                                pending.pop(0)()
                            pcs = epilogue(*v, ui + vi)
                            pcs.pop(0)()
                            pending.extend(pcs)
                        if g.bg:
                            g.bg.pop(0)()
            while pending:
                pending.pop(0)()
            ui += nu
            barrier(g)


def load_vec(g, src_flat, n):
    k, ar, c = g.k, g.ar, g.c
    dst = ar.f32(n)
    dres = Res("vec")
    done = 0
    while done < n:
        m = min(128, n - done)
        if not hasattr(g, "vstage"):
            g.vstage = [g.ar.f32(128), g.ar.f32(128)]
            g.vsres = [Res("vs0"), Res("vs1")]
            g.vs_i = 0
        b = g.vs_i % 2
        g.vs_i += 1
        st, sres = g.vstage[b], g.vsres[b]
        k.dma("sp", st[0:m, :], src_flat[done * 128:(done + m) * 128].rearrange("(r c) -> r c", c=128),
              [], [sres])
        ps, pres = g.bank()
        k.tr(ps[:, 0:m], st[0:m, :], c.ident_f[0:m, 0:m], [sres, c.res], [pres])
        k.cp("dve", dst[:, done:done + m], ps[:, 0:m], [pres], [dres])
        done += m
    return dst, dres


def all_res(rr):
    return [r for row in rr for r in row]


def barrier(g):
    k = g.k
    toks = []
    for e in k.ENG:
        if e in k.cursem and k.cnt[e] > 0:
            toks.append((k.cursem[e], k.cnt[e]))
    for q, slots in k.dma_slots.items():
        n = k.dma_i[q]
        for s_i, sem in enumerate(slots):
            uses = (n - s_i + NSLOT - 1) // NSLOT if n > s_i else 0
            if uses > 0:
                toks.append((sem, 16 * uses))
    for e in k.ENG:
        for t in toks:
            k._wait(e, t)


def load_stream(g, src, dstT, dres, L, ysrc=None, gate=None):
    k, ar, c = g.k, g.ar, g.c
    m = ar.mark()
    xin = [ar.f32(D), ar.f32(D)]
    xres = [Res("xin0"), Res("xin1")]
    yin = [ar.f32(D), ar.f32(D)] if ysrc is not None else None
    yres = [Res("yin0"), Res("yin1")]
    tw = min(L, 512)
    for ch in range(L // 128):
        b = ch % 2
        k.dma("sp", xin[b], src[ch * 128:(ch + 1) * 128, :], [], [xres[b]])
        tl = (ch * 128) // tw
        for q in range(2):
            ps, pres = g.bank()
            for j in range(4):
                blk = q * 4 + j
                k.tr(ps[:, j * 128:(j + 1) * 128], xin[b][:, blk * 128:(blk + 1) * 128], c.ident_f,
                     [xres[b], c.res], [pres], inc=(j == 3))
            k.cp("dve" if q == 0 else "act", dstT[:, q * 4:(q + 1) * 4, ch * 128:(ch + 1) * 128],
                 r3(ps, 4), [pres], [dres[q * 4 + j][tl] for j in range(4)])
        if ysrc is not None:
            modT, g0, s_, mres = gate
            k.dma("sp", yin[b], ysrc[ch * 128:(ch + 1) * 128, :], [], [yres[b]])
            for q in range(2):
                ps, pres = g.bank()
                for j in range(4):
                    blk = q * 4 + j
                    k.tr(ps[:, j * 128:(j + 1) * 128], yin[b][:, blk * 128:(blk + 1) * 128], c.ident_f,
                         [yres[b], c.res], [pres], inc=(j == 3))
                for j in range(4):
                    blk = q * 4 + j
                    dsl = dstT[:, blk, ch * 128:(ch + 1) * 128]
                    k.stt("dve", dsl, ps[:, j * 128:(j + 1) * 128], modT[:, g0 + blk, s_:s_ + 1], dsl,
                          ALU.mult, ALU.add, [pres, mres, dres[blk][tl]], [dres[blk][tl]])
    barrier(g)
    ar.release(m)


FN_STOP = 99


def final_norm(g):
    k, ar, c = g.k, g.ar, g.c
    m = ar.mark()
    fw, fres = load_vec(g, g.I["final_norm_w"], NB)
    if FN_STOP == 0:
        return
    sq = [ar.bf(512), ar.bf(512)]
    sqres = [Res("sq0"), Res("sq1")]
    rstd = ar.f32(512)
    rres = Res("rstd")
    yt = [ar.f32(512), ar.f32(512)]
    ytres = [Res("yt0"), Res("yt1")]
    ost = r3(ar.f32(4 * D), 4)
    ores = Res("ost")
    for tl in range(T // 512):
        sl = slice(tl * 512, (tl + 1) * 512)
        ps, pres = g.bank()
        for blk in range(NB):
            b = blk % 2
            k.act(sq[b], g.hT[:, blk, sl], AF.Square, [g.h_res[blk][tl]], [sqres[b]])
            k.mm(ps, c.ones_b, sq[b], blk == 0, blk == NB - 1, [sqres[b], c.res], [pres])
        if FN_STOP == 1:
            continue
        k.ts("dve", rstd, ps, 1.0 / D, EPS, ALU.mult, ALU.add, [pres], [rres])
        k.act(rstd, rstd, AF.Sqrt, [rres], [rres])
        if FN_STOP == 2:
            continue
        k.op("dve", lambda e: e.reciprocal(rstd, rstd), [rres], [rres])
        if FN_STOP == 3:
            continue
        for blk in range(NB):
            b = blk % 2
            k.stt("dve", yt[b], g.hT[:, blk, sl], fw[:, blk:blk + 1], rstd, ALU.mult, ALU.mult,
                  [g.h_res[blk][tl], fres, rres], [ytres[b]])
            if FN_STOP == 4:
                continue
            ps2, pres2 = g.bank()
            for j in range(4):
                k.tr(ps2[:, j * 128:(j + 1) * 128], yt[b][:, j * 128:(j + 1) * 128], c.ident_f,
                     [ytres[b], c.res], [pres2], inc=(j == 3))
            k.cp("act", ost[:, :, blk * 128:(blk + 1) * 128], r3(ps2, 4), [pres2], [ores])
        tok = k.dma("sp", g.out[sl, :].rearrange("(c p) d -> p c d", p=128), ost, [ores], [])
        k.out_tokens.append(tok)
    ar.release(m)


_CACHE = {}


def _prep_inputs(inp, b):
    f = lambda a: np.ascontiguousarray(np.asarray(a, dtype=np.float32))
    m = {}
    m["x"] = f(inp["x"][b])
    m["ctx"] = f(inp["ctx"][b])
    m["cvec"] = f(np.stack([np.asarray(inp["c"])[b], np.asarray(inp["c_ctx"])], 0))
    for nm in ("mod_w", "mod_b", "norm1_w", "norm2_w", "ffn_w_up", "ffn_conv_b", "ffn_w_down",
               "final_norm_w"):
        m[nm] = f(inp[nm])
    m["ffn_conv_w"] = f(np.asarray(inp["ffn_conv_w"]).reshape(2, 9, FH))
    for nm in ("ssd_w_in", "ssd_conv_w", "ssd_conv_b", "ssd_d", "ssd_norm_w", "ssd_w_out",
               "conf_w_pw1", "conf_b_pw1", "conf_w_dw", "conf_b_dw", "conf_ln_w", "conf_ln_b",
               "conf_w_pw2", "conf_b_pw2"):
        m[nm] = f(np.asarray(inp[nm])[0])
    m["ssd_dt_bias"] = f(np.asarray(inp["ssd_dt_bias"])[0].reshape(64))
    m["ssd_a_log"] = f(np.asarray(inp["ssd_a_log"])[0].reshape(64))
    return m


def kernel(**inputs):
    if "nc" not in _CACHE:
        _CACHE["nc"] = build()
    nc = _CACHE["nc"]
    in_maps = [_prep_inputs(inputs, b) for b in range(8)]
    res = run_bass_kernel_spmd(nc, in_maps, core_ids=list(range(8)))
    return np.stack([np.asarray(r["out"], dtype=np.float32) for r in res.results], 0)
```

```python
import numpy as np
import concourse.bass as bass
import concourse.mybir as mybir
from concourse.bass_utils import run_bass_kernel_spmd

F32 = mybir.dt.float32
BF16 = mybir.dt.bfloat16
AF = mybir.ActivationFunctionType
ALU = mybir.AluOpType
AX = mybir.AxisListType

D = 1024
T = 2048
TC = 256
NB = D // 128
DI = 2048
NH = 32
NG = 8
NS = 128
CONVD = 4096
INDIM = 6208
FH = 2816
NFB = FH // 128
EPS = 1e-6
EPOCH = 30000
NSLOT = 8


class Res:
    __slots__ = ("name", "lw", "rd")

    def __init__(self, name):
        self.name = name
        self.lw = None
        self.rd = {}


class KB:
    ENG = ("sp", "pe", "dve", "act", "pool")

    def __init__(self):
        self.nc = bass.Bass("TRN2", target_bir_lowering=False)
        self.streams = {e: [] for e in self.ENG}
        self.cnt = {e: 0 for e in self.ENG}
        self.cursem = {}
        self.known = {e: {} for e in self.ENG}
        self.semkey = {}
        self.nsem = 0
        self.dma_i = {e: 0 for e in self.ENG}
        self.dma_slots = {}
        self.uid = 0
        self.out_tokens = []

    def newsem(self, name):
        s = self.nc.alloc_semaphore(f"{name}_{self.nsem}")
        self.nsem += 1
        self.semkey[id(s)] = s
        return s

    def sb(self, name, shape, dt):
        self.uid += 1
        return self.nc.alloc_sbuf_tensor(f"{name}_{self.uid}", list(shape), dt)

    def dram(self, name, shape, dt, kind="Internal"):
        return self.nc.dram_tensor(name, list(shape), dt, kind=kind)

    def _engsem(self, e):
        if e not in self.cursem:
            self.cursem[e] = self.newsem("e" + e)
        return self.cursem[e]

    def _wait(self, e, tok):
        sem, val = tok
        k = self.known[e]
        if k.get(id(sem), 0) >= val:
            return
        k[id(sem)] = val
        self.streams[e].append(("w", sem, val))

    def _deps(self, e, reads, writes):
        own = id(self._engsem(e))
        for r in reads:
            if r.lw is not None:
                if e == "pe" and id(r.lw[0]) == own:
                    continue
                self._wait(e, r.lw)
        for w in writes:
            if w.lw is not None and id(w.lw[0]) != own:
                self._wait(e, w.lw)
            for t in w.rd.values():
                if id(t[0]) != own:
                    self._wait(e, t)

    def _mark(self, tok, reads, writes):
        for r in reads:
            k = id(tok[0])
            if k not in r.rd or r.rd[k][1] < tok[1]:
                r.rd[k] = tok
        for w in writes:
            w.lw = tok
            w.rd = {}

    def op(self, e, fn, reads=(), writes=(), inc=True):
        self._deps(e, reads, writes)
        sem = self._engsem(e)
        tok = (sem, self.cnt[e] + 1)
        self.streams[e].append(("o", fn, sem if inc else None, 1))
        if inc:
            self.cnt[e] += 1
            if self.cnt[e] >= EPOCH:
                del self.cursem[e]
                self.cnt[e] = 0
        self._mark(tok, reads, writes)
        return tok

    def dma(self, q, out, in_, reads=(), writes=(), **kw):
        self._deps(q, reads, writes)
        if q not in self.dma_slots:
            self.dma_slots[q] = [self.newsem("d" + q) for _ in range(NSLOT)]
        i = self.dma_i[q]
        self.dma_i[q] += 1
        sem = self.dma_slots[q][i % NSLOT]
        prev = 16 * (i // NSLOT)
        if prev > 0:
            self._wait(q, (sem, prev))
        tok = (sem, prev + 16)
        self.streams[q].append(("o", lambda eng: eng.dma_start(out=out, in_=in_, **kw), sem, 16))
        self._mark(tok, reads, writes)
        return tok

    def finish(self):
        for tok in self.out_tokens:
            self._wait("sp", tok)
        nc = self.nc
        streams = self.streams
        with nc.Block() as block:
            def mk(stream):
                def body(eng):
                    for it in stream:
                        if it[0] == "w":
                            eng.wait_ge(it[1], it[2])
                        else:
                            ins = it[1](eng)
                            if it[2] is not None:
                                ins.then_inc(it[2], it[3])
                return body
            block.sync(mk(streams["sp"]))
            block.tensor(mk(streams["pe"]))
            block.vector(mk(streams["dve"]))
            block.scalar(mk(streams["act"]))
            block.gpsimd(mk(streams["pool"]))
        return nc

    def mm(self, out, lhsT, rhs, start, stop, reads, writes, inc=None):
        if inc is None:
            inc = True
        return self.op("pe", lambda e: e.matmul(out, lhsT, rhs, start=start, stop=stop),
                       reads, writes, inc=inc)

    def mmg(self, out, lhsT, rhs, start, stop, reads, writes):
        return self.mm(out, lhsT, rhs, start, stop, reads, writes, inc=bool(stop))

    def tr(self, out, in_, ident, reads, writes, inc=True):
        return self.op("pe", lambda e: e.transpose(out, in_, ident), reads, writes, inc=inc)

    def act(self, out, in_, func, reads, writes, bias=0.0, scale=1.0, eng="act", accum_out=None):
        if accum_out is None:
            return self.op("act", lambda e: e.activation(out, in_, func, bias=bias, scale=scale),
                           reads, writes)
        return self.op("act", lambda e: e.activation(out, in_, func, bias=bias, scale=scale,
                                                     accum_out=accum_out), reads, writes)

    def tt(self, eng, out, in0, in1, op, reads, writes):
        return self.op(eng, lambda e: e.tensor_tensor(out, in0, in1, op), reads, writes)

    def ts(self, eng, out, in0, s1, s2, op0, op1, reads, writes):
        if s2 is None:
            return self.op(eng, lambda e: e.tensor_scalar(out, in0, s1, None, op0), reads, writes)
        return self.op(eng, lambda e: e.tensor_scalar(out, in0, s1, s2, op0, op1), reads, writes)

    def stt(self, eng, out, in0, scalar, in1, op0, op1, reads, writes):
        return self.op(eng, lambda e: e.scalar_tensor_tensor(out, in0, scalar, in1, op0, op1),
                       reads, writes)

    def cp(self, eng, out, in_, reads, writes):
        if eng == "act":
            return self.op(eng, lambda e: e.copy(out, in_), reads, writes)
        return self.op(eng, lambda e: e.tensor_copy(out, in_), reads, writes)

    def memset(self, eng, ap, val, writes):
        return self.op(eng, lambda e: e.memset(ap, val), (), writes)


class Arena:
    def __init__(self, kb, words):
        self.t = kb.nc.alloc_sbuf_tensor("arena", [128, words], F32)
        self.words = words
        self.top = 0

    def mark(self):
        return self.top

    def release(self, m):
        self.top = m

    def _alloc(self, words):
        words = (words + 7) // 8 * 8
        off = self.top
        self.top += words
        assert self.top <= self.words, f"arena overflow {self.top} > {self.words}"
        return off

    def f32(self, n):
        off = self._alloc(n)
        return self.t[:, off:off + n]

    def bf(self, n):
        w = (n + 1) // 2
        off = self._alloc(w)
        return self.t[:, off:off + w].bitcast(BF16)[:, 0:n]


class Ctx:
    pass


def r3(ap, a):
    return ap.rearrange("p (a b) -> p a b", a=a)


def build(stage=99):
    k = KB()
    nc = k.nc
    g = Ctx()
    g.k = k
    def din(name, shape):
        return nc.dram_tensor(name, list(shape), F32, kind="ExternalInput").ap()
    I = {}
    I["x"] = din("x", [T, D])
    I["ctx"] = din("ctx", [TC, D])
    I["cvec"] = din("cvec", [2, D])
    I["mod_w"] = din("mod_w", [2, D, 6 * D])
    I["mod_b"] = din("mod_b", [2, 6 * D])
    I["norm1_w"] = din("norm1_w", [2, D])
    I["norm2_w"] = din("norm2_w", [2, D])
    I["ssd_w_in"] = din("ssd_w_in", [D, INDIM])
    I["ssd_conv_w"] = din("ssd_conv_w", [5, CONVD])
    I["ssd_conv_b"] = din("ssd_conv_b", [CONVD])
    I["ssd_dt_bias"] = din("ssd_dt_bias", [64])
    I["ssd_a_log"] = din("ssd_a_log", [64])
    I["ssd_d"] = din("ssd_d", [NH])
    I["ssd_norm_w"] = din("ssd_norm_w", [DI])
    I["ssd_w_out"] = din("ssd_w_out", [DI, D])
    I["conf_w_pw1"] = din("conf_w_pw1", [D, 2 * D])
    I["conf_b_pw1"] = din("conf_b_pw1", [2 * D])
    I["conf_w_dw"] = din("conf_w_dw", [31, D])
    I["conf_b_dw"] = din("conf_b_dw", [D])
    I["conf_ln_w"] = din("conf_ln_w", [D])
    I["conf_ln_b"] = din("conf_ln_b", [D])
    I["conf_w_pw2"] = din("conf_w_pw2", [D, D])
    I["conf_b_pw2"] = din("conf_b_pw2", [D])
    I["ffn_w_up"] = din("ffn_w_up", [2, D, 2 * FH])
    I["ffn_conv_w"] = din("ffn_conv_w", [2, 9, FH])
    I["ffn_conv_b"] = din("ffn_conv_b", [2, FH])
    I["ffn_w_down"] = din("ffn_w_down", [2, FH, D])
    I["final_norm_w"] = din("final_norm_w", [D])
    out = nc.dram_tensor("out", [T, D], F32, kind="ExternalOutput").ap()
    g.I = I
    g.out = out

    ar = Arena(k, 53100)
    g.ar = ar
    g.psum = nc.alloc_psum_tensor("psall", [128, 4096], F32)
    g.banks = []
    for i in range(8):
        g.banks.append((g.psum[:, i * 512:(i + 1) * 512], Res(f"psb{i}")))
    g.bank_i = 0

    def bank():
        b = g.banks[g.bank_i % 6]
        g.bank_i += 1
        return b[0], b[1]
    g.bank = bank

    c = Ctx()
    g.c = c
    c.res = Res("consts")
    c.ident_f = ar.f32(128)
    c.ones_f = ar.f32(128)
    c.ident_b = ar.bf(128)
    c.ones_b = ar.bf(128)
    k.memset("pool", c.ident_f, 0.0, [c.res])
    k.op("pool", lambda e: e.affine_select(out=c.ident_f, in_=c.ident_f, pattern=[[-1, 128]],
                                           compare_op=ALU.not_equal, fill=1.0, base=0,
                                           channel_multiplier=1), [c.res], [c.res])
    k.memset("pool", c.ones_f, 1.0, [c.res])
    k.cp("pool", c.ident_b, c.ident_f, [c.res], [c.res])
    k.cp("pool", c.ones_b, c.ones_f, [c.res], [c.res])
    g.vstage = [ar.f32(128), ar.f32(128)]
    g.vsres = [Res("vs0"), Res("vs1")]
    g.vs_i = 0
    g.wst = ar.f32(4096)
    g.wstres = Res("wst")
    g.wsth_res = [Res("wsth0"), Res("wsth1")]
    g.wbf = [ar.bf(4096), ar.bf(4096)]
    g.wbfres = [Res("wbf0"), Res("wbf1")]
    g.w_i = 0

    V = Ctx()
    g.V = V
    V.fnw, V.fnw_r = load_vec(g, I["final_norm_w"], NB)
    V.n1w, V.n1w_r = load_vec(g, I["norm1_w"].rearrange("a b -> (a b)"), 2 * NB)
    V.n2w, V.n2w_r = load_vec(g, I["norm2_w"].rearrange("a b -> (a b)"), 2 * NB)
    V.modb, V.modb_r = load_vec(g, I["mod_b"].rearrange("a b -> (a b)"), 96)
    V.cv, V.cv_r = load_vec(g, I["cvec"].rearrange("a b -> (a b)"), 16)
    V.fcb, V.fcb_r = load_vec(g, I["ffn_conv_b"].rearrange("a b -> (a b)"), 2 * NFB)
    V.fcw, V.fcw_r = load_vec(g, I["ffn_conv_w"].rearrange("a b c -> (a b c)"), 2 * 9 * NFB)
    V.scb, V.scb_r = load_vec(g, I["ssd_conv_b"], 32)
    V.scw, V.scw_r = load_vec(g, I["ssd_conv_w"].rearrange("a b -> (a b)"), 5 * 32)
    V.cb1, V.cb1_r = load_vec(g, I["conf_b_pw1"], 16)
    V.cbdw, V.cbdw_r = load_vec(g, I["conf_b_dw"], 8)
    V.clnw, V.clnw_r = load_vec(g, I["conf_ln_w"], 8)
    V.clnb, V.clnb_r = load_vec(g, I["conf_ln_b"], 8)
    V.cb2, V.cb2_r = load_vec(g, I["conf_b_pw2"], 8)
    V.cdw, V.cdw_r = load_vec(g, I["conf_w_dw"].rearrange("a b -> (a b)"), 31 * 8)
    V.cs = r3(ar.bf(16), 8)
    V.cs_r = Res("cs")
    tmpc = ar.f32(16)
    tmpc_r = Res("tmpc")
    k.act(tmpc, V.cv, AF.Silu, [V.cv_r], [tmpc_r])
    V.cs32 = r3(ar.f32(16), 8)
    for s_ in range(2):
        k.cp("dve", V.cs[:, :, s_], tmpc[:, s_ * 8:(s_ + 1) * 8], [tmpc_r], [V.cs_r])
        k.cp("dve", V.cs32[:, :, s_], tmpc[:, s_ * 8:(s_ + 1) * 8], [tmpc_r], [V.cs_r])
    V.maskL = ar.f32(512)
    V.maskR = ar.f32(512)
    V.mask_r = Res("masks")
    k.memset("pool", V.maskL, 1.0, [V.mask_r])
    k.memset("pool", V.maskR, 1.0, [V.mask_r])
    k.memset("pool", r3(V.maskL, 8)[:, :, 63:64], 0.0, [V.mask_r])
    k.memset("pool", r3(V.maskR, 8)[:, :, 0:1], 0.0, [V.mask_r])
    g.modT = [r3(ar.f32(96), 48), r3(ar.f32(96), 48)]
    g.A1 = [r3(ar.f32(16), 8), r3(ar.f32(16), 8)]
    g.A2 = [r3(ar.f32(16), 8), r3(ar.f32(16), 8)]
    g.mod_r = [Res("mod0"), Res("mod1")]
    g.modrow = ar.f32(256)
    g.modrow_r = Res("modrow")

    g.h_off = ar.top
    g.hT = r3(ar.f32(NB * T), NB)
    g.hcT = r3(ar.f32(NB * TC), NB)
    g.h_res = [[Res(f"h{b}_{t}") for t in range(T // 512)] for b in range(NB)]
    g.hc_res = [[Res(f"hc{b}")] for b in range(NB)]
    g.aT = r3(ar.bf(NB * T), NB)
    g.acT = r3(ar.bf(NB * TC), NB)
    g.a_res = [[Res(f"a{b}_{t}") for t in range(T // 512)] for b in range(NB)]
    g.ac_res = [[Res(f"ac{b}")] for b in range(NB)]
    g.pmark = ar.mark()

    for it in mod_params_items(g, 0):
        it()
    g.bg = mod_params_items(g, 1)
    if stage < 2 or not BG_INTERLEAVE:
        while g.bg:
            g.bg.pop(0)()
    barrier(g)
    load_stream(g, I["x"], g.hT, g.h_res, T)
    load_stream(g, I["ctx"], g.hcT, g.hc_res, TC)
    if stage >= 1:
        modulate(g, g.hT, g.h_res, T, g.A1[0], g.modT[0], 0, 0, g.aT, g.a_res)
        modulate(g, g.hcT, g.hc_res, TC, g.A1[0], g.modT[0], 0, 1, g.acT, g.ac_res)
        barrier(g)
        if stage >= 2:
            ssd_layer(g)
        while g.bg:
            g.bg.pop(0)()
        barrier(g)
    if stage >= 3:
        modulate(g, g.hT, g.h_res, T, g.A2[0], g.modT[0], 24, 0, g.aT, g.a_res)
        modulate(g, g.hcT, g.hc_res, TC, g.A2[0], g.modT[0], 24, 1, g.acT, g.ac_res)
        ffn(g, 0, [(g.aT, g.a_res, T, True, g.hT, g.h_res, 0), (g.acT, g.ac_res, TC, False, g.hcT, g.hc_res, 1)])
        barrier(g)
    if stage >= 4:
        modulate(g, g.hT, g.h_res, T, g.A1[1], g.modT[1], 0, 0, g.aT, g.a_res)
        conformer(g)
        barrier(g)
    if stage >= 5:
        modulate(g, g.hT, g.h_res, T, g.A2[1], g.modT[1], 24, 0, g.aT, g.a_res)
        ffn(g, 1, [(g.aT, g.a_res, T, True, g.hT, g.h_res, 0)])
        barrier(g)
    if stage == 99:
        final_norm(g)
    else:
        dbg = nc.dram_tensor("dbg", [128, NB * T], F32, kind="ExternalOutput").ap()
        dbgc = nc.dram_tensor("dbgc", [128, NB * TC], F32, kind="ExternalOutput").ap()
        k.out_tokens.append(k.dma("sp", dbg, g.hT.rearrange("p a b -> p (a b)"), all_res(g.h_res), []))
        k.out_tokens.append(k.dma("sp", dbgc, g.hcT.rearrange("p a b -> p (a b)"), all_res(g.hc_res), []))
    return k.finish()


def wload(g, src2d, kblks, col0, ncols, row0=0):
    k = g.k
    n = kblks * ncols
    assert n <= 4096
    st = r3(g.wst[:, 0:n], kblks)
    b = g.w_i % 2
    g.w_i += 1
    wb = r3(g.wbf[b][:, 0:n], kblks)
    src = src2d[row0:row0 + kblks * 128, col0:col0 + ncols].rearrange("(kb p) n -> p kb n", p=128)
    rs = [g.wstres, g.wsth_res[0], g.wsth_res[1]]
    k.dma(WQ, st, src, [], rs)
    k.cp("pool" if b == 0 else "act", wb, st, rs, [g.wbfres[b]])
    return wb, g.wbfres[b]


class WStream:
    def __init__(self, g, specs, bufs, bres, ahead):
        self.g, self.specs, self.bufs, self.bres, self.ahead = g, specs, bufs, bres, ahead
        self.loaded = {}
        self.nxt = 0

    def _load(self, i):
        g = self.g
        k = g.k
        src2d, kblks, col0, ncols, row0 = self.specs[i]
        n = kblks * ncols
        b = i % len(self.bufs)
        if n <= 2048:
            h = g.w_i % 2
            stf, stres = g.wst[:, h * 2048:h * 2048 + n], g.wsth_res[h]
        else:
            stf, stres = g.wst[:, 0:n], g.wstres
        g.w_i += 1
        st = r3(stf, kblks)
        wb = r3(self.bufs[b][:, 0:n], kblks)
        src = src2d[row0:row0 + kblks * 128, col0:col0 + ncols].rearrange("(kb p) n -> p kb n", p=128)
        rs = [stres] if n <= 2048 else [g.wstres, g.wsth_res[0], g.wsth_res[1]]
        k.dma("sp", st, src, [], rs)
        k.cp("pool", wb, st, rs, [self.bres[b]])
        self.loaded[i] = (wb, self.bres[b])

    def get(self, i):
        while self.nxt <= min(i + self.ahead, len(self.specs) - 1):
            self._load(self.nxt)
            self.nxt += 1
        return self.loaded.pop(i)


def mod_params_items(g, i):
    k, ar, V = g.k, g.ar, g.V
    mr = g.mod_r[i]
    items = []

    loaded = {}

    def loader(cg):
        def run():
            h = cg % 2
            st = r3(g.wst[:, h * 2048:(h + 1) * 2048], 8)
            src = g.I["mod_w"][i][:, cg * 256:(cg + 1) * 256].rearrange("(kb p) n -> p kb n", p=128)
            k.dma("sp", st, src, [], [g.wsth_res[h]])
            loaded[cg] = (st, g.wsth_res[h])
        return run

    def compute(cg):
        def run():
            psb, pres = g.banks[7]
            w, wres = loaded.pop(cg)
            for kb in range(8):
                k.mmg(psb[0:2, 0:256], V.cs32[:, kb, :], w[:, kb, :], kb == 0, kb == 7, [wres, V.cs_r], [pres])
            k.cp("dve", g.modrow[0:2, :], psb[0:2, 0:256], [pres], [g.modrow_r])
            for j in range(2):
                k.tr(psb[:, 256 + j * 2:256 + (j + 1) * 2], g.modrow[0:2, j * 128:(j + 1) * 128],
                     g.c.ident_f[0:2, 0:2], [g.modrow_r, g.c.res], [pres])
            m0 = cg * 2
            k.tt("dve", g.modT[i][:, m0:m0 + 2, :], r3(psb[:, 256:260], 2),
                 V.modb[:, i * 48 + m0:i * 48 + m0 + 2].unsqueeze(2).broadcast_to([128, 2, 2]), ALU.add,
                 [pres, V.modb_r], [mr])
        return run

    def both(cg):
        def run():
            if cg not in loaded:
                loader(cg)()
            if cg + 1 < 24:
                loader(cg + 1)()
            compute(cg)()
        return run
    for cg in range(24):
        items.append(both(cg))

    def fin():
        k.stt("dve", g.A1[i], g.modT[i][:, 8:16, :], 1.0,
              V.n1w[:, i * 8:(i + 1) * 8].unsqueeze(2).broadcast_to([128, 8, 2]), ALU.add, ALU.mult,
              [mr, V.n1w_r], [mr])
        k.stt("dve", g.A2[i], g.modT[i][:, 32:40, :], 1.0,
              V.n2w[:, i * 8:(i + 1) * 8].unsqueeze(2).broadcast_to([128, 8, 2]), ALU.add, ALU.mult,
              [mr, V.n2w_r], [mr])
    items.append(fin)
    return items


def modulate(g, srcT, sres, L, A, modT, sh0, s_, dstT, dres, mres=None):
    k, ar, c = g.k, g.ar, g.c
    mres = g.mod_r[0] if modT is g.modT[0] else g.mod_r[1]
    m = ar.mark()
    tw = min(512, L)
    sq = [ar.bf(tw), ar.bf(tw)]
    sqres = [Res("sq0"), Res("sq1")]
    rstd = ar.f32(tw)
    rres = Res("rstd")
    tmp = [ar.f32(tw), ar.f32(tw)]
    tres = [Res("t0"), Res("t1")]
    for tl in range(L // tw):
        sl = slice(tl * tw, (tl + 1) * tw)
        ps, pres = g.bank()
        ps = ps[:, 0:tw]
        for blk in range(NB):
            b = blk % 2
            k.act(sq[b], srcT[:, blk, sl], AF.Square, [sres[blk][tl]], [sqres[b]])
            k.mm(ps, c.ones_b, sq[b], blk == 0, blk == NB - 1, [sqres[b], c.res], [pres])
        k.ts("dve", rstd, ps, 1.0 / D, EPS, ALU.mult, ALU.add, [pres], [rres])
        k.act(rstd, rstd, AF.Sqrt, [rres], [rres])
        k.op("dve", lambda e: e.reciprocal(rstd, rstd), [rres], [rres])
        for blk in range(NB):
            b = blk % 2
            k.tt("dve", tmp[b], srcT[:, blk, sl], rstd, ALU.mult, [sres[blk][tl], rres], [tres[b]])
            k.act(dstT[:, blk, sl], tmp[b], AF.Identity, [tres[b], mres], [dres[blk][tl]],
                  bias=modT[:, sh0 + blk, s_:s_ + 1], scale=A[:, blk, s_:s_ + 1])
    barrier(g)
    ar.release(m)


def ffn(g, layer, streams):
    k, ar, c, V = g.k, g.ar, g.c, g.V
    m = ar.mark()
    HL = 66
    SS = []
    for (aT, a_res, L, grid, hT, h_res, s_) in streams:
        q = Ctx()
        q.aT, q.a_res, q.L, q.grid, q.hT, q.h_res, q.s_ = aT, a_res, L, grid, hT, h_res, s_
        q.tw = min(512, L)
        q.nt = L // q.tw
        W = HL + L + HL
        q.gpre = ar.bf(W)
        q.gL = ar.bf(W) if grid else None
        q.gR = ar.bf(W) if grid else None
        q.gres = Res("gpre")
        k.memset("pool", q.gpre, 0.0, [q.gres])
        if grid:
            k.memset("pool", q.gL, 0.0, [q.gres])
            k.memset("pool", q.gR, 0.0, [q.gres])
        q.hid = [ar.bf(L), ar.bf(L)]
        q.hres = [Res("hid0"), Res("hid1")]
        q.taps = [(ky, kx) for ky in range(3) for kx in range(3)] if grid else [(1, kx) for kx in range(3)]
        SS.append(q)
    sg = [ar.f32(512), ar.f32(512)]
    sgres = [Res("sg0"), Res("sg1")]
    sgi = 0
    diag = [ar.bf(128) for _ in range(9)]
    dres = Res("diag")
    wup = g.I["ffn_w_up"][layer]
    wdn = g.I["ffn_w_down"][layer]
    specs = []
    for fp_ in range(NFB // 2):
        specs.append((wup, 8, fp_ * 256, 256, 0))
        specs.append((wup, 8, FH + fp_ * 256, 256, 0))
        specs.append((wdn, 2, 0, D, fp_ * 256))
    wsm = WStream(g, specs, [ar.bf(2048) for _ in range(6)], [Res(f"fw{i_}") for i_ in range(6)], 3)
    g2 = g.modT[layer]
    mres = g.mod_r[layer]
    for fp in range(NFB // 2):
        wv, wvres = wsm.get(fp * 3)
        wg, wgres = wsm.get(fp * 3 + 1)
        for fi in range(2):
            f = fp * 2 + fi
            for q in SS:
                tw = q.tw
                for tl in range(q.nt):
                    ps, pres = g.bank()
                    ps = ps[:, 0:tw]
                    for kb in range(NB):
                        k.mmg(ps, wg[:, kb, fi * 128:(fi + 1) * 128], q.aT[:, kb, tl * tw:(tl + 1) * tw],
                              kb == 0, kb == NB - 1, [wgres, q.a_res[kb][tl]], [pres])
                    dsl = slice(HL + tl * tw, HL + (tl + 1) * tw)
                    k.cp("act", q.gpre[:, dsl], ps, [pres], [q.gres])
                    if q.grid:
                        k.tt("dve", q.gL[:, dsl], ps, V.maskL, ALU.mult, [pres, V.mask_r], [q.gres])
                        k.tt("dve", q.gR[:, dsl], ps, V.maskR, ALU.mult, [pres, V.mask_r], [q.gres])
            for ti in range(9):
                col = layer * 9 * NFB + ti * NFB + f
                k.ts("dve", diag[ti], c.ident_b, V.fcw[:, col:col + 1], None, ALU.mult, None,
                     [c.res, V.fcw_r], [dres])
            for q in SS:
                tw = q.tw
                for tl in range(q.nt):
                    ps, pres = g.bank()
                    ps = ps[:, 0:tw]
                    for ti, (ky, kx) in enumerate(q.taps):
                        srcb = q.gpre if (not q.grid or kx == 1) else (q.gL if kx == 0 else q.gR)
                        off = HL + tl * tw + ((ky - 1) * 64 if q.grid else 0) + (kx - 1)
                        k.mmg(ps, diag[ky * 3 + kx], srcb[:, off:off + tw], ti == 0, ti == len(q.taps) - 1,
                              [dres, q.gres], [pres])
                    b = sgi % 2
                    sgi += 1
                    cbc = layer * NFB + f
                    k.act(sg[b][:, 0:tw], ps, AF.Silu, [pres, V.fcb_r], [sgres[b]], bias=V.fcb[:, cbc:cbc + 1])
                    ps2, pres2 = g.bank()
                    ps2 = ps2[:, 0:tw]
                    for kb in range(NB):
                        k.mmg(ps2, wv[:, kb, fi * 128:(fi + 1) * 128], q.aT[:, kb, tl * tw:(tl + 1) * tw],
                              kb == 0, kb == NB - 1, [wvres, q.a_res[kb][tl]], [pres2])
                    k.tt("dve", q.hid[fi][:, tl * tw:(tl + 1) * tw], ps2, sg[b][:, 0:tw], ALU.mult,
                         [pres2, sgres[b]], [q.hres[fi]])
        wd, wdres = wsm.get(fp * 3 + 2)
        for q in SS:
            tw = q.tw
            for db in range(NB):
                for tl in range(q.nt):
                    ps, pres = g.bank()
                    ps = ps[:, 0:tw]
                    for fi in range(2):
                        k.mmg(ps, wd[:, fi, db * 128:(db + 1) * 128], q.hid[fi][:, tl * tw:(tl + 1) * tw],
                              fi == 0, fi == 1, [wdres, q.hres[fi]], [pres])
                    hsl = q.hT[:, db, tl * tw:(tl + 1) * tw]
                    k.stt("dve", hsl, ps, g2[:, 40 + db, q.s_:q.s_ + 1], hsl, ALU.mult, ALU.add,
                          [pres, mres, q.h_res[db][tl]], [q.h_res[db][tl]])
    barrier(g)
    ar.release(m)


def conformer(g):
    k, ar, c, V = g.k, g.ar, g.c, g.V
    m = ar.mark()
    HL = 16
    W = HL + T + HL
    glu = [ar.bf(W) for _ in range(NB)]
    glu_r = [Res(f"glu{i}") for i in range(NB)]
    sgm = [ar.f32(512), ar.f32(512)]
    sgm_r = [Res("sgm0"), Res("sgm1")]
    w1 = g.I["conf_w_pw1"]
    nt = T // 512
    specs = []
    for q4 in range(2):
        specs.append((w1, 8, q4 * 512, 512, 0))
        specs.append((w1, 8, D + q4 * 512, 512, 0))
    ws1 = WStream(g, specs, g.wbf, g.wbfres, 0)
    w1cur = {}
    for cb in range(NB):
        k.memset("pool", glu[cb], 0.0, [glu_r[cb]])
        if cb % 4 == 0:
            w1cur["a"] = ws1.get((cb // 4) * 2)
            w1cur["g"] = ws1.get((cb // 4) * 2 + 1)
        wa, wares = w1cur["a"][0][:, :, (cb % 4) * 128:(cb % 4 + 1) * 128], w1cur["a"][1]
        wgt, wgres = w1cur["g"][0][:, :, (cb % 4) * 128:(cb % 4 + 1) * 128], w1cur["g"][1]
        for tl in range(nt):
            sl = slice(tl * 512, (tl + 1) * 512)
            psg, presg = g.bank()
            for kb in range(NB):
                k.mmg(psg, wgt[:, kb, :], g.aT[:, kb, sl], kb == 0, kb == NB - 1,
                     [wgres, g.a_res[kb][tl]], [presg])
            b = tl % 2
            k.act(sgm[b], psg, AF.Sigmoid, [presg, V.cb1_r], [sgm_r[b]], bias=V.cb1[:, 8 + cb:9 + cb])
            psa, presa = g.bank()
            for kb in range(NB):
                k.mmg(psa, wa[:, kb, :], g.aT[:, kb, sl], kb == 0, kb == NB - 1,
                     [wares, g.a_res[kb][tl]], [presa])
            k.stt("dve", glu[cb][:, HL + tl * 512:HL + (tl + 1) * 512], psa, V.cb1[:, cb:cb + 1], sgm[b],
                  ALU.add, ALU.mult, [presa, V.cb1_r, sgm_r[b]], [glu_r[cb]])
    barrier(g)
    cv = g.aT
    cv_r = g.a_res
    diag = [ar.bf(128) for _ in range(31)]
    dres = Res("cdiag")
    for cb in range(NB):
        for tp in range(31):
            col = tp * 8 + cb
            k.ts("dve", diag[tp], c.ident_b, V.cdw[:, col:col + 1], None, ALU.mult, None,
                 [c.res, V.cdw_r], [dres])
        for tl in range(nt):
            ps, pres = g.bank()
            for tp in range(31):
                off = HL + tl * 512 + tp - 15
                k.mmg(ps, diag[tp], glu[cb][:, off:off + 512], tp == 0, tp == 30, [dres, glu_r[cb]], [pres])
            k.act(cv[:, cb, tl * 512:(tl + 1) * 512], ps, AF.Identity, [pres, V.cbdw_r], [cv_r[cb][tl]],
                  bias=V.cbdw[:, cb:cb + 1])
    barrier(g)
    sq = [ar.bf(512), ar.bf(512)]
    sq_r = [Res("csq0"), Res("csq1")]
    mean = ar.f32(512)
    rstd = ar.f32(512)
    nmr = ar.f32(512)
    st_r = Res("lnstat")
    t1 = sgm
    t1_r = [Res("lt0"), Res("lt1")]
    hln = glu
    for tl in range(nt):
        sl = slice(tl * 512, (tl + 1) * 512)
        ps1, pres1 = g.bank()
        ps2, pres2 = g.bank()
        for cb in range(NB):
            b = cb % 2
            k.mmg(ps1, c.ones_b, cv[:, cb, sl], cb == 0, cb == NB - 1, [c.res, cv_r[cb][tl]], [pres1])
            k.tt("dve", sq[b], cv[:, cb, sl], cv[:, cb, sl], ALU.mult, [cv_r[cb][tl]], [sq_r[b]])
            k.mm(ps2, c.ones_b, sq[b], cb == 0, cb == NB - 1, [c.res, sq_r[b]], [pres2])
        k.ts("dve", mean, ps1, 1.0 / D, None, ALU.mult, None, [pres1], [st_r])
        k.tt("dve", nmr, mean, mean, ALU.mult, [st_r], [st_r])
        k.stt("dve", rstd, ps2, 1.0 / D, nmr, ALU.mult, ALU.subtract, [pres2, st_r], [st_r])
        k.ts("dve", rstd, rstd, EPS, None, ALU.add, None, [st_r], [st_r])
        k.act(rstd, rstd, AF.Sqrt, [st_r], [st_r])
        k.op("dve", lambda e: e.reciprocal(rstd, rstd), [st_r], [st_r])
        k.stt("dve", nmr, mean, -1.0, rstd, ALU.mult, ALU.mult, [st_r], [st_r])
        for cb in range(NB):
            b = cb % 2
            k.tt("dve", t1[b], cv[:, cb, sl], rstd, ALU.mult, [cv_r[cb][tl], st_r], [t1_r[b]])
            k.tt("dve", t1[b], t1[b], nmr, ALU.add, [t1_r[b], st_r], [t1_r[b]])
            k.act(hln[cb][:, sl], t1[b], AF.Silu, [t1_r[b], V.clnw_r, V.clnb_r], [glu_r[cb]],
                  bias=V.clnb[:, cb:cb + 1], scale=V.clnw[:, cb:cb + 1])
    barrier(g)
    w2 = g.I["conf_w_pw2"]
    ws2 = WStream(g, [(w2, 8, q4 * 512, 512, 0) for q4 in range(2)], g.wbf, g.wbfres, 1)
    w2cur = {}
    g1 = g.modT[1]
    mres = g.mod_r[1]
    yb = sgm
    yb_r = [Res("yb0"), Res("yb1")]
    for db in range(NB):
        if db % 4 == 0:
            w2cur["w"] = ws2.get(db // 4)
        wp, wpres = w2cur["w"][0][:, :, (db % 4) * 128:(db % 4 + 1) * 128], w2cur["w"][1]
        for tl in range(nt):
            sl = slice(tl * 512, (tl + 1) * 512)
            ps, pres = g.bank()
            for cb in range(NB):
                k.mmg(ps, wp[:, cb, :], hln[cb][:, sl], cb == 0, cb == NB - 1, [wpres, glu_r[cb]], [pres])
            b = tl % 2
            k.ts("dve", yb[b], ps, V.cb2[:, db:db + 1], g1[:, 16 + db, 0:1], ALU.add, ALU.mult,
                 [pres, V.cb2_r, mres], [yb_r[b]])
            k.tt("dve", g.hT[:, db, sl], g.hT[:, db, sl], yb[b], ALU.add,
                 [yb_r[b], g.h_res[db][tl]], [g.h_res[db][tl]])
    barrier(g)
    ar.release(m)


def psbf(ps):
    return ps.bitcast(BF16)


def ssd_layer(g):
    k, ar, c, V, nc = g.k, g.ar, g.c, g.V, g.k.nc
    top_save = ar.top
    ar.top = g.h_off
    S = Ctx()
    g.S = S
    S.cres = Res("ssdc")

    tmpf = None

    def tri(dst_b, fill, pattern, cm, base=0, view=None, init=1.0):
        src = tmpf[:, 0:dst_b.shape[1]]
        k.memset("pool", src, init, [S.cres])
        vv = src if view is None else view(src)
        k.op("pool", lambda e: e.affine_select(out=vv, in_=vv, pattern=pattern, compare_op=ALU.is_ge,
                                               fill=fill, base=base, channel_multiplier=cm),
             [S.cres], [S.cres])
        k.cp("pool", dst_b, src, [S.cres], [S.cres])
    S.U = [ar.bf(128), ar.bf(128)]
    S.Vm = [ar.bf(128), ar.bf(128)]
    S.NEG = [ar.bf(512), ar.bf(512)]
    S.dtb = ar.f32(64)
    S.Abc = ar.f32(64)
    S.Dbc = ar.f32(32)
    S.nwbc = ar.f32(DI)
    k.dma("sp", S.dtb, g.I["ssd_dt_bias"].partition_broadcast(128), [], [S.cres])
    k.dma("sp", S.Abc, g.I["ssd_a_log"].partition_broadcast(128), [], [S.cres])
    k.dma("sp", S.Dbc, g.I["ssd_d"].partition_broadcast(128), [], [S.cres])
    k.dma("sp", S.nwbc, g.I["ssd_norm_w"].partition_broadcast(128), [], [S.cres])
    k.act(S.Abc, S.Abc, AF.Exp, [S.cres], [S.cres])
    k.ts("dve", S.Abc, S.Abc, -1.0, None, ALU.mult, None, [S.cres], [S.cres])
    NCH = (TC + T) // 128
    S.dt = r3(ar.f32(NCH * 64), NCH)
    S.dthi = r3(ar.bf(NCH * 64), NCH)
    S.dtlo = r3(ar.bf(NCH * 64), NCH)
    S.ndthi = r3(ar.bf(NCH * 64), NCH)
    S.ndtlo = r3(ar.bf(NCH * 64), NCH)
    S.dt_r = Res("dtall")
    S.St = [ar.f32(DI), ar.f32(DI)]
    S.St_r = [[Res(f"S{d}_{q}") for q in range(4)] for d in range(2)]
    for d_ in range(2):
        k.memset("pool", S.St[d_], 0.0, S.St_r[d_])
    barrier(g)
    base_mark = ar.mark()
    tmpf = ar.f32(512)
    tri(S.U[0], 0.0, [[1, 128]], -1)
    tri(S.U[1], 0.0, [[-1, 128]], 1)
    tri(S.Vm[0], 0.0, [[-1, 128]], 1, base=-1)
    tri(S.Vm[1], 0.0, [[1, 128]], -1, base=-1)
    tri(S.NEG[0], -30000.0, [[0, 4], [1, 128]], -1, view=lambda a: r3(a, 4), init=0.0)
    tri(S.NEG[1], -30000.0, [[0, 4], [-1, 128]], 1, view=lambda a: r3(a, 4), init=0.0)
    barrier(g)
    ar.release(base_mark)
    seqs = []
    for nm, L, aT, a_res, ch0 in (("c", TC, g.acT, g.ac_res, 0), ("l", T, g.aT, g.a_res, TC // 128)):
        q_ = Ctx()
        q_.L, q_.aT, q_.a_res, q_.ch0, q_.nm = L, aT, a_res, ch0, nm
        q_.ZS = nc.dram_tensor("ZS" + nm, [L, DI], F32, kind="Internal").ap()
        q_.YP = nc.dram_tensor("YP" + nm, [L, DI], F32, kind="Internal").ap()
        q_.Y1 = nc.dram_tensor("Y1" + nm, [L, D], F32, kind="Internal").ap()
        q_.XT = nc.dram_tensor("XT" + nm, [L, DI], BF16, kind="Internal").ap()
        q_.BK = nc.dram_tensor("BK" + nm, [L, 1024], BF16, kind="Internal").ap()
        q_.BTd = nc.dram_tensor("BT" + nm, [L // 128, 128, 1024], BF16, kind="Internal").ap()
        q_.CTd = nc.dram_tensor("CT" + nm, [L // 128, 128, 1024], BF16, kind="Internal").ap()
        q_.dres = Res("dram" + nm)
        seqs.append(q_)
    for q_ in seqs:
        ar.top = top_save
        ssd_phaseA(g, q_)
        barrier(g)
    ar.release(base_mark)
    ssd_sweeps(g, seqs)
    barrier(g)
    ar.top = top_save
    S.dres_all = Res('dres_all')
    load_stream(g, g.I["x"], g.hT, g.h_res, T, ysrc=seqs[1].Y1, gate=(g.modT[0], 16, 0, g.mod_r[0]))
    load_stream(g, g.I["ctx"], g.hcT, g.hc_res, TC, ysrc=seqs[0].Y1, gate=(g.modT[0], 16, 1, g.mod_r[0]))


def ssd_phaseA(g, q_):
    k, ar, c, V, S = g.k, g.ar, g.c, g.V, g.S
    L, aT, a_res = q_.L, q_.aT, q_.a_res
    tw = min(512, L)
    nt = L // tw
    nch = L // 128
    win = g.I["ssd_w_in"]
    tmp = [ar.f32(64), ar.f32(64)]
    tmp_r = [Res("dtt0"), Res("dtt1")]
    specs = [(win, 8, DI + CONVD, 64, 0)] + [(win, 8, cg_ * 512, 512, 0) for cg_ in range(4)] + \
            [(win, 8, DI + f_ * 512, 512, 0) for f_ in range(8)]
    wsa = WStream(g, specs, g.wbf, g.wbfres, 1)
    wdt, wdtres = wsa.get(0)
    for ch in range(nch):
        gch = q_.ch0 + ch
        ps, pres = g.bank()
        ps = ps[:, 0:64]
        tl = (ch * 128) // tw
        for kb in range(NB):
            k.mmg(ps, aT[:, kb, ch * 128:(ch + 1) * 128], wdt[:, kb, :], kb == 0, kb == NB - 1,
                 [wdtres, a_res[kb][tl]], [pres])
        b = ch % 2
        k.tt("dve", tmp[b], ps, S.dtb, ALU.add, [pres, S.cres], [tmp_r[b]])
        k.act(S.dt[:, gch, :], tmp[b], AF.Softplus, [tmp_r[b]], [S.dt_r])
        k.tt("dve", tmp[b], S.dt[:, gch, :], S.Abc, ALU.mult, [S.dt_r, S.cres], [tmp_r[b]])
        k.cp("dve", S.dthi[:, gch, :], tmp[b], [tmp_r[b]], [S.dt_r])
        k.tt("dve", S.dtlo[:, gch, :], tmp[b], S.dthi[:, gch, :], ALU.subtract, [tmp_r[b], S.dt_r], [S.dt_r])
        k.ts("pool", S.ndthi[:, gch, :], S.dthi[:, gch, :], -1.0, None, ALU.mult, None, [S.dt_r], [S.dt_r])
        k.ts("pool", S.ndtlo[:, gch, :], S.dtlo[:, gch, :], -1.0, None, ALU.mult, None, [S.dt_r], [S.dt_r])
    zs = [ar.f32(512), ar.f32(512)]
    zs_r = [Res("zs0"), Res("zs1")]
    zi = 0
    for cg in range(4):
        wz, wzres = wsa.get(1 + cg)
        for ch in range(nch):
            tl = (ch * 128) // tw
            ps, pres = g.bank()
            for kb in range(NB):
                k.mmg(ps, aT[:, kb, ch * 128:(ch + 1) * 128], wz[:, kb, :], kb == 0, kb == NB - 1,
                     [wzres, a_res[kb][tl]], [pres])
            b = zi % 2
            zi += 1
            k.act(zs[b], ps, AF.Silu, [pres], [zs_r[b]])
            k.dma("sp", q_.ZS[ch * 128:(ch + 1) * 128, cg * 512:(cg + 1) * 512], zs[b], [zs_r[b]], [])
    pres_ = [ar.bf(L + 4), ar.bf(L + 4)]
    pre_rs = [Res("pre0"), Res("pre1")]
    xcs = [ar.bf(L), ar.bf(L)]
    xc_rs = [Res("xc0"), Res("xc1")]
    tokTs = [r3(ar.bf(nch * 128), nch), r3(ar.bf(nch * 128), nch)]
    tokT_rs = [Res("tokT0"), Res("tokT1")]
    diags = [[ar.bf(128) for _ in range(5)] for _ in range(2)]
    dg_rs = [Res("sdiag0"), Res("sdiag1")]
    for b in range(2):
        k.memset("pool", pres_[b], 0.0, [pre_rs[b]])
    for fbg in range(8):
        w, wres = wsa.get(5 + fbg)
        for fi in range(4):
            fb = fbg * 4 + fi
            pb_ = fb % 2
            pre, pre_r, xc, xc_r = pres_[pb_], pre_rs[pb_], xcs[pb_], xc_rs[pb_]
            tokT, tokT_r, diag, dg_r = tokTs[pb_], tokT_rs[pb_], diags[pb_], dg_rs[pb_]
            for tl in range(nt):
                ps, pres = g.bank()
                ps = ps[:, 0:tw]
                for kb in range(NB):
                    k.mmg(ps, w[:, kb, fi * 128:(fi + 1) * 128], aT[:, kb, tl * tw:(tl + 1) * tw],
                         kb == 0, kb == NB - 1, [wres, a_res[kb][tl]], [pres])
                k.cp("act", pre[:, 2 + tl * tw:2 + (tl + 1) * tw], ps, [pres], [pre_r])
            for tp in range(5):
                col = tp * 32 + fb
                k.ts("dve", diag[tp], c.ident_b, V.scw[:, col:col + 1], None, ALU.mult, None,
                     [c.res, V.scw_r], [dg_r])
            for tl in range(nt):
                ps, pres = g.bank()
                ps = ps[:, 0:tw]
                for tp in range(5):
                    k.mmg(ps, diag[tp], pre[:, tl * tw + tp:tl * tw + tp + tw], tp == 0, tp == 4,
                         [dg_r, pre_r], [pres])
                k.act(xc[:, tl * tw:(tl + 1) * tw], ps, AF.Silu, [pres, V.scb_r], [xc_r],
                      bias=V.scb[:, fb:fb + 1])
            if fb < 24:
                n4 = 2 if nch < 4 else 4
                for c4 in range(nch // n4):
                    ps, pres = g.bank()
                    pb = psbf(ps)
                    for j in range(n4):
                        ch = c4 * n4 + j
                        k.tr(pb[:, j * 128:(j + 1) * 128], xc[:, ch * 128:(ch + 1) * 128], c.ident_b,
                             [xc_r, c.res], [pres])
                    k.cp("dve", tokT[:, c4 * n4:(c4 + 1) * n4, :], r3(pb[:, 0:n4 * 128], n4), [pres], [tokT_r])
                if fb < 16:
                    dst = q_.XT.rearrange("(c p) f -> p c f", p=128)[:, :, fb * 128:(fb + 1) * 128]
                else:
                    dst = q_.BK.rearrange("(c p) f -> p c f", p=128)[:, :, (fb - 16) * 128:(fb - 15) * 128]
                k.dma("sp", dst, tokT, [tokT_r], [])
            if fb >= 16:
                gi = (fb - 16) % 8
                dd = q_.BTd if fb < 24 else q_.CTd
                dst = dd.rearrange("c n (g t) -> n c g t", g=8)[:, :, gi, :]
                k.dma("sp", dst, r3(xc, nch), [xc_r], [])


PIPE = True
WQ = "sp"
BG_INTERLEAVE = True


def ssd_sweeps(g, seqs):
    k, ar, c, V, S = g.k, g.ar, g.c, g.V, g.S
    wout = r3(ar.bf(16 * D), 16)
    wout_r = Res("wout")
    for qc in range(4):
        st = r3(g.wst[:, 0:4096], 16)
        rs = [g.wstres, g.wsth_res[0], g.wsth_res[1]]
        k.dma("sp", st, g.I["ssd_w_out"][:, qc * 256:(qc + 1) * 256].rearrange("(kb p) n -> p kb n", p=128),
              [], rs)
        k.cp("pool", wout[:, :, qc * 256:(qc + 1) * 256], st, rs, [wout_r])
    Sbf = ar.bf(DI)
    Sbf_r = [Res(f"Sbf{q}") for q in range(8)]
    St_r = [[Res(f"St{d}_{q}") for q in range(8)] for d in range(2)]
    LB = []
    for P in range(3):
        b = Ctx()
        b.xtok = ar.bf(DI); b.btok = ar.bf(1024)
        b.BT = r3(ar.bf(1024), 8); b.CT = r3(ar.bf(1024), 8)
        b.x_r = Res(f"inx{P}"); b.b_r = Res(f"inb{P}"); b.BT_r = Res(f"inBT{P}"); b.CT_r = Res(f"inCT{P}")
        LB.append(b)
    CH = []
    for P in range(2):
        b = Ctx()
        b.cbT = r3(ar.bf(1024), 8); b.cb_r = Res(f"cbT{P}")
        b.eall = ar.f32(96); b.dtdec = ar.f32(32); b.e_r = Res(f"eall{P}")
        b.xdt = ar.bf(DI); b.xdd = ar.bf(DI); b.xdt_r = Res(f"xdt{P}"); b.xdd_r = Res(f"xdd{P}")
        CH.append(b)
    UN = []
    for P in range(3):
        b = Ctx()
        b.E = ar.bf(512); b.E_r = Res(f"E{P}")
        b.MT = ar.bf(512); b.MT_r = Res(f"MT{P}")
        UN.append(b)
    UY = []
    for P in range(2):
        b = Ctx()
        b.ytmp = ar.f32(256); b.ytmp_r = Res(f"ytmp{P}")
        b.ytmp2 = ar.f32(256); b.ytmp2_r = Res(f"ytmpb{P}")
        b.ydir = ar.f32(256); b.ydir_r = Res(f"ydir{P}")
        UY.append(b)
    UL = []
    for P in range(3):
        b = Ctx()
        b.yp = ar.f32(256); b.yp_r = Res(f"ypt{P}")
        b.zs = ar.f32(256); b.zs_r = Res(f"zst{P}")
        UL.append(b)
    yg = ar.f32(DI); yg_r = Res("yg")
    yn = ar.bf(DI); yn_r = Res("yn")
    ssq = ar.f32(16); ssq_r = Res("ssq")
    ynT = r3(ar.bf(DI), 16); ynT_r = Res("ynT")
    B = g.banks
    seg_b = [B[0], B[1], B[2]]
    y_b = [B[3], B[4]]
    os_b = [B[5], B[6]]
    pro_b = [B[7], B[7]]

    def loads(q_, ci, ch, d):
        Lb = LB[ci % 3]
        rows = slice(ch * 128, (ch + 1) * 128)
        k.dma("sp", Lb.BT.rearrange("p a b -> p (a b)"), q_.BTd[ch], [], [Lb.BT_r])
        k.dma("sp", Lb.CT.rearrange("p a b -> p (a b)"), q_.CTd[ch], [], [Lb.CT_r])
        k.dma("sp", Lb.xtok, q_.XT[rows, :], [], [Lb.x_r])
        k.dma("sp", Lb.btok, q_.BK[rows, :], [], [Lb.b_r])

    def prologue(q_, ci, ch, d):
        P = CH[ci % 2]
        Lb = LB[ci % 3]
        gch = q_.ch0 + ch
        dsl = slice(d * 32, (d + 1) * 32)
        for half in range(2):
            ps, pres = pro_b[0]
            for gl in range(4):
                gg = half * 4 + gl
                k.mmg(ps[:, gl * 128:(gl + 1) * 128], Lb.BT[:, gg, :], Lb.CT[:, gg, :], gl == 0, gl == 3,
                      [Lb.BT_r, Lb.CT_r], [pres])
            k.cp("act", P.cbT[:, half * 4:(half + 1) * 4, :], r3(ps, 4), [pres], [P.cb_r])
        ps, pres = pro_b[1]
        first = True
        for ci_, lh in enumerate((S.Vm[d], S.U[d], c.ones_b)):
            k.mmg(ps[:, ci_ * 32:(ci_ + 1) * 32], lh, S.dthi[:, gch, dsl], first, False,
                  [S.dt_r, S.cres, c.res], [pres])
            first = False
            k.mmg(ps[:, ci_ * 32:(ci_ + 1) * 32], lh, S.dtlo[:, gch, dsl], False, ci_ == 2,
                  [S.dt_r, S.cres, c.res], [pres])
        k.act(P.eall, ps[:, 0:96], AF.Exp, [pres], [P.e_r])
        k.tt("dve", P.dtdec, S.dt[:, gch, dsl], P.eall[:, 0:32], ALU.mult, [S.dt_r, P.e_r], [P.e_r])
        k.tt("pool", r3(P.xdt, 32), r3(Lb.xtok, 32),
             S.dt[:, gch, dsl].unsqueeze(2).broadcast_to([128, 32, 64]), ALU.mult, [Lb.x_r, S.dt_r], [P.xdt_r])
        k.tt("pool", r3(P.xdd, 32), r3(Lb.xtok, 32), P.dtdec.unsqueeze(2).broadcast_to([128, 32, 64]),
             ALU.mult, [Lb.x_r, P.e_r], [P.xdd_r])

    def seg(q_, ci, ch, d, gg, ui):
        P = CH[ci % 2]
        U_ = UN[ui % 3]
        gch = q_.ch0 + ch
        if d == 1:
            L_ = UL[ui % 3]
            rows_ = slice(ch * 128, (ch + 1) * 128)
            gc_ = slice(gg * 256, (gg + 1) * 256)
            k.dma("sp", L_.yp, q_.YP[rows_, gc_], [], [L_.yp_r])
            k.dma("sp", L_.zs, q_.ZS[rows_, gc_], [], [L_.zs_r])
        ps, pres = seg_b[ui % 3]
        first = True
        for hl in range(4):
            h = d * 32 + gg * 4 + hl
            for arr in (S.dthi, S.dtlo):
                k.mmg(ps[:, hl * 128:(hl + 1) * 128], arr[:, gch, h:h + 1].broadcast_to([128, 128]), S.U[d],
                     first, False, [S.dt_r, S.cres], [pres])
                first = False
        hs = d * 32 + gg * 4
        for arr in (S.ndthi, S.ndtlo):
            k.mmg(ps, S.U[d], arr[:, gch, hs:hs + 4].unsqueeze(2).broadcast_to([128, 4, 128]), False, False,
                 [S.dt_r, S.cres], [pres])
        k.mmg(ps, c.ident_b, S.NEG[d], False, True, [c.res, S.cres], [pres])
        k.act(U_.E, ps, AF.Exp, [pres], [U_.E_r])
        k.tt("pool", r3(U_.MT, 4), r3(U_.E, 4), P.cbT[:, gg, :].unsqueeze(1).broadcast_to([128, 4, 128]),
             ALU.mult, [U_.E_r, P.cb_r], [U_.MT_r])

    def rest(q_, ci, ch, d, gg, ui):
        P = CH[ci % 2]
        Lb = LB[ci % 3]
        U_ = UN[ui % 3]
        Y_ = UY[ui % 2]
        rows = slice(ch * 128, (ch + 1) * 128)
        gc = slice(gg * 256, (gg + 1) * 256)
        psY, presY = y_b[ui % 2]
        for hl in range(4):
            h = gg * 4 + hl
            k.mmg(psY[:, hl * 64:(hl + 1) * 64], U_.MT[:, hl * 128:(hl + 1) * 128], P.xdt[:, h * 64:(h + 1) * 64],
                 hl == 0, hl == 3, [U_.MT_r, P.xdt_r], [presY])
        psO, presO = os_b[ui % 2]
        k.mmg(psO[:, 0:256], Lb.CT[:, gg, :], Sbf[:, gc], True, False, [Lb.CT_r, Sbf_r[gg]], [presO])
        k.mmg(psO[:, 256:512], Lb.btok[:, gg * 128:(gg + 1) * 128], P.xdd[:, gc], False, True,
             [Lb.b_r, P.xdd_r], [presO])
        k.tt("dve", r3(Y_.ytmp, 4), r3(psO[:, 0:256], 4),
             P.eall[:, 32 + gg * 4:32 + (gg + 1) * 4].unsqueeze(2).broadcast_to([128, 4, 64]), ALU.mult,
             [presO, P.e_r], [Y_.ytmp_r])
        k.tt("dve", Y_.ydir, psY[:, 0:256], Y_.ytmp, ALU.add, [presY, Y_.ytmp_r], [Y_.ydir_r])
        if d == 0:
            k.tt("dve", r3(Y_.ytmp2, 4), r3(Lb.xtok[:, gc], 4),
                 S.Dbc[:, gg * 4:(gg + 1) * 4].unsqueeze(2).broadcast_to([128, 4, 64]), ALU.mult,
                 [Lb.x_r, S.cres], [Y_.ytmp2_r])
            k.tt("dve", Y_.ydir, Y_.ydir, Y_.ytmp2, ALU.add, [Y_.ytmp2_r, Y_.ydir_r], [Y_.ydir_r])
            k.dma("sp", q_.YP[rows, gc], Y_.ydir, [Y_.ydir_r], [])
        else:
            L_ = UL[ui % 3]
            k.tt("dve", Y_.ydir, Y_.ydir, L_.yp, ALU.add, [Y_.ydir_r, L_.yp_r], [Y_.ydir_r])
            k.tt("dve", yg[:, gc], Y_.ydir, L_.zs, ALU.mult, [Y_.ydir_r, L_.zs_r], [yg_r])
        st = S.St[d][:, gc]
        k.tt("pool", r3(st, 4), r3(st, 4),
             P.eall[:, 64 + gg * 4:64 + (gg + 1) * 4].unsqueeze(2).broadcast_to([128, 4, 64]), ALU.mult,
             [St_r[d][gg], P.e_r], [St_r[d][gg]])
        k.tt("dve", st, st, psO[:, 256:512], ALU.add, [St_r[d][gg], presO], [St_r[d][gg]])
        k.cp("dve", Sbf[:, gc], st, [St_r[d][gg]], [Sbf_r[gg]])

    def epilogue(q_, ci, ch, d, gg, ui):
        rows = slice(ch * 128, (ch + 1) * 128)
        sqv = yn.bitcast(F32)
        yout = yn.bitcast(F32)

        def p0():
            for hf in range(2):
                hs = slice(hf * 1024, (hf + 1) * 1024)
                k.tt("dve", sqv, yg[:, hs], yg[:, hs], ALU.mult, [yg_r], [yn_r])
                k.op("dve", lambda e, hf=hf: e.reduce_sum(ssq[:, hf:hf + 1], sqv, axis=AX.X), [yn_r], [ssq_r])
            k.tt("dve", ssq[:, 8:9], ssq[:, 0:1], ssq[:, 1:2], ALU.add, [ssq_r], [ssq_r])
            k.ts("dve", ssq[:, 9:10], ssq[:, 8:9], 1.0 / DI, EPS, ALU.mult, ALU.add, [ssq_r], [ssq_r])
            k.act(ssq[:, 9:10], ssq[:, 9:10], AF.Sqrt, [ssq_r], [ssq_r])
            k.op("dve", lambda e: e.reciprocal(ssq[:, 10:11], ssq[:, 9:10]), [ssq_r], [ssq_r])
            k.stt("dve", yn, yg, ssq[:, 10:11], S.nwbc, ALU.mult, ALU.mult, [yg_r, ssq_r, S.cres], [yn_r])

        def ptr(f4):
            def run():
                ps, pres = g.banks[7]
                pb = psbf(ps)
                for j in range(4):
                    fb = f4 * 4 + j
                    k.tr(pb[:, j * 128:(j + 1) * 128], yn[:, fb * 128:(fb + 1) * 128], c.ident_b,
                         [yn_r, c.res], [pres])
                k.cp("act", ynT[:, f4 * 4:(f4 + 1) * 4, :], r3(pb[:, 0:512], 4), [pres], [ynT_r])
            return run

        def pout(half):
            def run():
                ps, pres = g.banks[7]
                for fb in range(16):
                    k.mmg(ps, ynT[:, fb, :], wout[:, fb, half * 512:(half + 1) * 512], fb == 0, fb == 15,
                          [ynT_r, wout_r], [pres])
                k.cp("act", yout[:, half * 512:(half + 1) * 512], ps, [pres], [yn_r])
                if half == 1:
                    k.dma("sp", q_.Y1[rows, :], yout, [yn_r], [])
            return run
        return [p0] + [ptr(f4) for f4 in range(4)] + [pout(0), pout(1)]

    ui = 0
    for q_ in seqs:
        nch = q_.L // 128
        for d in range(2):
            for gq in range(8):
                k.cp("pool", Sbf[:, gq * 256:(gq + 1) * 256], S.St[d][:, gq * 256:(gq + 1) * 256],
                     [St_r[d][gq]], [Sbf_r[gq]])
            order = list(range(nch)) if d == 0 else list(range(nch - 1, -1, -1))
            units = [(q_, ci, ch, d, gg) for ci, ch in enumerate(order) for gg in range(8)]
            AHEAD = 2
            nu = len(units)
            PRO = 5
            LD = 10
            pending = []
            for step in range(-LD, nu + AHEAD):
                lstep = step + LD
                if 0 <= lstep < nu and units[lstep][4] == 0:
                    u = units[lstep]
                    loads(u[0], u[1], u[2], u[3])
                pstep = step + PRO
                if 0 <= pstep < nu and units[pstep][4] == 0:
                    u = units[pstep]
                    prologue(u[0], u[1], u[2], u[3])
                if 0 <= step < nu:
                    seg(*units[step], ui + step)
                if step >= AHEAD:
                    vi = step - AHEAD
                    v = units[vi]
                    rest(*v, ui + vi)
                    if pending:
                        pending.pop(0)()
                    if v[4] == 7:
                        if d == 1:
                            while pending:
                                pending.pop(0)()
                            pcs = epilogue(*v, ui + vi)
                            pcs.pop(0)()
                            pending.extend(pcs)
                        if g.bg:
                            g.bg.pop(0)()
            while pending:
                pending.pop(0)()
            ui += nu
            barrier(g)


def load_vec(g, src_flat, n):
    k, ar, c = g.k, g.ar, g.c
    dst = ar.f32(n)
    dres = Res("vec")
    done = 0
    while done < n:
        m = min(128, n - done)
        if not hasattr(g, "vstage"):
            g.vstage = [g.ar.f32(128), g.ar.f32(128)]
            g.vsres = [Res("vs0"), Res("vs1")]
            g.vs_i = 0
        b = g.vs_i % 2
        g.vs_i += 1
        st, sres = g.vstage[b], g.vsres[b]
        k.dma("sp", st[0:m, :], src_flat[done * 128:(done + m) * 128].rearrange("(r c) -> r c", c=128),
              [], [sres])
        ps, pres = g.bank()
        k.tr(ps[:, 0:m], st[0:m, :], c.ident_f[0:m, 0:m], [sres, c.res], [pres])
        k.cp("dve", dst[:, done:done + m], ps[:, 0:m], [pres], [dres])
        done += m
    return dst, dres


def all_res(rr):
    return [r for row in rr for r in row]


def barrier(g):
    k = g.k
    toks = []
    for e in k.ENG:
        if e in k.cursem and k.cnt[e] > 0:
            toks.append((k.cursem[e], k.cnt[e]))
    for q, slots in k.dma_slots.items():
        n = k.dma_i[q]
        for s_i, sem in enumerate(slots):
            uses = (n - s_i + NSLOT - 1) // NSLOT if n > s_i else 0
            if uses > 0:
                toks.append((sem, 16 * uses))
    for e in k.ENG:
        for t in toks:
            k._wait(e, t)


def load_stream(g, src, dstT, dres, L, ysrc=None, gate=None):
    k, ar, c = g.k, g.ar, g.c
    m = ar.mark()
    xin = [ar.f32(D), ar.f32(D)]
    xres = [Res("xin0"), Res("xin1")]
    yin = [ar.f32(D), ar.f32(D)] if ysrc is not None else None
    yres = [Res("yin0"), Res("yin1")]
    tw = min(L, 512)
    for ch in range(L // 128):
        b = ch % 2
        k.dma("sp", xin[b], src[ch * 128:(ch + 1) * 128, :], [], [xres[b]])
        tl = (ch * 128) // tw
        for q in range(2):
            ps, pres = g.bank()
            for j in range(4):
                blk = q * 4 + j
                k.tr(ps[:, j * 128:(j + 1) * 128], xin[b][:, blk * 128:(blk + 1) * 128], c.ident_f,
                     [xres[b], c.res], [pres], inc=(j == 3))
            k.cp("dve" if q == 0 else "act", dstT[:, q * 4:(q + 1) * 4, ch * 128:(ch + 1) * 128],
                 r3(ps, 4), [pres], [dres[q * 4 + j][tl] for j in range(4)])
        if ysrc is not None:
            modT, g0, s_, mres = gate
            k.dma("sp", yin[b], ysrc[ch * 128:(ch + 1) * 128, :], [], [yres[b]])
            for q in range(2):
                ps, pres = g.bank()
                for j in range(4):
                    blk = q * 4 + j
                    k.tr(ps[:, j * 128:(j + 1) * 128], yin[b][:, blk * 128:(blk + 1) * 128], c.ident_f,
                         [yres[b], c.res], [pres], inc=(j == 3))
                for j in range(4):
                    blk = q * 4 + j
                    dsl = dstT[:, blk, ch * 128:(ch + 1) * 128]
                    k.stt("dve", dsl, ps[:, j * 128:(j + 1) * 128], modT[:, g0 + blk, s_:s_ + 1], dsl,
                          ALU.mult, ALU.add, [pres, mres, dres[blk][tl]], [dres[blk][tl]])
    barrier(g)
    ar.release(m)


FN_STOP = 99


def final_norm(g):
    k, ar, c = g.k, g.ar, g.c
    m = ar.mark()
    fw, fres = load_vec(g, g.I["final_norm_w"], NB)
    if FN_STOP == 0:
        return
    sq = [ar.bf(512), ar.bf(512)]
    sqres = [Res("sq0"), Res("sq1")]
    rstd = ar.f32(512)
    rres = Res("rstd")
    yt = [ar.f32(512), ar.f32(512)]
    ytres = [Res("yt0"), Res("yt1")]
    ost = r3(ar.f32(4 * D), 4)
    ores = Res("ost")
    for tl in range(T // 512):
        sl = slice(tl * 512, (tl + 1) * 512)
        ps, pres = g.bank()
        for blk in range(NB):
            b = blk % 2
            k.act(sq[b], g.hT[:, blk, sl], AF.Square, [g.h_res[blk][tl]], [sqres[b]])
            k.mm(ps, c.ones_b, sq[b], blk == 0, blk == NB - 1, [sqres[b], c.res], [pres])
        if FN_STOP == 1:
            continue
        k.ts("dve", rstd, ps, 1.0 / D, EPS, ALU.mult, ALU.add, [pres], [rres])
        k.act(rstd, rstd, AF.Sqrt, [rres], [rres])
        if FN_STOP == 2:
            continue
        k.op("dve", lambda e: e.reciprocal(rstd, rstd), [rres], [rres])
        if FN_STOP == 3:
            continue
        for blk in range(NB):
            b = blk % 2
            k.stt("dve", yt[b], g.hT[:, blk, sl], fw[:, blk:blk + 1], rstd, ALU.mult, ALU.mult,
                  [g.h_res[blk][tl], fres, rres], [ytres[b]])
            if FN_STOP == 4:
                continue
            ps2, pres2 = g.bank()
            for j in range(4):
                k.tr(ps2[:, j * 128:(j + 1) * 128], yt[b][:, j * 128:(j + 1) * 128], c.ident_f,
                     [ytres[b], c.res], [pres2], inc=(j == 3))
            k.cp("act", ost[:, :, blk * 128:(blk + 1) * 128], r3(ps2, 4), [pres2], [ores])
        tok = k.dma("sp", g.out[sl, :].rearrange("(c p) d -> p c d", p=128), ost, [ores], [])
        k.out_tokens.append(tok)
    ar.release(m)


_CACHE = {}


def _prep_inputs(inp, b):
    f = lambda a: np.ascontiguousarray(np.asarray(a, dtype=np.float32))
    m = {}
    m["x"] = f(inp["x"][b])
    m["ctx"] = f(inp["ctx"][b])
    m["cvec"] = f(np.stack([np.asarray(inp["c"])[b], np.asarray(inp["c_ctx"])], 0))
    for nm in ("mod_w", "mod_b", "norm1_w", "norm2_w", "ffn_w_up", "ffn_conv_b", "ffn_w_down",
               "final_norm_w"):
        m[nm] = f(inp[nm])
    m["ffn_conv_w"] = f(np.asarray(inp["ffn_conv_w"]).reshape(2, 9, FH))
    for nm in ("ssd_w_in", "ssd_conv_w", "ssd_conv_b", "ssd_d", "ssd_norm_w", "ssd_w_out",
               "conf_w_pw1", "conf_b_pw1", "conf_w_dw", "conf_b_dw", "conf_ln_w", "conf_ln_b",
               "conf_w_pw2", "conf_b_pw2"):
        m[nm] = f(np.asarray(inp[nm])[0])
    m["ssd_dt_bias"] = f(np.asarray(inp["ssd_dt_bias"])[0].reshape(64))
    m["ssd_a_log"] = f(np.asarray(inp["ssd_a_log"])[0].reshape(64))
    return m


def kernel(**inputs):
    if "nc" not in _CACHE:
        _CACHE["nc"] = build()
    nc = _CACHE["nc"]
    in_maps = [_prep_inputs(inputs, b) for b in range(8)]
    res = run_bass_kernel_spmd(nc, in_maps, core_ids=list(range(8)))
    return np.stack([np.asarray(r["out"], dtype=np.float32) for r in res.results], 0)
```

```python
import numpy as np
import concourse.bass as bass
import concourse.mybir as mybir
from concourse.bass_utils import run_bass_kernel_spmd

F32 = mybir.dt.float32
BF16 = mybir.dt.bfloat16
AF = mybir.ActivationFunctionType
ALU = mybir.AluOpType
AX = mybir.AxisListType

D = 1024
T = 2048
TC = 256
NB = D // 128
DI = 2048
NH = 32
NG = 8
NS = 128
CONVD = 4096
INDIM = 6208
FH = 2816
NFB = FH // 128
EPS = 1e-6
EPOCH = 30000
NSLOT = 8


class Res:
    __slots__ = ("name", "lw", "rd")

    def __init__(self, name):
        self.name = name
        self.lw = None
        self.rd = {}


class KB:
    ENG = ("sp", "pe", "dve", "act", "pool")

    def __init__(self):
        self.nc = bass.Bass("TRN2", target_bir_lowering=False)
        self.streams = {e: [] for e in self.ENG}
        self.cnt = {e: 0 for e in self.ENG}
        self.cursem = {}
        self.known = {e: {} for e in self.ENG}
        self.semkey = {}
        self.nsem = 0
        self.dma_i = {e: 0 for e in self.ENG}
        self.dma_slots = {}
        self.uid = 0
        self.out_tokens = []

    def newsem(self, name):
        s = self.nc.alloc_semaphore(f"{name}_{self.nsem}")
        self.nsem += 1
        self.semkey[id(s)] = s
        return s

    def sb(self, name, shape, dt):
        self.uid += 1
        return self.nc.alloc_sbuf_tensor(f"{name}_{self.uid}", list(shape), dt)

    def dram(self, name, shape, dt, kind="Internal"):
        return self.nc.dram_tensor(name, list(shape), dt, kind=kind)

    def _engsem(self, e):
        if e not in self.cursem:
            self.cursem[e] = self.newsem("e" + e)
        return self.cursem[e]

    def _wait(self, e, tok):
        sem, val = tok
        k = self.known[e]
        if k.get(id(sem), 0) >= val:
            return
        k[id(sem)] = val
        self.streams[e].append(("w", sem, val))

    def _deps(self, e, reads, writes):
        own = id(self._engsem(e))
        for r in reads:
            if r.lw is not None:
                if e == "pe" and id(r.lw[0]) == own:
                    continue
                self._wait(e, r.lw)
        for w in writes:
            if w.lw is not None and id(w.lw[0]) != own:
                self._wait(e, w.lw)
            for t in w.rd.values():
                if id(t[0]) != own:
                    self._wait(e, t)

    def _mark(self, tok, reads, writes):
        for r in reads:
            k = id(tok[0])
            if k not in r.rd or r.rd[k][1] < tok[1]:
                r.rd[k] = tok
        for w in writes:
            w.lw = tok
            w.rd = {}

    def op(self, e, fn, reads=(), writes=(), inc=True):
        self._deps(e, reads, writes)
        sem = self._engsem(e)
        tok = (sem, self.cnt[e] + 1)
        self.streams[e].append(("o", fn, sem if inc else None, 1))
        if inc:
            self.cnt[e] += 1
            if self.cnt[e] >= EPOCH:
                del self.cursem[e]
                self.cnt[e] = 0
        self._mark(tok, reads, writes)
        return tok

    def dma(self, q, out, in_, reads=(), writes=(), **kw):
        self._deps(q, reads, writes)
        if q not in self.dma_slots:
            self.dma_slots[q] = [self.newsem("d" + q) for _ in range(NSLOT)]
        i = self.dma_i[q]
        self.dma_i[q] += 1
        sem = self.dma_slots[q][i % NSLOT]
        prev = 16 * (i // NSLOT)
        if prev > 0:
            self._wait(q, (sem, prev))
        tok = (sem, prev + 16)
        self.streams[q].append(("o", lambda eng: eng.dma_start(out=out, in_=in_, **kw), sem, 16))
        self._mark(tok, reads, writes)
        return tok

    def finish(self):
        for tok in self.out_tokens:
            self._wait("sp", tok)
        nc = self.nc
        streams = self.streams
        with nc.Block() as block:
            def mk(stream):
                def body(eng):
                    for it in stream:
                        if it[0] == "w":
                            eng.wait_ge(it[1], it[2])
                        else:
                            ins = it[1](eng)
                            if it[2] is not None:
                                ins.then_inc(it[2], it[3])
                return body
            block.sync(mk(streams["sp"]))
            block.tensor(mk(streams["pe"]))
            block.vector(mk(streams["dve"]))
            block.scalar(mk(streams["act"]))
            block.gpsimd(mk(streams["pool"]))
        return nc

    def mm(self, out, lhsT, rhs, start, stop, reads, writes, inc=None):
        if inc is None:
            inc = True
        return self.op("pe", lambda e: e.matmul(out, lhsT, rhs, start=start, stop=stop),
                       reads, writes, inc=inc)

    def mmg(self, out, lhsT, rhs, start, stop, reads, writes):
        return self.mm(out, lhsT, rhs, start, stop, reads, writes, inc=bool(stop))

    def tr(self, out, in_, ident, reads, writes, inc=True):
        return self.op("pe", lambda e: e.transpose(out, in_, ident), reads, writes, inc=inc)

    def act(self, out, in_, func, reads, writes, bias=0.0, scale=1.0, eng="act", accum_out=None):
        if accum_out is None:
            return self.op("act", lambda e: e.activation(out, in_, func, bias=bias, scale=scale),
                           reads, writes)
        return self.op("act", lambda e: e.activation(out, in_, func, bias=bias, scale=scale,
                                                     accum_out=accum_out), reads, writes)

    def tt(self, eng, out, in0, in1, op, reads, writes):
        return self.op(eng, lambda e: e.tensor_tensor(out, in0, in1, op), reads, writes)

    def ts(self, eng, out, in0, s1, s2, op0, op1, reads, writes):
        if s2 is None:
            return self.op(eng, lambda e: e.tensor_scalar(out, in0, s1, None, op0), reads, writes)
        return self.op(eng, lambda e: e.tensor_scalar(out, in0, s1, s2, op0, op1), reads, writes)

    def stt(self, eng, out, in0, scalar, in1, op0, op1, reads, writes):
        return self.op(eng, lambda e: e.scalar_tensor_tensor(out, in0, scalar, in1, op0, op1),
                       reads, writes)

    def cp(self, eng, out, in_, reads, writes):
        if eng == "act":
            return self.op(eng, lambda e: e.copy(out, in_), reads, writes)
        return self.op(eng, lambda e: e.tensor_copy(out, in_), reads, writes)

    def memset(self, eng, ap, val, writes):
        return self.op(eng, lambda e: e.memset(ap, val), (), writes)


class Arena:
    def __init__(self, kb, words):
        self.t = kb.nc.alloc_sbuf_tensor("arena", [128, words], F32)
        self.words = words
        self.top = 0

    def mark(self):
        return self.top

    def release(self, m):
        self.top = m

    def _alloc(self, words):
        words = (words + 7) // 8 * 8
        off = self.top
        self.top += words
        assert self.top <= self.words, f"arena overflow {self.top} > {self.words}"
        return off

    def f32(self, n):
        off = self._alloc(n)
        return self.t[:, off:off + n]

    def bf(self, n):
        w = (n + 1) // 2
        off = self._alloc(w)
        return self.t[:, off:off + w].bitcast(BF16)[:, 0:n]


class Ctx:
    pass


def r3(ap, a):
    return ap.rearrange("p (a b) -> p a b", a=a)


def build(stage=99):
    k = KB()
    nc = k.nc
    g = Ctx()
    g.k = k
    def din(name, shape):
        return nc.dram_tensor(name, list(shape), F32, kind="ExternalInput").ap()
    I = {}
    I["x"] = din("x", [T, D])
    I["ctx"] = din("ctx", [TC, D])
    I["cvec"] = din("cvec", [2, D])
    I["mod_w"] = din("mod_w", [2, D, 6 * D])
    I["mod_b"] = din("mod_b", [2, 6 * D])
    I["norm1_w"] = din("norm1_w", [2, D])
    I["norm2_w"] = din("norm2_w", [2, D])
    I["ssd_w_in"] = din("ssd_w_in", [D, INDIM])
    I["ssd_conv_w"] = din("ssd_conv_w", [5, CONVD])
    I["ssd_conv_b"] = din("ssd_conv_b", [CONVD])
    I["ssd_dt_bias"] = din("ssd_dt_bias", [64])
    I["ssd_a_log"] = din("ssd_a_log", [64])
    I["ssd_d"] = din("ssd_d", [NH])
    I["ssd_norm_w"] = din("ssd_norm_w", [DI])
    I["ssd_w_out"] = din("ssd_w_out", [DI, D])
    I["conf_w_pw1"] = din("conf_w_pw1", [D, 2 * D])
    I["conf_b_pw1"] = din("conf_b_pw1", [2 * D])
    I["conf_w_dw"] = din("conf_w_dw", [31, D])
    I["conf_b_dw"] = din("conf_b_dw", [D])
    I["conf_ln_w"] = din("conf_ln_w", [D])
    I["conf_ln_b"] = din("conf_ln_b", [D])
    I["conf_w_pw2"] = din("conf_w_pw2", [D, D])
    I["conf_b_pw2"] = din("conf_b_pw2", [D])
    I["ffn_w_up"] = din("ffn_w_up", [2, D, 2 * FH])
    I["ffn_conv_w"] = din("ffn_conv_w", [2, 9, FH])
    I["ffn_conv_b"] = din("ffn_conv_b", [2, FH])
    I["ffn_w_down"] = din("ffn_w_down", [2, FH, D])
    I["final_norm_w"] = din("final_norm_w", [D])
    out = nc.dram_tensor("out", [T, D], F32, kind="ExternalOutput").ap()
    g.I = I
    g.out = out

    ar = Arena(k, 53100)
    g.ar = ar
    g.psum = nc.alloc_psum_tensor("psall", [128, 4096], F32)
    g.banks = []
    for i in range(8):
        g.banks.append((g.psum[:, i * 512:(i + 1) * 512], Res(f"psb{i}")))
    g.bank_i = 0

    def bank():
        b = g.banks[g.bank_i % 6]
        g.bank_i += 1
        return b[0], b[1]
    g.bank = bank

    c = Ctx()
    g.c = c
    c.res = Res("consts")
    c.ident_f = ar.f32(128)
    c.ones_f = ar.f32(128)
    c.ident_b = ar.bf(128)
    c.ones_b = ar.bf(128)
    k.memset("pool", c.ident_f, 0.0, [c.res])
    k.op("pool", lambda e: e.affine_select(out=c.ident_f, in_=c.ident_f, pattern=[[-1, 128]],
                                           compare_op=ALU.not_equal, fill=1.0, base=0,
                                           channel_multiplier=1), [c.res], [c.res])
    k.memset("pool", c.ones_f, 1.0, [c.res])
    k.cp("pool", c.ident_b, c.ident_f, [c.res], [c.res])
    k.cp("pool", c.ones_b, c.ones_f, [c.res], [c.res])
    g.vstage = [ar.f32(128), ar.f32(128)]
    g.vsres = [Res("vs0"), Res("vs1")]
    g.vs_i = 0
    g.wst = ar.f32(4096)
    g.wstres = Res("wst")
    g.wsth_res = [Res("wsth0"), Res("wsth1")]
    g.wbf = [ar.bf(4096), ar.bf(4096)]
    g.wbfres = [Res("wbf0"), Res("wbf1")]
    g.w_i = 0

    V = Ctx()
    g.V = V
    V.fnw, V.fnw_r = load_vec(g, I["final_norm_w"], NB)
    V.n1w, V.n1w_r = load_vec(g, I["norm1_w"].rearrange("a b -> (a b)"), 2 * NB)
    V.n2w, V.n2w_r = load_vec(g, I["norm2_w"].rearrange("a b -> (a b)"), 2 * NB)
    V.modb, V.modb_r = load_vec(g, I["mod_b"].rearrange("a b -> (a b)"), 96)
    V.cv, V.cv_r = load_vec(g, I["cvec"].rearrange("a b -> (a b)"), 16)
    V.fcb, V.fcb_r = load_vec(g, I["ffn_conv_b"].rearrange("a b -> (a b)"), 2 * NFB)
    V.fcw, V.fcw_r = load_vec(g, I["ffn_conv_w"].rearrange("a b c -> (a b c)"), 2 * 9 * NFB)
    V.scb, V.scb_r = load_vec(g, I["ssd_conv_b"], 32)
    V.scw, V.scw_r = load_vec(g, I["ssd_conv_w"].rearrange("a b -> (a b)"), 5 * 32)
    V.cb1, V.cb1_r = load_vec(g, I["conf_b_pw1"], 16)
    V.cbdw, V.cbdw_r = load_vec(g, I["conf_b_dw"], 8)
    V.clnw, V.clnw_r = load_vec(g, I["conf_ln_w"], 8)
    V.clnb, V.clnb_r = load_vec(g, I["conf_ln_b"], 8)
    V.cb2, V.cb2_r = load_vec(g, I["conf_b_pw2"], 8)
    V.cdw, V.cdw_r = load_vec(g, I["conf_w_dw"].rearrange("a b -> (a b)"), 31 * 8)
    V.cs = r3(ar.bf(16), 8)
    V.cs_r = Res("cs")
    tmpc = ar.f32(16)
    tmpc_r = Res("tmpc")
    k.act(tmpc, V.cv, AF.Silu, [V.cv_r], [tmpc_r])
    V.cs32 = r3(ar.f32(16), 8)
    for s_ in range(2):
        k.cp("dve", V.cs[:, :, s_], tmpc[:, s_ * 8:(s_ + 1) * 8], [tmpc_r], [V.cs_r])
        k.cp("dve", V.cs32[:, :, s_], tmpc[:, s_ * 8:(s_ + 1) * 8], [tmpc_r], [V.cs_r])
    V.maskL = ar.f32(512)
    V.maskR = ar.f32(512)
    V.mask_r = Res("masks")
    k.memset("pool", V.maskL, 1.0, [V.mask_r])
    k.memset("pool", V.maskR, 1.0, [V.mask_r])
    k.memset("pool", r3(V.maskL, 8)[:, :, 63:64], 0.0, [V.mask_r])
    k.memset("pool", r3(V.maskR, 8)[:, :, 0:1], 0.0, [V.mask_r])
    g.modT = [r3(ar.f32(96), 48), r3(ar.f32(96), 48)]
    g.A1 = [r3(ar.f32(16), 8), r3(ar.f32(16), 8)]
    g.A2 = [r3(ar.f32(16), 8), r3(ar.f32(16), 8)]
    g.mod_r = [Res("mod0"), Res("mod1")]
    g.modrow = ar.f32(256)
    g.modrow_r = Res("modrow")

    g.h_off = ar.top
    g.hT = r3(ar.f32(NB * T), NB)
    g.hcT = r3(ar.f32(NB * TC), NB)
    g.h_res = [[Res(f"h{b}_{t}") for t in range(T // 512)] for b in range(NB)]
    g.hc_res = [[Res(f"hc{b}")] for b in range(NB)]
    g.aT = r3(ar.bf(NB * T), NB)
    g.acT = r3(ar.bf(NB * TC), NB)
    g.a_res = [[Res(f"a{b}_{t}") for t in range(T // 512)] for b in range(NB)]
    g.ac_res = [[Res(f"ac{b}")] for b in range(NB)]
    g.pmark = ar.mark()

    for it in mod_params_items(g, 0):
        it()
    g.bg = mod_params_items(g, 1)
    if stage < 2 or not BG_INTERLEAVE:
        while g.bg:
            g.bg.pop(0)()
    barrier(g)
    load_stream(g, I["x"], g.hT, g.h_res, T)
    load_stream(g, I["ctx"], g.hcT, g.hc_res, TC)
    if stage >= 1:
        modulate(g, g.hT, g.h_res, T, g.A1[0], g.modT[0], 0, 0, g.aT, g.a_res)
        modulate(g, g.hcT, g.hc_res, TC, g.A1[0], g.modT[0], 0, 1, g.acT, g.ac_res)
        barrier(g)
        if stage >= 2:
            ssd_layer(g)
        while g.bg:
            g.bg.pop(0)()
        barrier(g)
    if stage >= 3:
        modulate(g, g.hT, g.h_res, T, g.A2[0], g.modT[0], 24, 0, g.aT, g.a_res)
        modulate(g, g.hcT, g.hc_res, TC, g.A2[0], g.modT[0], 24, 1, g.acT, g.ac_res)
        ffn(g, 0, [(g.aT, g.a_res, T, True, g.hT, g.h_res, 0), (g.acT, g.ac_res, TC, False, g.hcT, g.hc_res, 1)])
        barrier(g)
    if stage >= 4:
        modulate(g, g.hT, g.h_res, T, g.A1[1], g.modT[1], 0, 0, g.aT, g.a_res)
        conformer(g)
        barrier(g)
    if stage >= 5:
        modulate(g, g.hT, g.h_res, T, g.A2[1], g.modT[1], 24, 0, g.aT, g.a_res)
        ffn(g, 1, [(g.aT, g.a_res, T, True, g.hT, g.h_res, 0)])
        barrier(g)
    if stage == 99:
        final_norm(g)
    else:
        dbg = nc.dram_tensor("dbg", [128, NB * T], F32, kind="ExternalOutput").ap()
        dbgc = nc.dram_tensor("dbgc", [128, NB * TC], F32, kind="ExternalOutput").ap()
        k.out_tokens.append(k.dma("sp", dbg, g.hT.rearrange("p a b -> p (a b)"), all_res(g.h_res), []))
        k.out_tokens.append(k.dma("sp", dbgc, g.hcT.rearrange("p a b -> p (a b)"), all_res(g.hc_res), []))
    return k.finish()


def wload(g, src2d, kblks, col0, ncols, row0=0):
    k = g.k
    n = kblks * ncols
    assert n <= 4096
    st = r3(g.wst[:, 0:n], kblks)
    b = g.w_i % 2
    g.w_i += 1
    wb = r3(g.wbf[b][:, 0:n], kblks)
    src = src2d[row0:row0 + kblks * 128, col0:col0 + ncols].rearrange("(kb p) n -> p kb n", p=128)
    rs = [g.wstres, g.wsth_res[0], g.wsth_res[1]]
    k.dma(WQ, st, src, [], rs)
    k.cp("pool" if b == 0 else "act", wb, st, rs, [g.wbfres[b]])
    return wb, g.wbfres[b]


class WStream:
    def __init__(self, g, specs, bufs, bres, ahead):
        self.g, self.specs, self.bufs, self.bres, self.ahead = g, specs, bufs, bres, ahead
        self.loaded = {}
        self.nxt = 0

    def _load(self, i):
        g = self.g
        k = g.k
        src2d, kblks, col0, ncols, row0 = self.specs[i]
        n = kblks * ncols
        b = i % len(self.bufs)
        if n <= 2048:
            h = g.w_i % 2
            stf, stres = g.wst[:, h * 2048:h * 2048 + n], g.wsth_res[h]
        else:
            stf, stres = g.wst[:, 0:n], g.wstres
        g.w_i += 1
        st = r3(stf, kblks)
        wb = r3(self.bufs[b][:, 0:n], kblks)
        src = src2d[row0:row0 + kblks * 128, col0:col0 + ncols].rearrange("(kb p) n -> p kb n", p=128)
        rs = [stres] if n <= 2048 else [g.wstres, g.wsth_res[0], g.wsth_res[1]]
        k.dma("sp", st, src, [], rs)
        k.cp("pool", wb, st, rs, [self.bres[b]])
        self.loaded[i] = (wb, self.bres[b])

    def get(self, i):
        while self.nxt <= min(i + self.ahead, len(self.specs) - 1):
            self._load(self.nxt)
            self.nxt += 1
        return self.loaded.pop(i)


def mod_params_items(g, i):
    k, ar, V = g.k, g.ar, g.V
    mr = g.mod_r[i]
    items = []

    loaded = {}

    def loader(cg):
        def run():
            h = cg % 2
            st = r3(g.wst[:, h * 2048:(h + 1) * 2048], 8)
            src = g.I["mod_w"][i][:, cg * 256:(cg + 1) * 256].rearrange("(kb p) n -> p kb n", p=128)
            k.dma("sp", st, src, [], [g.wsth_res[h]])
            loaded[cg] = (st, g.wsth_res[h])
        return run

    def compute(cg):
        def run():
            psb, pres = g.banks[7]
            w, wres = loaded.pop(cg)
            for kb in range(8):
                k.mmg(psb[0:2, 0:256], V.cs32[:, kb, :], w[:, kb, :], kb == 0, kb == 7, [wres, V.cs_r], [pres])
            k.cp("dve", g.modrow[0:2, :], psb[0:2, 0:256], [pres], [g.modrow_r])
            for j in range(2):
                k.tr(psb[:, 256 + j * 2:256 + (j + 1) * 2], g.modrow[0:2, j * 128:(j + 1) * 128],
                     g.c.ident_f[0:2, 0:2], [g.modrow_r, g.c.res], [pres])
            m0 = cg * 2
            k.tt("dve", g.modT[i][:, m0:m0 + 2, :], r3(psb[:, 256:260], 2),
                 V.modb[:, i * 48 + m0:i * 48 + m0 + 2].unsqueeze(2).broadcast_to([128, 2, 2]), ALU.add,
                 [pres, V.modb_r], [mr])
        return run

    def both(cg):
        def run():
            if cg not in loaded:
                loader(cg)()
            if cg + 1 < 24:
                loader(cg + 1)()
            compute(cg)()
        return run
    for cg in range(24):
        items.append(both(cg))

    def fin():
        k.stt("dve", g.A1[i], g.modT[i][:, 8:16, :], 1.0,
              V.n1w[:, i * 8:(i + 1) * 8].unsqueeze(2).broadcast_to([128, 8, 2]), ALU.add, ALU.mult,
              [mr, V.n1w_r], [mr])
        k.stt("dve", g.A2[i], g.modT[i][:, 32:40, :], 1.0,
              V.n2w[:, i * 8:(i + 1) * 8].unsqueeze(2).broadcast_to([128, 8, 2]), ALU.add, ALU.mult,
              [mr, V.n2w_r], [mr])
    items.append(fin)
    return items


def modulate(g, srcT, sres, L, A, modT, sh0, s_, dstT, dres, mres=None):
    k, ar, c = g.k, g.ar, g.c
    mres = g.mod_r[0] if modT is g.modT[0] else g.mod_r[1]
    m = ar.mark()
    tw = min(512, L)
    sq = [ar.bf(tw), ar.bf(tw)]
    sqres = [Res("sq0"), Res("sq1")]
    rstd = ar.f32(tw)
    rres = Res("rstd")
    tmp = [ar.f32(tw), ar.f32(tw)]
    tres = [Res("t0"), Res("t1")]
    for tl in range(L // tw):
        sl = slice(tl * tw, (tl + 1) * tw)
        ps, pres = g.bank()
        ps = ps[:, 0:tw]
        for blk in range(NB):
            b = blk % 2
            k.act(sq[b], srcT[:, blk, sl], AF.Square, [sres[blk][tl]], [sqres[b]])
            k.mm(ps, c.ones_b, sq[b], blk == 0, blk == NB - 1, [sqres[b], c.res], [pres])
        k.ts("dve", rstd, ps, 1.0 / D, EPS, ALU.mult, ALU.add, [pres], [rres])
        k.act(rstd, rstd, AF.Sqrt, [rres], [rres])
        k.op("dve", lambda e: e.reciprocal(rstd, rstd), [rres], [rres])
        for blk in range(NB):
            b = blk % 2
            k.tt("dve", tmp[b], srcT[:, blk, sl], rstd, ALU.mult, [sres[blk][tl], rres], [tres[b]])
            k.act(dstT[:, blk, sl], tmp[b], AF.Identity, [tres[b], mres], [dres[blk][tl]],
                  bias=modT[:, sh0 + blk, s_:s_ + 1], scale=A[:, blk, s_:s_ + 1])
    barrier(g)
    ar.release(m)


def ffn(g, layer, streams):
    k, ar, c, V = g.k, g.ar, g.c, g.V
    m = ar.mark()
    HL = 66
    SS = []
    for (aT, a_res, L, grid, hT, h_res, s_) in streams:
        q = Ctx()
        q.aT, q.a_res, q.L, q.grid, q.hT, q.h_res, q.s_ = aT, a_res, L, grid, hT, h_res, s_
        q.tw = min(512, L)
        q.nt = L // q.tw
        W = HL + L + HL
        q.gpre = ar.bf(W)
        q.gL = ar.bf(W) if grid else None
        q.gR = ar.bf(W) if grid else None
        q.gres = Res("gpre")
        k.memset("pool", q.gpre, 0.0, [q.gres])
        if grid:
            k.memset("pool", q.gL, 0.0, [q.gres])
            k.memset("pool", q.gR, 0.0, [q.gres])
        q.hid = [ar.bf(L), ar.bf(L)]
        q.hres = [Res("hid0"), Res("hid1")]
        q.taps = [(ky, kx) for ky in range(3) for kx in range(3)] if grid else [(1, kx) for kx in range(3)]
        SS.append(q)
    sg = [ar.f32(512), ar.f32(512)]
    sgres = [Res("sg0"), Res("sg1")]
    sgi = 0
    diag = [ar.bf(128) for _ in range(9)]
    dres = Res("diag")
    wup = g.I["ffn_w_up"][layer]
    wdn = g.I["ffn_w_down"][layer]
    specs = []
    for fp_ in range(NFB // 2):
        specs.append((wup, 8, fp_ * 256, 256, 0))
        specs.append((wup, 8, FH + fp_ * 256, 256, 0))
        specs.append((wdn, 2, 0, D, fp_ * 256))
    wsm = WStream(g, specs, [ar.bf(2048) for _ in range(6)], [Res(f"fw{i_}") for i_ in range(6)], 3)
    g2 = g.modT[layer]
    mres = g.mod_r[layer]
    for fp in range(NFB // 2):
        wv, wvres = wsm.get(fp * 3)
        wg, wgres = wsm.get(fp * 3 + 1)
        for fi in range(2):
            f = fp * 2 + fi
            for q in SS:
                tw = q.tw
                for tl in range(q.nt):
                    ps, pres = g.bank()
                    ps = ps[:, 0:tw]
                    for kb in range(NB):
                        k.mmg(ps, wg[:, kb, fi * 128:(fi + 1) * 128], q.aT[:, kb, tl * tw:(tl + 1) * tw],
                              kb == 0, kb == NB - 1, [wgres, q.a_res[kb][tl]], [pres])
                    dsl = slice(HL + tl * tw, HL + (tl + 1) * tw)
                    k.cp("act", q.gpre[:, dsl], ps, [pres], [q.gres])
                    if q.grid:
                        k.tt("dve", q.gL[:, dsl], ps, V.maskL, ALU.mult, [pres, V.mask_r], [q.gres])
                        k.tt("dve", q.gR[:, dsl], ps, V.maskR, ALU.mult, [pres, V.mask_r], [q.gres])
            for ti in range(9):
                col = layer * 9 * NFB + ti * NFB + f
                k.ts("dve", diag[ti], c.ident_b, V.fcw[:, col:col + 1], None, ALU.mult, None,
                     [c.res, V.fcw_r], [dres])
            for q in SS:
                tw = q.tw
                for tl in range(q.nt):
                    ps, pres = g.bank()
                    ps = ps[:, 0:tw]
                    for ti, (ky, kx) in enumerate(q.taps):
                        srcb = q.gpre if (not q.grid or kx == 1) else (q.gL if kx == 0 else q.gR)
                        off = HL + tl * tw + ((ky - 1) * 64 if q.grid else 0) + (kx - 1)
                        k.mmg(ps, diag[ky * 3 + kx], srcb[:, off:off + tw], ti == 0, ti == len(q.taps) - 1,
                              [dres, q.gres], [pres])
                    b = sgi % 2
                    sgi += 1
                    cbc = layer * NFB + f
                    k.act(sg[b][:, 0:tw], ps, AF.Silu, [pres, V.fcb_r], [sgres[b]], bias=V.fcb[:, cbc:cbc + 1])
                    ps2, pres2 = g.bank()
                    ps2 = ps2[:, 0:tw]
                    for kb in range(NB):
                        k.mmg(ps2, wv[:, kb, fi * 128:(fi + 1) * 128], q.aT[:, kb, tl * tw:(tl + 1) * tw],
                              kb == 0, kb == NB - 1, [wvres, q.a_res[kb][tl]], [pres2])
                    k.tt("dve", q.hid[fi][:, tl * tw:(tl + 1) * tw], ps2, sg[b][:, 0:tw], ALU.mult,
                         [pres2, sgres[b]], [q.hres[fi]])
        wd, wdres = wsm.get(fp * 3 + 2)
        for q in SS:
            tw = q.tw
            for db in range(NB):
                for tl in range(q.nt):
                    ps, pres = g.bank()
                    ps = ps[:, 0:tw]
                    for fi in range(2):
                        k.mmg(ps, wd[:, fi, db * 128:(db + 1) * 128], q.hid[fi][:, tl * tw:(tl + 1) * tw],
                              fi == 0, fi == 1, [wdres, q.hres[fi]], [pres])
                    hsl = q.hT[:, db, tl * tw:(tl + 1) * tw]
                    k.stt("dve", hsl, ps, g2[:, 40 + db, q.s_:q.s_ + 1], hsl, ALU.mult, ALU.add,
                          [pres, mres, q.h_res[db][tl]], [q.h_res[db][tl]])
    barrier(g)
    ar.release(m)


def conformer(g):
    k, ar, c, V = g.k, g.ar, g.c, g.V
    m = ar.mark()
    HL = 16
    W = HL + T + HL
    glu = [ar.bf(W) for _ in range(NB)]
    glu_r = [Res(f"glu{i}") for i in range(NB)]
    sgm = [ar.f32(512), ar.f32(512)]
    sgm_r = [Res("sgm0"), Res("sgm1")]
    w1 = g.I["conf_w_pw1"]
    nt = T // 512
    specs = []
    for q4 in range(2):
        specs.append((w1, 8, q4 * 512, 512, 0))
        specs.append((w1, 8, D + q4 * 512, 512, 0))
    ws1 = WStream(g, specs, g.wbf, g.wbfres, 0)
    w1cur = {}
    for cb in range(NB):
        k.memset("pool", glu[cb], 0.0, [glu_r[cb]])
        if cb % 4 == 0:
            w1cur["a"] = ws1.get((cb // 4) * 2)
            w1cur["g"] = ws1.get((cb // 4) * 2 + 1)
        wa, wares = w1cur["a"][0][:, :, (cb % 4) * 128:(cb % 4 + 1) * 128], w1cur["a"][1]
        wgt, wgres = w1cur["g"][0][:, :, (cb % 4) * 128:(cb % 4 + 1) * 128], w1cur["g"][1]
        for tl in range(nt):
            sl = slice(tl * 512, (tl + 1) * 512)
            psg, presg = g.bank()
            for kb in range(NB):
                k.mmg(psg, wgt[:, kb, :], g.aT[:, kb, sl], kb == 0, kb == NB - 1,
                     [wgres, g.a_res[kb][tl]], [presg])
            b = tl % 2
            k.act(sgm[b], psg, AF.Sigmoid, [presg, V.cb1_r], [sgm_r[b]], bias=V.cb1[:, 8 + cb:9 + cb])
            psa, presa = g.bank()
            for kb in range(NB):
                k.mmg(psa, wa[:, kb, :], g.aT[:, kb, sl], kb == 0, kb == NB - 1,
                     [wares, g.a_res[kb][tl]], [presa])
            k.stt("dve", glu[cb][:, HL + tl * 512:HL + (tl + 1) * 512], psa, V.cb1[:, cb:cb + 1], sgm[b],
                  ALU.add, ALU.mult, [presa, V.cb1_r, sgm_r[b]], [glu_r[cb]])
    barrier(g)
    cv = g.aT
    cv_r = g.a_res
    diag = [ar.bf(128) for _ in range(31)]
    dres = Res("cdiag")
    for cb in range(NB):
        for tp in range(31):
            col = tp * 8 + cb
            k.ts("dve", diag[tp], c.ident_b, V.cdw[:, col:col + 1], None, ALU.mult, None,
                 [c.res, V.cdw_r], [dres])
        for tl in range(nt):
            ps, pres = g.bank()
            for tp in range(31):
                off = HL + tl * 512 + tp - 15
                k.mmg(ps, diag[tp], glu[cb][:, off:off + 512], tp == 0, tp == 30, [dres, glu_r[cb]], [pres])
            k.act(cv[:, cb, tl * 512:(tl + 1) * 512], ps, AF.Identity, [pres, V.cbdw_r], [cv_r[cb][tl]],
                  bias=V.cbdw[:, cb:cb + 1])
    barrier(g)
    sq = [ar.bf(512), ar.bf(512)]
    sq_r = [Res("csq0"), Res("csq1")]
    mean = ar.f32(512)
    rstd = ar.f32(512)
    nmr = ar.f32(512)
    st_r = Res("lnstat")
    t1 = sgm
    t1_r = [Res("lt0"), Res("lt1")]
    hln = glu
    for tl in range(nt):
        sl = slice(tl * 512, (tl + 1) * 512)
        ps1, pres1 = g.bank()
        ps2, pres2 = g.bank()
        for cb in range(NB):
            b = cb % 2
            k.mmg(ps1, c.ones_b, cv[:, cb, sl], cb == 0, cb == NB - 1, [c.res, cv_r[cb][tl]], [pres1])
            k.tt("dve", sq[b], cv[:, cb, sl], cv[:, cb, sl], ALU.mult, [cv_r[cb][tl]], [sq_r[b]])
            k.mm(ps2, c.ones_b, sq[b], cb == 0, cb == NB - 1, [c.res, sq_r[b]], [pres2])
        k.ts("dve", mean, ps1, 1.0 / D, None, ALU.mult, None, [pres1], [st_r])
        k.tt("dve", nmr, mean, mean, ALU.mult, [st_r], [st_r])
        k.stt("dve", rstd, ps2, 1.0 / D, nmr, ALU.mult, ALU.subtract, [pres2, st_r], [st_r])
        k.ts("dve", rstd, rstd, EPS, None, ALU.add, None, [st_r], [st_r])
        k.act(rstd, rstd, AF.Sqrt, [st_r], [st_r])
        k.op("dve", lambda e: e.reciprocal(rstd, rstd), [st_r], [st_r])
        k.stt("dve", nmr, mean, -1.0, rstd, ALU.mult, ALU.mult, [st_r], [st_r])
        for cb in range(NB):
            b = cb % 2
            k.tt("dve", t1[b], cv[:, cb, sl], rstd, ALU.mult, [cv_r[cb][tl], st_r], [t1_r[b]])
            k.tt("dve", t1[b], t1[b], nmr, ALU.add, [t1_r[b], st_r], [t1_r[b]])
            k.act(hln[cb][:, sl], t1[b], AF.Silu, [t1_r[b], V.clnw_r, V.clnb_r], [glu_r[cb]],
                  bias=V.clnb[:, cb:cb + 1], scale=V.clnw[:, cb:cb + 1])
    barrier(g)
    w2 = g.I["conf_w_pw2"]
    ws2 = WStream(g, [(w2, 8, q4 * 512, 512, 0) for q4 in range(2)], g.wbf, g.wbfres, 1)
    w2cur = {}
    g1 = g.modT[1]
    mres = g.mod_r[1]
    yb = sgm
    yb_r = [Res("yb0"), Res("yb1")]
    for db in range(NB):
        if db % 4 == 0:
            w2cur["w"] = ws2.get(db // 4)
        wp, wpres = w2cur["w"][0][:, :, (db % 4) * 128:(db % 4 + 1) * 128], w2cur["w"][1]
        for tl in range(nt):
            sl = slice(tl * 512, (tl + 1) * 512)
            ps, pres = g.bank()
            for cb in range(NB):
                k.mmg(ps, wp[:, cb, :], hln[cb][:, sl], cb == 0, cb == NB - 1, [wpres, glu_r[cb]], [pres])
            b = tl % 2
            k.ts("dve", yb[b], ps, V.cb2[:, db:db + 1], g1[:, 16 + db, 0:1], ALU.add, ALU.mult,
                 [pres, V.cb2_r, mres], [yb_r[b]])
            k.tt("dve", g.hT[:, db, sl], g.hT[:, db, sl], yb[b], ALU.add,
                 [yb_r[b], g.h_res[db][tl]], [g.h_res[db][tl]])
    barrier(g)
    ar.release(m)


def psbf(ps):
    return ps.bitcast(BF16)


def ssd_layer(g):
    k, ar, c, V, nc = g.k, g.ar, g.c, g.V, g.k.nc
    top_save = ar.top
    ar.top = g.h_off
    S = Ctx()
    g.S = S
    S.cres = Res("ssdc")

    tmpf = None

    def tri(dst_b, fill, pattern, cm, base=0, view=None, init=1.0):
        src = tmpf[:, 0:dst_b.shape[1]]
        k.memset("pool", src, init, [S.cres])
        vv = src if view is None else view(src)
        k.op("pool", lambda e: e.affine_select(out=vv, in_=vv, pattern=pattern, compare_op=ALU.is_ge,
                                               fill=fill, base=base, channel_multiplier=cm),
             [S.cres], [S.cres])
        k.cp("pool", dst_b, src, [S.cres], [S.cres])
    S.U = [ar.bf(128), ar.bf(128)]
    S.Vm = [ar.bf(128), ar.bf(128)]
    S.NEG = [ar.bf(512), ar.bf(512)]
    S.dtb = ar.f32(64)
    S.Abc = ar.f32(64)
    S.Dbc = ar.f32(32)
    S.nwbc = ar.f32(DI)
    k.dma("sp", S.dtb, g.I["ssd_dt_bias"].partition_broadcast(128), [], [S.cres])
    k.dma("sp", S.Abc, g.I["ssd_a_log"].partition_broadcast(128), [], [S.cres])
    k.dma("sp", S.Dbc, g.I["ssd_d"].partition_broadcast(128), [], [S.cres])
    k.dma("sp", S.nwbc, g.I["ssd_norm_w"].partition_broadcast(128), [], [S.cres])
    k.act(S.Abc, S.Abc, AF.Exp, [S.cres], [S.cres])
    k.ts("dve", S.Abc, S.Abc, -1.0, None, ALU.mult, None, [S.cres], [S.cres])
    NCH = (TC + T) // 128
    S.dt = r3(ar.f32(NCH * 64), NCH)
    S.dthi = r3(ar.bf(NCH * 64), NCH)
    S.dtlo = r3(ar.bf(NCH * 64), NCH)
    S.ndthi = r3(ar.bf(NCH * 64), NCH)
    S.ndtlo = r3(ar.bf(NCH * 64), NCH)
    S.dt_r = Res("dtall")
    S.St = [ar.f32(DI), ar.f32(DI)]
    S.St_r = [[Res(f"S{d}_{q}") for q in range(4)] for d in range(2)]
    for d_ in range(2):
        k.memset("pool", S.St[d_], 0.0, S.St_r[d_])
    barrier(g)
    base_mark = ar.mark()
    tmpf = ar.f32(512)
    tri(S.U[0], 0.0, [[1, 128]], -1)
    tri(S.U[1], 0.0, [[-1, 128]], 1)
    tri(S.Vm[0], 0.0, [[-1, 128]], 1, base=-1)
    tri(S.Vm[1], 0.0, [[1, 128]], -1, base=-1)
    tri(S.NEG[0], -30000.0, [[0, 4], [1, 128]], -1, view=lambda a: r3(a, 4), init=0.0)
    tri(S.NEG[1], -30000.0, [[0, 4], [-1, 128]], 1, view=lambda a: r3(a, 4), init=0.0)
    barrier(g)
    ar.release(base_mark)
    seqs = []
    for nm, L, aT, a_res, ch0 in (("c", TC, g.acT, g.ac_res, 0), ("l", T, g.aT, g.a_res, TC // 128)):
        q_ = Ctx()
        q_.L, q_.aT, q_.a_res, q_.ch0, q_.nm = L, aT, a_res, ch0, nm
        q_.ZS = nc.dram_tensor("ZS" + nm, [L, DI], F32, kind="Internal").ap()
        q_.YP = nc.dram_tensor("YP" + nm, [L, DI], F32, kind="Internal").ap()
        q_.Y1 = nc.dram_tensor("Y1" + nm, [L, D], F32, kind="Internal").ap()
        q_.XT = nc.dram_tensor("XT" + nm, [L, DI], BF16, kind="Internal").ap()
        q_.BK = nc.dram_tensor("BK" + nm, [L, 1024], BF16, kind="Internal").ap()
        q_.BTd = nc.dram_tensor("BT" + nm, [L // 128, 128, 1024], BF16, kind="Internal").ap()
        q_.CTd = nc.dram_tensor("CT" + nm, [L // 128, 128, 1024], BF16, kind="Internal").ap()
        q_.dres = Res("dram" + nm)
        seqs.append(q_)
    ar.top = top_save
    ssd_phaseA(g, seqs)
    barrier(g)
    ar.release(base_mark)
    ssd_sweeps(g, seqs)
    barrier(g)
    ar.top = top_save
    S.dres_all = Res('dres_all')
    load_stream(g, g.I["x"], g.hT, g.h_res, T, ysrc=seqs[1].Y1, gate=(g.modT[0], 16, 0, g.mod_r[0]))
    load_stream(g, g.I["ctx"], g.hcT, g.hc_res, TC, ysrc=seqs[0].Y1, gate=(g.modT[0], 16, 1, g.mod_r[0]))


def ssd_phaseA(g, seqs):
    k, ar, c, V, S = g.k, g.ar, g.c, g.V, g.S
    win = g.I["ssd_w_in"]
    tmp = [ar.f32(64), ar.f32(64)]
    tmp_r = [Res("dtt0"), Res("dtt1")]
    zs = [ar.f32(512), ar.f32(512)]
    zs_r = [Res("zs0"), Res("zs1")]
    diags = [[ar.bf(128) for _ in range(5)] for _ in range(2)]
    dg_rs = [Res("sdiag0"), Res("sdiag1")]
    for q_ in seqs:
        L = q_.L
        q_.tw = min(512, L)
        q_.nt = L // q_.tw
        q_.nch = L // 128
        q_.pres_ = [ar.bf(L + 4), ar.bf(L + 4)]
        q_.pre_rs = [Res("pre0"), Res("pre1")]
        q_.xcs = [ar.bf(L), ar.bf(L)]
        q_.xc_rs = [Res("xc0"), Res("xc1")]
        q_.tokTs = [r3(ar.bf(q_.nch * 128), q_.nch), r3(ar.bf(q_.nch * 128), q_.nch)]
        q_.tokT_rs = [Res("tokT0"), Res("tokT1")]
        for b in range(2):
            k.memset("pool", q_.pres_[b], 0.0, [q_.pre_rs[b]])
    specs = [(win, 8, DI + CONVD, 64, 0)] + [(win, 8, cg_ * 512, 512, 0) for cg_ in range(4)] + \
            [(win, 8, DI + f_ * 512, 512, 0) for f_ in range(8)]
    wsa = WStream(g, specs, g.wbf, g.wbfres, 1)
    wdt, wdtres = wsa.get(0)
    ti = 0
    for q_ in seqs:
        aT, a_res, tw = q_.aT, q_.a_res, q_.tw
        for ch in range(q_.nch):
            gch = q_.ch0 + ch
            ps, pres = g.bank()
            ps = ps[:, 0:64]
            tl = (ch * 128) // tw
            for kb in range(NB):
                k.mmg(ps, aT[:, kb, ch * 128:(ch + 1) * 128], wdt[:, kb, :], kb == 0, kb == NB - 1,
                      [wdtres, a_res[kb][tl]], [pres])
            b = ti % 2
            ti += 1
            k.tt("dve", tmp[b], ps, S.dtb, ALU.add, [pres, S.cres], [tmp_r[b]])
            k.act(S.dt[:, gch, :], tmp[b], AF.Softplus, [tmp_r[b]], [S.dt_r])
            k.tt("dve", tmp[b], S.dt[:, gch, :], S.Abc, ALU.mult, [S.dt_r, S.cres], [tmp_r[b]])
            k.cp("dve", S.dthi[:, gch, :], tmp[b], [tmp_r[b]], [S.dt_r])
            k.tt("dve", S.dtlo[:, gch, :], tmp[b], S.dthi[:, gch, :], ALU.subtract, [tmp_r[b], S.dt_r], [S.dt_r])
            k.ts("pool", S.ndthi[:, gch, :], S.dthi[:, gch, :], -1.0, None, ALU.mult, None, [S.dt_r], [S.dt_r])
            k.ts("pool", S.ndtlo[:, gch, :], S.dtlo[:, gch, :], -1.0, None, ALU.mult, None, [S.dt_r], [S.dt_r])
    zi = 0
    for cg in range(4):
        wz, wzres = wsa.get(1 + cg)
        for q_ in seqs:
            aT, a_res, tw = q_.aT, q_.a_res, q_.tw
            for ch in range(q_.nch):
                tl = (ch * 128) // tw
                ps, pres = g.bank()
                for kb in range(NB):
                    k.mmg(ps, aT[:, kb, ch * 128:(ch + 1) * 128], wz[:, kb, :], kb == 0, kb == NB - 1,
                          [wzres, a_res[kb][tl]], [pres])
                b = zi % 2
                zi += 1
                k.act(zs[b], ps, AF.Silu, [pres], [zs_r[b]])
                k.dma("sp", q_.ZS[ch * 128:(ch + 1) * 128, cg * 512:(cg + 1) * 512], zs[b], [zs_r[b]], [])
    for fbg in range(8):
        w, wres = wsa.get(5 + fbg)
        for fi in range(4):
            fb = fbg * 4 + fi
            pb_ = fb % 2
            diag, dg_r = diags[pb_], dg_rs[pb_]
            for tp in range(5):
                col = tp * 32 + fb
                k.ts("dve", diag[tp], c.ident_b, V.scw[:, col:col + 1], None, ALU.mult, None,
                     [c.res, V.scw_r], [dg_r])
            for q_ in seqs:
                aT, a_res, tw, nt, nch = q_.aT, q_.a_res, q_.tw, q_.nt, q_.nch
                pre, pre_r, xc, xc_r = q_.pres_[pb_], q_.pre_rs[pb_], q_.xcs[pb_], q_.xc_rs[pb_]
                tokT, tokT_r = q_.tokTs[pb_], q_.tokT_rs[pb_]
                for tl in range(nt):
                    ps, pres = g.bank()
                    ps = ps[:, 0:tw]
                    for kb in range(NB):
                        k.mmg(ps, w[:, kb, fi * 128:(fi + 1) * 128], aT[:, kb, tl * tw:(tl + 1) * tw],
                              kb == 0, kb == NB - 1, [wres, a_res[kb][tl]], [pres])
                    k.cp("act", pre[:, 2 + tl * tw:2 + (tl + 1) * tw], ps, [pres], [pre_r])
                for tl in range(nt):
                    ps, pres = g.bank()
                    ps = ps[:, 0:tw]
                    for tp in range(5):
                        k.mmg(ps, diag[tp], pre[:, tl * tw + tp:tl * tw + tp + tw], tp == 0, tp == 4,
                              [dg_r, pre_r], [pres])
                    k.act(xc[:, tl * tw:(tl + 1) * tw], ps, AF.Silu, [pres, V.scb_r], [xc_r],
                          bias=V.scb[:, fb:fb + 1])
                if fb < 24:
                    n4 = 2 if nch < 4 else 4
                    for c4 in range(nch // n4):
                        ps, pres = g.bank()
                        pb = psbf(ps)
                        for j in range(n4):
                            ch = c4 * n4 + j
                            k.tr(pb[:, j * 128:(j + 1) * 128], xc[:, ch * 128:(ch + 1) * 128], c.ident_b,
                                 [xc_r, c.res], [pres])
                        k.cp("dve", tokT[:, c4 * n4:(c4 + 1) * n4, :], r3(pb[:, 0:n4 * 128], n4), [pres],
                             [tokT_r])
                    if fb < 16:
                        dst = q_.XT.rearrange("(c p) f -> p c f", p=128)[:, :, fb * 128:(fb + 1) * 128]
                    else:
                        dst = q_.BK.rearrange("(c p) f -> p c f", p=128)[:, :, (fb - 16) * 128:(fb - 15) * 128]
                    k.dma("sp", dst, tokT, [tokT_r], [])
                if fb >= 16:
                    gi = (fb - 16) % 8
                    dd = q_.BTd if fb < 24 else q_.CTd
                    dst = dd.rearrange("c n (g t) -> n c g t", g=8)[:, :, gi, :]
                    k.dma("sp", dst, r3(xc, nch), [xc_r], [])


PIPE = True
WQ = "sp"
BG_INTERLEAVE = True


def ssd_sweeps(g, seqs):
    k, ar, c, V, S = g.k, g.ar, g.c, g.V, g.S
    wout = r3(ar.bf(16 * D), 16)
    wout_r = Res("wout")
    for qc in range(4):
        st = r3(g.wst[:, 0:4096], 16)
        rs = [g.wstres, g.wsth_res[0], g.wsth_res[1]]
        k.dma("sp", st, g.I["ssd_w_out"][:, qc * 256:(qc + 1) * 256].rearrange("(kb p) n -> p kb n", p=128),
              [], rs)
        k.cp("pool", wout[:, :, qc * 256:(qc + 1) * 256], st, rs, [wout_r])
    Sbf = ar.bf(DI)
    Sbf_r = [Res(f"Sbf{q}") for q in range(8)]
    St_r = [[Res(f"St{d}_{q}") for q in range(8)] for d in range(2)]
    LB = []
    for P in range(3):
        b = Ctx()
        b.xtok = ar.bf(DI); b.btok = ar.bf(1024)
        b.BT = r3(ar.bf(1024), 8); b.CT = r3(ar.bf(1024), 8)
        b.x_r = Res(f"inx{P}"); b.b_r = Res(f"inb{P}"); b.BT_r = Res(f"inBT{P}"); b.CT_r = Res(f"inCT{P}")
        LB.append(b)
    CH = []
    for P in range(2):
        b = Ctx()
        b.cbT = r3(ar.bf(1024), 8); b.cb_r = Res(f"cbT{P}")
        b.eall = ar.f32(96); b.dtdec = ar.f32(32); b.e_r = Res(f"eall{P}")
        b.xdt = ar.bf(DI); b.xdd = ar.bf(DI); b.xdt_r = Res(f"xdt{P}"); b.xdd_r = Res(f"xdd{P}")
        CH.append(b)
    UN = []
    for P in range(3):
        b = Ctx()
        b.E = ar.bf(512); b.E_r = Res(f"E{P}")
        b.MT = ar.bf(512); b.MT_r = Res(f"MT{P}")
        UN.append(b)
    UY = []
    for P in range(2):
        b = Ctx()
        b.ytmp = ar.f32(256); b.ytmp_r = Res(f"ytmp{P}")
        b.ytmp2 = ar.f32(256); b.ytmp2_r = Res(f"ytmpb{P}")
        b.ydir = ar.f32(256); b.ydir_r = Res(f"ydir{P}")
        UY.append(b)
    UL = []
    for P in range(3):
        b = Ctx()
        b.yp = ar.f32(256); b.yp_r = Res(f"ypt{P}")
        b.zs = ar.f32(256); b.zs_r = Res(f"zst{P}")
        UL.append(b)
    yg = ar.f32(DI); yg_r = Res("yg")
    yn = ar.bf(DI); yn_r = Res("yn")
    ssq = ar.f32(16); ssq_r = Res("ssq")
    ynT = r3(ar.bf(DI), 16); ynT_r = Res("ynT")
    B = g.banks
    seg_b = [B[0], B[1], B[2]]
    y_b = [B[3], B[4]]
    os_b = [B[5], B[6]]
    pro_b = [B[7], B[7]]

    def loads(q_, ci, ch, d):
        Lb = LB[ci % 3]
        rows = slice(ch * 128, (ch + 1) * 128)
        k.dma("sp", Lb.BT.rearrange("p a b -> p (a b)"), q_.BTd[ch], [], [Lb.BT_r])
        k.dma("sp", Lb.CT.rearrange("p a b -> p (a b)"), q_.CTd[ch], [], [Lb.CT_r])
        k.dma("sp", Lb.xtok, q_.XT[rows, :], [], [Lb.x_r])
        k.dma("sp", Lb.btok, q_.BK[rows, :], [], [Lb.b_r])

    def prologue(q_, ci, ch, d):
        P = CH[ci % 2]
        Lb = LB[ci % 3]
        gch = q_.ch0 + ch
        dsl = slice(d * 32, (d + 1) * 32)
        for half in range(2):
            ps, pres = pro_b[0]
            for gl in range(4):
                gg = half * 4 + gl
                k.mmg(ps[:, gl * 128:(gl + 1) * 128], Lb.BT[:, gg, :], Lb.CT[:, gg, :], gl == 0, gl == 3,
                      [Lb.BT_r, Lb.CT_r], [pres])
            k.cp("act", P.cbT[:, half * 4:(half + 1) * 4, :], r3(ps, 4), [pres], [P.cb_r])
        ps, pres = pro_b[1]
        first = True
        for ci_, lh in enumerate((S.Vm[d], S.U[d], c.ones_b)):
            k.mmg(ps[:, ci_ * 32:(ci_ + 1) * 32], lh, S.dthi[:, gch, dsl], first, False,
                  [S.dt_r, S.cres, c.res], [pres])
            first = False
            k.mmg(ps[:, ci_ * 32:(ci_ + 1) * 32], lh, S.dtlo[:, gch, dsl], False, ci_ == 2,
                  [S.dt_r, S.cres, c.res], [pres])
        k.act(P.eall, ps[:, 0:96], AF.Exp, [pres], [P.e_r])
        k.tt("dve", P.dtdec, S.dt[:, gch, dsl], P.eall[:, 0:32], ALU.mult, [S.dt_r, P.e_r], [P.e_r])
        k.tt("pool", r3(P.xdt, 32), r3(Lb.xtok, 32),
             S.dt[:, gch, dsl].unsqueeze(2).broadcast_to([128, 32, 64]), ALU.mult, [Lb.x_r, S.dt_r], [P.xdt_r])
        k.tt("pool", r3(P.xdd, 32), r3(Lb.xtok, 32), P.dtdec.unsqueeze(2).broadcast_to([128, 32, 64]),
             ALU.mult, [Lb.x_r, P.e_r], [P.xdd_r])

    def seg(q_, ci, ch, d, gg, ui):
        P = CH[ci % 2]
        U_ = UN[ui % 3]
        gch = q_.ch0 + ch
        if d == 1:
            L_ = UL[ui % 3]
            rows_ = slice(ch * 128, (ch + 1) * 128)
            gc_ = slice(gg * 256, (gg + 1) * 256)
            k.dma("sp", L_.yp, q_.YP[rows_, gc_], [], [L_.yp_r])
            k.dma("sp", L_.zs, q_.ZS[rows_, gc_], [], [L_.zs_r])
        ps, pres = seg_b[ui % 3]
        first = True
        for hl in range(4):
            h = d * 32 + gg * 4 + hl
            for arr in (S.dthi, S.dtlo):
                k.mmg(ps[:, hl * 128:(hl + 1) * 128], arr[:, gch, h:h + 1].broadcast_to([128, 128]), S.U[d],
                     first, False, [S.dt_r, S.cres], [pres])
                first = False
        hs = d * 32 + gg * 4
        for arr in (S.ndthi, S.ndtlo):
            k.mmg(ps, S.U[d], arr[:, gch, hs:hs + 4].unsqueeze(2).broadcast_to([128, 4, 128]), False, False,
                 [S.dt_r, S.cres], [pres])
        k.mmg(ps, c.ident_b, S.NEG[d], False, True, [c.res, S.cres], [pres])
        k.act(U_.E, ps, AF.Exp, [pres], [U_.E_r])
        k.tt("pool", r3(U_.MT, 4), r3(U_.E, 4), P.cbT[:, gg, :].unsqueeze(1).broadcast_to([128, 4, 128]),
             ALU.mult, [U_.E_r, P.cb_r], [U_.MT_r])

    def rest(q_, ci, ch, d, gg, ui):
        P = CH[ci % 2]
        Lb = LB[ci % 3]
        U_ = UN[ui % 3]
        Y_ = UY[ui % 2]
        rows = slice(ch * 128, (ch + 1) * 128)
        gc = slice(gg * 256, (gg + 1) * 256)
        psY, presY = y_b[ui % 2]
        for hl in range(4):
            h = gg * 4 + hl
            k.mmg(psY[:, hl * 64:(hl + 1) * 64], U_.MT[:, hl * 128:(hl + 1) * 128], P.xdt[:, h * 64:(h + 1) * 64],
                 hl == 0, hl == 3, [U_.MT_r, P.xdt_r], [presY])
        psO, presO = os_b[ui % 2]
        k.mmg(psO[:, 0:256], Lb.CT[:, gg, :], Sbf[:, gc], True, False, [Lb.CT_r, Sbf_r[gg]], [presO])
        k.mmg(psO[:, 256:512], Lb.btok[:, gg * 128:(gg + 1) * 128], P.xdd[:, gc], False, True,
             [Lb.b_r, P.xdd_r], [presO])
        k.tt("dve", r3(Y_.ytmp, 4), r3(psO[:, 0:256], 4),
             P.eall[:, 32 + gg * 4:32 + (gg + 1) * 4].unsqueeze(2).broadcast_to([128, 4, 64]), ALU.mult,
             [presO, P.e_r], [Y_.ytmp_r])
        k.tt("dve", Y_.ydir, psY[:, 0:256], Y_.ytmp, ALU.add, [presY, Y_.ytmp_r], [Y_.ydir_r])
        if d == 0:
            k.tt("dve", r3(Y_.ytmp2, 4), r3(Lb.xtok[:, gc], 4),
                 S.Dbc[:, gg * 4:(gg + 1) * 4].unsqueeze(2).broadcast_to([128, 4, 64]), ALU.mult,
                 [Lb.x_r, S.cres], [Y_.ytmp2_r])
            k.tt("dve", Y_.ydir, Y_.ydir, Y_.ytmp2, ALU.add, [Y_.ytmp2_r, Y_.ydir_r], [Y_.ydir_r])
            k.dma("sp", q_.YP[rows, gc], Y_.ydir, [Y_.ydir_r], [])
        else:
            L_ = UL[ui % 3]
            k.tt("dve", Y_.ydir, Y_.ydir, L_.yp, ALU.add, [Y_.ydir_r, L_.yp_r], [Y_.ydir_r])
            k.tt("dve", yg[:, gc], Y_.ydir, L_.zs, ALU.mult, [Y_.ydir_r, L_.zs_r], [yg_r])
        st = S.St[d][:, gc]
        k.tt("pool", r3(st, 4), r3(st, 4),
             P.eall[:, 64 + gg * 4:64 + (gg + 1) * 4].unsqueeze(2).broadcast_to([128, 4, 64]), ALU.mult,
             [St_r[d][gg], P.e_r], [St_r[d][gg]])
        k.tt("dve", st, st, psO[:, 256:512], ALU.add, [St_r[d][gg], presO], [St_r[d][gg]])
        k.cp("dve", Sbf[:, gc], st, [St_r[d][gg]], [Sbf_r[gg]])

    def epilogue(q_, ci, ch, d, gg, ui):
        rows = slice(ch * 128, (ch + 1) * 128)
        sqv = yn.bitcast(F32)
        yout = yn.bitcast(F32)

        def p0():
            for hf in range(2):
                hs = slice(hf * 1024, (hf + 1) * 1024)
                k.tt("dve", sqv, yg[:, hs], yg[:, hs], ALU.mult, [yg_r], [yn_r])
                k.op("dve", lambda e, hf=hf: e.reduce_sum(ssq[:, hf:hf + 1], sqv, axis=AX.X), [yn_r], [ssq_r])
            k.tt("dve", ssq[:, 8:9], ssq[:, 0:1], ssq[:, 1:2], ALU.add, [ssq_r], [ssq_r])
            k.ts("dve", ssq[:, 9:10], ssq[:, 8:9], 1.0 / DI, EPS, ALU.mult, ALU.add, [ssq_r], [ssq_r])
            k.act(ssq[:, 9:10], ssq[:, 9:10], AF.Sqrt, [ssq_r], [ssq_r])
            k.op("dve", lambda e: e.reciprocal(ssq[:, 10:11], ssq[:, 9:10]), [ssq_r], [ssq_r])
            k.stt("dve", yn, yg, ssq[:, 10:11], S.nwbc, ALU.mult, ALU.mult, [yg_r, ssq_r, S.cres], [yn_r])

        def ptr(f4):
            def run():
                ps, pres = g.banks[7]
                pb = psbf(ps)
                for j in range(4):
                    fb = f4 * 4 + j
                    k.tr(pb[:, j * 128:(j + 1) * 128], yn[:, fb * 128:(fb + 1) * 128], c.ident_b,
                         [yn_r, c.res], [pres])
                k.cp("act", ynT[:, f4 * 4:(f4 + 1) * 4, :], r3(pb[:, 0:512], 4), [pres], [ynT_r])
            return run

        def pout(half):
            def run():
                ps, pres = g.banks[7]
                for fb in range(16):
                    k.mmg(ps, ynT[:, fb, :], wout[:, fb, half * 512:(half + 1) * 512], fb == 0, fb == 15,
                          [ynT_r, wout_r], [pres])
                k.cp("act", yout[:, half * 512:(half + 1) * 512], ps, [pres], [yn_r])
                if half == 1:
                    k.dma("sp", q_.Y1[rows, :], yout, [yn_r], [])
            return run
        return [p0] + [ptr(f4) for f4 in range(4)] + [pout(0), pout(1)]

    ui = 0
    for q_ in seqs:
        nch = q_.L // 128
        for d in range(2):
            for gq in range(8):
                k.cp("pool", Sbf[:, gq * 256:(gq + 1) * 256], S.St[d][:, gq * 256:(gq + 1) * 256],
                     [St_r[d][gq]], [Sbf_r[gq]])
            order = list(range(nch)) if d == 0 else list(range(nch - 1, -1, -1))
            units = [(q_, ci, ch, d, gg) for ci, ch in enumerate(order) for gg in range(8)]
            AHEAD = 2
            nu = len(units)
            PRO = 5
            LD = 10
            pending = []
            for step in range(-LD, nu + AHEAD):
                lstep = step + LD
                if 0 <= lstep < nu and units[lstep][4] == 0:
                    u = units[lstep]
                    loads(u[0], u[1], u[2], u[3])
                pstep = step + PRO
                if 0 <= pstep < nu and units[pstep][4] == 0:
                    u = units[pstep]
                    prologue(u[0], u[1], u[2], u[3])
                if 0 <= step < nu:
                    seg(*units[step], ui + step)
                if step >= AHEAD:
                    vi = step - AHEAD
                    v = units[vi]
                    rest(*v, ui + vi)
                    if pending:
                        pending.pop(0)()
                    if v[4] == 7:
                        if d == 1:
                            while pending:
                                pending.pop(0)()
                            pcs = epilogue(*v, ui + vi)
                            pcs.pop(0)()
                            pending.extend(pcs)
                        if g.bg:
                            g.bg.pop(0)()
            while pending:
                pending.pop(0)()
            ui += nu
            barrier(g)


def load_vec(g, src_flat, n):
    k, ar, c = g.k, g.ar, g.c
    dst = ar.f32(n)
    dres = Res("vec")
    done = 0
    while done < n:
        m = min(128, n - done)
        if not hasattr(g, "vstage"):
            g.vstage = [g.ar.f32(128), g.ar.f32(128)]
            g.vsres = [Res("vs0"), Res("vs1")]
            g.vs_i = 0
        b = g.vs_i % 2
        g.vs_i += 1
        st, sres = g.vstage[b], g.vsres[b]
        k.dma("sp", st[0:m, :], src_flat[done * 128:(done + m) * 128].rearrange("(r c) -> r c", c=128),
              [], [sres])
        ps, pres = g.bank()
        k.tr(ps[:, 0:m], st[0:m, :], c.ident_f[0:m, 0:m], [sres, c.res], [pres])
        k.cp("dve", dst[:, done:done + m], ps[:, 0:m], [pres], [dres])
        done += m
    return dst, dres


def all_res(rr):
    return [r for row in rr for r in row]


def barrier(g):
    k = g.k
    toks = []
    for e in k.ENG:
        if e in k.cursem and k.cnt[e] > 0:
            toks.append((k.cursem[e], k.cnt[e]))
    for q, slots in k.dma_slots.items():
        n = k.dma_i[q]
        for s_i, sem in enumerate(slots):
            uses = (n - s_i + NSLOT - 1) // NSLOT if n > s_i else 0
            if uses > 0:
                toks.append((sem, 16 * uses))
    for e in k.ENG:
        for t in toks:
            k._wait(e, t)


def load_stream(g, src, dstT, dres, L, ysrc=None, gate=None):
    k, ar, c = g.k, g.ar, g.c
    m = ar.mark()
    xin = [ar.f32(D), ar.f32(D)]
    xres = [Res("xin0"), Res("xin1")]
    yin = [ar.f32(D), ar.f32(D)] if ysrc is not None else None
    yres = [Res("yin0"), Res("yin1")]
    tw = min(L, 512)
    for ch in range(L // 128):
        b = ch % 2
        k.dma("sp", xin[b], src[ch * 128:(ch + 1) * 128, :], [], [xres[b]])
        tl = (ch * 128) // tw
        for q in range(2):
            ps, pres = g.bank()
            for j in range(4):
                blk = q * 4 + j
                k.tr(ps[:, j * 128:(j + 1) * 128], xin[b][:, blk * 128:(blk + 1) * 128], c.ident_f,
                     [xres[b], c.res], [pres], inc=(j == 3))
            k.cp("dve" if q == 0 else "act", dstT[:, q * 4:(q + 1) * 4, ch * 128:(ch + 1) * 128],
                 r3(ps, 4), [pres], [dres[q * 4 + j][tl] for j in range(4)])
        if ysrc is not None:
            modT, g0, s_, mres = gate
            k.dma("sp", yin[b], ysrc[ch * 128:(ch + 1) * 128, :], [], [yres[b]])
            for q in range(2):
                ps, pres = g.bank()
                for j in range(4):
                    blk = q * 4 + j
                    k.tr(ps[:, j * 128:(j + 1) * 128], yin[b][:, blk * 128:(blk + 1) * 128], c.ident_f,
                         [yres[b], c.res], [pres], inc=(j == 3))
                for j in range(4):
                    blk = q * 4 + j
                    dsl = dstT[:, blk, ch * 128:(ch + 1) * 128]
                    k.stt("dve", dsl, ps[:, j * 128:(j + 1) * 128], modT[:, g0 + blk, s_:s_ + 1], dsl,
                          ALU.mult, ALU.add, [pres, mres, dres[blk][tl]], [dres[blk][tl]])
    barrier(g)
    ar.release(m)


FN_STOP = 99


def final_norm(g):
    k, ar, c = g.k, g.ar, g.c
    m = ar.mark()
    fw, fres = load_vec(g, g.I["final_norm_w"], NB)
    if FN_STOP == 0:
        return
    sq = [ar.bf(512), ar.bf(512)]
    sqres = [Res("sq0"), Res("sq1")]
    rstd = ar.f32(512)
    rres = Res("rstd")
    yt = [ar.f32(512), ar.f32(512)]
    ytres = [Res("yt0"), Res("yt1")]
    ost = r3(ar.f32(4 * D), 4)
    ores = Res("ost")
    for tl in range(T // 512):
        sl = slice(tl * 512, (tl + 1) * 512)
        ps, pres = g.bank()
        for blk in range(NB):
            b = blk % 2
            k.act(sq[b], g.hT[:, blk, sl], AF.Square, [g.h_res[blk][tl]], [sqres[b]])
            k.mm(ps, c.ones_b, sq[b], blk == 0, blk == NB - 1, [sqres[b], c.res], [pres])
        if FN_STOP == 1:
            continue
        k.ts("dve", rstd, ps, 1.0 / D, EPS, ALU.mult, ALU.add, [pres], [rres])
        k.act(rstd, rstd, AF.Sqrt, [rres], [rres])
        if FN_STOP == 2:
            continue
        k.op("dve", lambda e: e.reciprocal(rstd, rstd), [rres], [rres])
        if FN_STOP == 3:
            continue
        for blk in range(NB):
            b = blk % 2
            k.stt("dve", yt[b], g.hT[:, blk, sl], fw[:, blk:blk + 1], rstd, ALU.mult, ALU.mult,
                  [g.h_res[blk][tl], fres, rres], [ytres[b]])
            if FN_STOP == 4:
                continue
            ps2, pres2 = g.bank()
            for j in range(4):
                k.tr(ps2[:, j * 128:(j + 1) * 128], yt[b][:, j * 128:(j + 1) * 128], c.ident_f,
                     [ytres[b], c.res], [pres2], inc=(j == 3))
            k.cp("act", ost[:, :, blk * 128:(blk + 1) * 128], r3(ps2, 4), [pres2], [ores])
        tok = k.dma("sp", g.out[sl, :].rearrange("(c p) d -> p c d", p=128), ost, [ores], [])
        k.out_tokens.append(tok)
    ar.release(m)


_CACHE = {}


def _prep_inputs(inp, b):
    f = lambda a: np.ascontiguousarray(np.asarray(a, dtype=np.float32))
    m = {}
    m["x"] = f(inp["x"][b])
    m["ctx"] = f(inp["ctx"][b])
    m["cvec"] = f(np.stack([np.asarray(inp["c"])[b], np.asarray(inp["c_ctx"])], 0))
    for nm in ("mod_w", "mod_b", "norm1_w", "norm2_w", "ffn_w_up", "ffn_conv_b", "ffn_w_down",
               "final_norm_w"):
        m[nm] = f(inp[nm])
    m["ffn_conv_w"] = f(np.asarray(inp["ffn_conv_w"]).reshape(2, 9, FH))
    for nm in ("ssd_w_in", "ssd_conv_w", "ssd_conv_b", "ssd_d", "ssd_norm_w", "ssd_w_out",
               "conf_w_pw1", "conf_b_pw1", "conf_w_dw", "conf_b_dw", "conf_ln_w", "conf_ln_b",
               "conf_w_pw2", "conf_b_pw2"):
        m[nm] = f(np.asarray(inp[nm])[0])
    m["ssd_dt_bias"] = f(np.asarray(inp["ssd_dt_bias"])[0].reshape(64))
    m["ssd_a_log"] = f(np.asarray(inp["ssd_a_log"])[0].reshape(64))
    return m


def kernel(**inputs):
    if "nc" not in _CACHE:
        _CACHE["nc"] = build()
    nc = _CACHE["nc"]
    in_maps = [_prep_inputs(inputs, b) for b in range(8)]
    res = run_bass_kernel_spmd(nc, in_maps, core_ids=list(range(8)))
    return np.stack([np.asarray(r["out"], dtype=np.float32) for r in res.results], 0)
```
